# Optimizing a Trainium2 kernel written in Bass

```python
import jax, jax.numpy as jnp
from jax import lax
import numpy as np

D_MODEL = 1024
BATCH = 32
SEQ = 2048
DEPTH = 2

D_MIX = D_MODEL
HEAD_DIM = 64
N_FOX_HEADS = 4
N_GMLP_GROUPS = 4
N_NSA_HEADS = 4
N_NSA_KV = 1
N_POOL_GROUPS = 4
W_FOX = N_FOX_HEADS * HEAD_DIM
W_GMLP = N_GMLP_GROUPS * HEAD_DIM
W_NSA = N_NSA_HEADS * HEAD_DIM
W_POOL = N_POOL_GROUPS * HEAD_DIM
W_NSA_KV = N_NSA_KV * HEAD_DIM
IN_SPLITS = (W_FOX, W_FOX, W_FOX, N_FOX_HEADS, W_GMLP, W_GMLP, W_NSA, W_NSA_KV, W_NSA_KV, W_NSA_KV, W_NSA_KV, W_NSA_KV, W_NSA_KV, 3 * N_NSA_HEADS, W_POOL)
N_IN = 3 * W_FOX + N_FOX_HEADS + 2 * W_GMLP + W_NSA + 6 * W_NSA_KV + 3 * N_NSA_HEADS + W_POOL
D_FF = 2816
ROPE_THETA = 500000.0
ROPE_DIM = HEAD_DIM // 4
Q_BLOCK = 128
NSA_Q_BLOCK = 64
GMLP_CHUNK = 128
CMP_LEN = 32
CMP_STRIDE = 16
CMP_HIDDEN = 256
SEL_LEN = 64
SEL_TOP = 16
WINDOW = 512
POOL_SIZES = (2, 4, 8, 16)
FFN_RES_WEIGHT = 0.5
EPS = 1e-6
NEG_INF = -1e30
SEL_FORCE = 1e3

kernel_name = 'hybrid_fox_gmlp_nsa_pool_macaron'

F32 = jnp.float32


def rms_norm(x, g):
    xf = x.astype(F32)
    y = xf * lax.rsqrt(jnp.mean(xf * xf, axis=-1, keepdims=True) + EPS)
    return (y * g.astype(F32)).astype(x.dtype)


def swiglu_ffn(x, g, w1, w3, w2):
    h = rms_norm(x, g)
    return (jax.nn.silu(h @ w1) * (h @ w3)) @ w2


def rope_partial(x, pos):
    half = ROPE_DIM // 2
    inv = ROPE_THETA ** (-jnp.arange(half, dtype=F32) * 2.0 / ROPE_DIM)
    ang = pos.astype(F32)[:, None] * inv[None, :]
    cos = jnp.cos(ang)[:, None, :].astype(x.dtype)
    sin = jnp.sin(ang)[:, None, :].astype(x.dtype)
    x1 = x[..., :half]
    x2 = x[..., half:ROPE_DIM]
    return jnp.concatenate([x1 * cos - x2 * sin, x2 * cos + x1 * sin, x[..., ROPE_DIM:]], axis=-1)


def split_mixing_columns(p):
    offs = []
    acc = 0
    for s in IN_SPLITS[:-1]:
        acc += s
        offs.append(acc)
    return jnp.split(p, offs, axis=-1)


def fox_mixer(q, k, v, f_logit, f_bias, g_q, g_k):
    B, T, H, Dh = q.shape
    q = rms_norm(q, g_q)
    k = rms_norm(k, g_k)
    log_f = jax.nn.log_sigmoid((f_logit + f_bias).astype(F32))
    c = jnp.cumsum(log_f, axis=1).transpose(0, 2, 1)
    scale = Dh ** -0.5
    kpos = jnp.arange(T)

    def block(i):
        t0 = i * Q_BLOCK
        q_i = lax.dynamic_slice_in_dim(q, t0, Q_BLOCK, axis=1)
        c_i = lax.dynamic_slice_in_dim(c, t0, Q_BLOCK, axis=2)
        s = jnp.einsum('bqhd,bshd->bhqs', q_i, k).astype(F32) * scale
        s = s + c_i[..., :, None] - c[:, :, None, :]
        qpos = t0 + jnp.arange(Q_BLOCK)
        s = jnp.where(kpos[None, :] <= qpos[:, None], s, NEG_INF)
        p = jax.nn.softmax(s, axis=-1).astype(v.dtype)
        return jnp.einsum('bhqs,bshd->bqhd', p, v)

    o = lax.map(block, jnp.arange(T // Q_BLOCK))
    return o.transpose(1, 0, 2, 3, 4).reshape(B, T, H * Dh)


def gmlp_mixer(u, v, g_v, w_s, b_s):
    B, T, W = u.shape
    G = N_GMLP_GROUPS
    Dg = W // G
    C = GMLP_CHUNK
    u = jax.nn.gelu(u)
    v = rms_norm(jax.nn.gelu(v).reshape(B, T, G, Dg), g_v.reshape(G, Dg))
    vc = v.reshape(B, T // C, C, G, Dg)
    w = w_s * jnp.tril(jnp.ones((C, C), w_s.dtype))
    s = jnp.einsum('gts,bcsgd->bctgd', w, vc) + b_s.T[:, :, None]
    return u * s.reshape(B, T, W)


def compress_kv(kv, pos_emb, w1, w2):
    B, T, Hk, Dh = kv.shape
    nc = (T - CMP_LEN) // CMP_STRIDE + 1
    idx = jnp.arange(nc)[:, None] * CMP_STRIDE + jnp.arange(CMP_LEN)[None, :]
    blocks = kv[:, idx] + pos_emb[None, None, :, None, :]
    flat = blocks.transpose(0, 1, 3, 2, 4).reshape(B, nc, Hk, CMP_LEN * Dh)
    return jax.nn.gelu(flat @ w1) @ w2


def cmp_to_sel_overlap(nc, nsel):
    cs = np.arange(nc) * CMP_STRIDE
    ce = cs + CMP_LEN
    ss = np.arange(nsel) * SEL_LEN
    se = ss + SEL_LEN
    ov = np.clip(np.minimum(ce[:, None], se[None, :]) - np.maximum(cs[:, None], ss[None, :]), 0, None)
    return jnp.asarray(ov / CMP_LEN, dtype=F32)


def nsa_mixer(q, kc, vc, ks, vs, kw, vw, gate_logit, gate_b, g_q, g_kc, g_ks, g_kw,
              pos_k, k_w1, k_w2, pos_v, v_w1, v_w2):
    B, T = q.shape[:2]
    H, Hk, Dh = N_NSA_HEADS, N_NSA_KV, HEAD_DIM
    Hg = H // Hk
    Qb = NSA_Q_BLOCK
    scale = Dh ** -0.5
    pos = jnp.arange(T)
    q = rope_partial(rms_norm(q.reshape(B, T, H, Dh), g_q), pos).reshape(B, T, Hk, Hg, Dh)
    kc, vc, ks, vs, kw, vw = [a.reshape(B, T, Hk, Dh) for a in (kc, vc, ks, vs, kw, vw)]

    nc = (T - CMP_LEN) // CMP_STRIDE + 1
    cmp_end = jnp.arange(nc) * CMP_STRIDE + CMP_LEN - 1
    k_cmp = rope_partial(rms_norm(compress_kv(kc, pos_k, k_w1, k_w2), g_kc), cmp_end)
    v_cmp = compress_kv(vc, pos_v, v_w1, v_w2)
    s = jnp.einsum('btghd,bngd->bghtn', q, k_cmp).astype(F32) * scale
    m_cmp = cmp_end[None, :] <= pos[:, None]
    p_cmp = jnp.where(m_cmp, jax.nn.softmax(jnp.where(m_cmp, s, NEG_INF), axis=-1), 0.0)
    o_cmp = jnp.einsum('bghtn,bngd->btghd', p_cmp.astype(v_cmp.dtype), v_cmp)

    nsel = T // SEL_LEN
    n_top = min(SEL_TOP, nsel)
    imp = jnp.einsum('bghtn,nj->bgtj', p_cmp, cmp_to_sel_overlap(nc, nsel))
    blk = jnp.arange(nsel)[None, :]
    cur = (pos // SEL_LEN)[:, None]
    forced = ((blk == 0) | (blk == cur) | (blk == cur - 1)).astype(F32)
    imp = jnp.where(blk <= cur, imp + SEL_FORCE * forced, NEG_INF)
    top_val, top_idx = lax.top_k(imp, n_top)
    top_ok = top_val > NEG_INF * 0.5

    ks = rope_partial(rms_norm(ks, g_ks), pos)
    ks_blocks = ks.reshape(B, nsel, SEL_LEN, Hk, Dh).transpose(0, 3, 1, 2, 4)
    vs_blocks = vs.reshape(B, nsel, SEL_LEN, Hk, Dh).transpose(0, 3, 1, 2, 4)
    kw_pad = jnp.pad(rope_partial(rms_norm(kw, g_kw), pos), ((0, 0), (WINDOW, 0), (0, 0), (0, 0)))
    vw_pad = jnp.pad(vw, ((0, 0), (WINDOW, 0), (0, 0), (0, 0)))
    bi = jnp.arange(B)[:, None, None, None]
    gi = jnp.arange(Hk)[None, :, None, None]
    m_len = n_top * SEL_LEN

    def block(i):
        t0 = i * Qb
        q_i = lax.dynamic_slice_in_dim(q, t0, Qb, axis=1)
        t_i = t0 + jnp.arange(Qb)
        idx = lax.dynamic_slice_in_dim(top_idx, t0, Qb, axis=2)
        ok = lax.dynamic_slice_in_dim(top_ok, t0, Qb, axis=2)
        k_g = ks_blocks[bi, gi, idx].reshape(B, Hk, Qb, m_len, Dh)
        v_g = vs_blocks[bi, gi, idx].reshape(B, Hk, Qb, m_len, Dh)
        kpos = (idx[..., None] * SEL_LEN + jnp.arange(SEL_LEN)).reshape(B, Hk, Qb, m_len)
        m_s = jnp.repeat(ok, SEL_LEN, axis=-1) & (kpos <= t_i[None, None, :, None])
        s = jnp.einsum('bqghd,bgqmd->bghqm', q_i, k_g).astype(F32) * scale
        p = jax.nn.softmax(jnp.where(m_s[:, :, None], s, NEG_INF), axis=-1)
        o_s = jnp.einsum('bghqm,bgqmd->bqghd', p.astype(v_g.dtype), v_g)
        kw_i = lax.dynamic_slice_in_dim(kw_pad, t0, WINDOW + Qb, axis=1)
        vw_i = lax.dynamic_slice_in_dim(vw_pad, t0, WINDOW + Qb, axis=1)
        wpos = t0 - WINDOW + jnp.arange(WINDOW + Qb)
        d = t_i[:, None] - wpos[None, :]
        m_w = (d >= 0) & (d < WINDOW) & (wpos[None, :] >= 0)
        s = jnp.einsum('bqghd,bsgd->bghqs', q_i, kw_i).astype(F32) * scale
        p = jax.nn.softmax(jnp.where(m_w, s, NEG_INF), axis=-1)
        o_w = jnp.einsum('bghqs,bsgd->bqghd', p.astype(vw_i.dtype), vw_i)
        return o_s, o_w

    o_s, o_w = lax.map(block, jnp.arange(T // Qb))
    o_s = o_s.transpose(1, 0, 2, 3, 4, 5).reshape(B, T, Hk, Hg, Dh)
    o_w = o_w.transpose(1, 0, 2, 3, 4, 5).reshape(B, T, Hk, Hg, Dh)
    g = jax.nn.sigmoid((gate_logit + gate_b).astype(F32)).astype(q.dtype).reshape(B, T, Hk, Hg, 3)
    o = g[..., 0:1] * o_cmp + g[..., 1:2] * o_s + g[..., 2:3] * o_w
    return o.reshape(B, T, H * Dh)


def pool_mixer(z, w_p, scale):
    B, T, W = z.shape
    G = N_POOL_GROUPS
    Dg = W // G
    zf = z.astype(F32)
    cs = jnp.cumsum(zf, axis=1)
    cs = jnp.concatenate([jnp.zeros_like(cs[:, :1]), cs], axis=1)
    win = jnp.repeat(jnp.array(POOL_SIZES, jnp.int32), Dg)
    t = jnp.arange(T)[:, None]
    lo = jnp.maximum(t + 1 - win[None, :], 0)
    lo_sum = jnp.take_along_axis(cs, jnp.broadcast_to(lo[None], (B, T, W)), axis=1)
    cnt = jnp.minimum(t + 1, win[None, :]).astype(F32)
    pooled = ((cs[:, 1:] - lo_sum) / cnt - zf).astype(z.dtype).reshape(B, T, G, Dg)
    y = jnp.einsum('btgd,gde->btge', pooled, w_p)
    return y.reshape(B, T, W) * scale


def setup_inputs(seed: int = 0) -> dict:
    key = jax.random.key(seed)
    ks = list(jax.random.split(key, 40))
    L = DEPTH

    def nrm(shape, scale):
        return jax.random.normal(ks.pop(), shape, F32) * scale

    def gain(shape):
        return 1.0 + 0.02 * jax.random.normal(ks.pop(), shape, F32)

    return {
        'x': nrm((BATCH, SEQ, D_MODEL), 1.0),
        'ffn1_norm': gain((L, D_MODEL)),
        'ffn1_w1': nrm((L, D_MODEL, D_FF), D_MODEL ** -0.5),
        'ffn1_w3': nrm((L, D_MODEL, D_FF), D_MODEL ** -0.5),
        'ffn1_w2': nrm((L, D_FF, D_MODEL), D_FF ** -0.5),
        'mix_norm': gain((L, D_MODEL)),
        'w_in': nrm((L, D_MODEL, N_IN), D_MODEL ** -0.5),
        'w_out': nrm((L, D_MIX, D_MODEL), D_MIX ** -0.5),
        'fox_f_bias': jax.random.uniform(ks.pop(), (L, N_FOX_HEADS), F32, 1.0, 4.0),
        'fox_q_norm': gain((L, HEAD_DIM)),
        'fox_k_norm': gain((L, HEAD_DIM)),
        'gmlp_v_norm': gain((L, W_GMLP)),
        'gmlp_w_s': nrm((L, N_GMLP_GROUPS, GMLP_CHUNK, GMLP_CHUNK), GMLP_CHUNK ** -0.5),
        'gmlp_b_s': gain((L, N_GMLP_GROUPS, GMLP_CHUNK)),
        'nsa_q_norm': gain((L, HEAD_DIM)),
        'nsa_kc_norm': gain((L, HEAD_DIM)),
        'nsa_ks_norm': gain((L, HEAD_DIM)),
        'nsa_kw_norm': gain((L, HEAD_DIM)),
        'nsa_cmp_pos_k': nrm((L, CMP_LEN, HEAD_DIM), 0.1),
        'nsa_cmp_k_w1': nrm((L, CMP_LEN * HEAD_DIM, CMP_HIDDEN), (CMP_LEN * HEAD_DIM) ** -0.5),
        'nsa_cmp_k_w2': nrm((L, CMP_HIDDEN, HEAD_DIM), CMP_HIDDEN ** -0.5),
        'nsa_cmp_pos_v': nrm((L, CMP_LEN, HEAD_DIM), 0.1),
        'nsa_cmp_v_w1': nrm((L, CMP_LEN * HEAD_DIM, CMP_HIDDEN), (CMP_LEN * HEAD_DIM) ** -0.5),
        'nsa_cmp_v_w2': nrm((L, CMP_HIDDEN, HEAD_DIM), CMP_HIDDEN ** -0.5),
        'nsa_gate_bias': nrm((L, 3 * N_NSA_HEADS), 0.02),
        'pool_w': nrm((L, N_POOL_GROUPS, W_POOL // N_POOL_GROUPS, W_POOL // N_POOL_GROUPS), (W_POOL // N_POOL_GROUPS) ** -0.5),
        'pool_scale': gain((L, W_POOL)),
        'ffn2_norm': gain((L, D_MODEL)),
        'ffn2_w1': nrm((L, D_MODEL, D_FF), D_MODEL ** -0.5),
        'ffn2_w3': nrm((L, D_MODEL, D_FF), D_MODEL ** -0.5),
        'ffn2_w2': nrm((L, D_FF, D_MODEL), D_FF ** -0.5),
    }


def reference(x, ffn1_norm, ffn1_w1, ffn1_w3, ffn1_w2, mix_norm, w_in, w_out,
              fox_f_bias, fox_q_norm, fox_k_norm, gmlp_v_norm, gmlp_w_s, gmlp_b_s,
              nsa_q_norm, nsa_kc_norm, nsa_ks_norm, nsa_kw_norm,
              nsa_cmp_pos_k, nsa_cmp_k_w1, nsa_cmp_k_w2, nsa_cmp_pos_v, nsa_cmp_v_w1, nsa_cmp_v_w2,
              nsa_gate_bias, pool_w, pool_scale, ffn2_norm, ffn2_w1, ffn2_w3, ffn2_w2):
    B, T, _ = x.shape
    for l in range(DEPTH):
        x = x + FFN_RES_WEIGHT * swiglu_ffn(x, ffn1_norm[l], ffn1_w1[l], ffn1_w3[l], ffn1_w2[l])
        h = rms_norm(x, mix_norm[l])
        (fq, fk, fv, ff, gu, gv, nq, nkc, nvc, nks, nvs, nkw, nvw, ng, pz) = split_mixing_columns(h @ w_in[l])
        fshape = (B, T, N_FOX_HEADS, HEAD_DIM)
        o_a = fox_mixer(fq.reshape(fshape), fk.reshape(fshape), fv.reshape(fshape), ff,
                        fox_f_bias[l], fox_q_norm[l], fox_k_norm[l])
        o_b = gmlp_mixer(gu, gv, gmlp_v_norm[l], gmlp_w_s[l], gmlp_b_s[l])
        o_c = nsa_mixer(nq, nkc, nvc, nks, nvs, nkw, nvw, ng, nsa_gate_bias[l],
                        nsa_q_norm[l], nsa_kc_norm[l], nsa_ks_norm[l], nsa_kw_norm[l],
                        nsa_cmp_pos_k[l], nsa_cmp_k_w1[l], nsa_cmp_k_w2[l],
                        nsa_cmp_pos_v[l], nsa_cmp_v_w1[l], nsa_cmp_v_w2[l])
        o_d = pool_mixer(pz, pool_w[l], pool_scale[l])
        x = x + jnp.concatenate([o_a, o_b, o_c, o_d], axis=-1) @ w_out[l]
        x = x + FFN_RES_WEIGHT * swiglu_ffn(x, ffn2_norm[l], ffn2_w1[l], ffn2_w3[l], ffn2_w2[l])
    return x
```

```python
import contextlib
import numpy as np
import ml_dtypes
import concourse.bass as bass
import concourse.mybir as mybir
from concourse.bass_utils import run_bass_kernel_spmd

F32 = mybir.dt.float32
BF16 = mybir.dt.bfloat16
AF = mybir.ActivationFunctionType
ALU = mybir.AluOpType
AX = mybir.AxisListType

T = 2048
D = 1024
DFF = 2816
NIN = 2192
NT = 16
EPS = 1e-6
NEG = -30000.0
ROPE_THETA = 500000.0


class Prog:
    ENG = ("pe", "act", "dve", "pool", "sp")
    EPOCH = 8000

    def __init__(self, nc):
        self.nc = nc
        self.ops = []
        self.last_w = {}
        self.readers = {}
        self.fence_op = None

    def op(self, eng, fn, reads=(), writes=(), dma=False, sem_key=None, extra_deps=()):
        i = len(self.ops)
        deps = set(extra_deps)
        if self.fence_op is not None:
            deps.add(self.fence_op)
        for t in reads:
            w = self.last_w.get(t)
            if w is not None:
                deps.add(w)
        for t in writes:
            w = self.last_w.get(t)
            if w is not None:
                deps.add(w)
            for r in self.readers.get(t, ()):
                deps.add(r)
        for t in reads:
            self.readers.setdefault(t, []).append(i)
        for t in writes:
            self.last_w[t] = i
            self.readers[t] = []
        if dma and sem_key is None:
            sem_key = ("dma",) + tuple(writes)
        self.ops.append(dict(eng=eng, fn=fn, deps=deps, dma=dma, sem_key=sem_key, flag=False))
        return i

    def dma(self, eng, out, in_, reads=(), writes=(), sem_key=None, **kw):
        return self.op(eng, lambda e: e.dma_start(out=out, in_=in_, **kw), reads, writes, dma=True, sem_key=sem_key)

    def fence(self, dummy):
        ops = self.ops
        outstanding = set(self.last_w.values())
        for rs in self.readers.values():
            outstanding.update(rs)
        if self.fence_op is not None:
            outstanding.add(self.fence_op)
        best = {}
        deps = set()
        for d in outstanding:
            o = ops[d]
            if o["dma"]:
                k = ("D", o["sem_key"])
            else:
                k = ("E", o["eng"])
            if k not in best or best[k] < d:
                best[k] = d
        deps = set(best.values())
        self.last_w = {}
        self.readers = {}
        self.fence_op = None
        i = self.op("dve", lambda e: e.memset(dummy, 0.0), extra_deps=deps)
        self.fence_op = i
        return i

    def emit(self, final_waits=()):
        nc = self.nc
        ops = self.ops
        for o in ops:
            if o["eng"] == "pe" and not o["dma"]:
                o["deps"] = {d for d in o["deps"] if not (ops[d]["eng"] == "pe" and not ops[d]["dma"])}
            for d in o["deps"]:
                ops[d]["flag"] = True
        for i in final_waits:
            ops[i]["flag"] = True
        for o in ops:
            if o["dma"]:
                o["flag"] = True
        sem_names = []
        seen = set()
        cnt = {}
        for i, o in enumerate(ops):
            if not o["flag"]:
                continue
            if o["dma"]:
                key = ("D", o["sem_key"])
                cnt[key] = cnt.get(key, 0) + 16
                o["sig"] = (key, cnt[key])
            else:
                ep = cnt.get(("ep", o["eng"]), 0)
                key = ("E", o["eng"], ep)
                cnt[key] = cnt.get(key, 0) + 1
                o["sig"] = (key, cnt[key])
                if cnt[key] >= self.EPOCH:
                    cnt[("ep", o["eng"])] = ep + 1
            if o["sig"][0] not in seen:
                seen.add(o["sig"][0])
                sem_names.append(o["sig"][0])
        self.n_sems = len(sem_names)
        with contextlib.ExitStack() as st:
            sems = {k: st.enter_context(nc.semaphore("s%d" % n)) for n, k in enumerate(sem_names)}
            block = st.enter_context(nc.Block())
            for en in self.ENG:
                mine = [(i, o) for i, o in enumerate(ops) if o["eng"] == en]
                fin = list(final_waits) if en == "sp" else []

                def body(e, mine=mine, fin=fin):
                    waited = {}

                    def wait_for(d):
                        key, val = ops[d]["sig"]
                        if waited.get(key, 0) >= val:
                            return
                        e.wait_ge(sems[key], val)
                        waited[key] = val

                    for i, o in mine:
                        for d in sorted(o["deps"]):
                            wait_for(d)
                        ins = o["fn"](e)
                        if o["flag"]:
                            key, val = o["sig"]
                            ins.then_inc(sems[key], 16 if o["dma"] else 1)
                    for d in fin:
                        wait_for(d)

                dec = {"pe": block.tensor, "act": block.scalar, "dve": block.vector,
                       "pool": block.gpsimd, "sp": block.sync}[en]
                dec(body)
        return self


CB = {}
CF = {}


def _alloc(tab, name, n):
    off = tab.get("_n", 0)
    tab[name] = off
    tab["_n"] = off + n
    return off


for _n, _w in [("ident", 128), ("caus", 128), ("winup", 128), ("ov", 64), ("ones", 128),
               ("ad", 512), ("ap", 512), ("a0h", 512), ("a0l", 512), ("mcmp", 2048), ("E", 2048)]:
    _alloc(CB, _n, _w)
for _n, _w in [("identf", 128), ("tril", 128), ("ltri", 128), ("onesf", 128), ("row64", 128),
               ("addmask", 512), ("rope", 17 * 16), ("sel", 12 * 64)]:
    _alloc(CF, _n, _w)
NCB = CB["_n"]
NCF = CF["_n"]


def make_consts():
    cb = np.zeros((128, NCB), np.float32)
    cf = np.zeros((128, NCF), np.float32)
    p = np.arange(128)[:, None]
    q = np.arange(128)[None, :]
    cb[:, CB["ident"]:CB["ident"] + 128] = (p == q)
    cb[:, CB["caus"]:CB["caus"] + 128] = np.where(p <= q, 0.0, NEG)
    cb[:, CB["winup"]:CB["winup"] + 128] = np.where(p > q, 0.0, NEG)
    ncmp = 127
    cs = np.arange(ncmp) * 16
    ce = cs + 32
    ss = np.arange(32) * 64
    se = ss + 64
    ov = np.clip(np.minimum(ce[:, None], se[None, :]) - np.maximum(cs[:, None], ss[None, :]), 0, None) / 32.0
    cb[0:127, CB["ov"]:CB["ov"] + 32] = ov
    cb[0:127, CB["ov"] + 32] = 1.0
    cb[:, CB["ones"]:CB["ones"] + 128] = 1.0
    sizes = (2, 4, 8, 16)
    for g, wn in enumerate(sizes):
        ad = np.zeros((128, 128)); apv = np.zeros((128, 128)); a0 = np.zeros((128, 128))
        for t in range(128):
            for s in range(t - wn + 1, t + 1):
                if s >= 0:
                    ad[s, t] += 1.0 / wn
                else:
                    apv[128 + s, t] += 1.0 / wn
            ad[t, t] -= 1.0
            cntv = min(t + 1, wn)
            for s in range(max(0, t - wn + 1), t + 1):
                a0[s, t] += 1.0 / cntv
            a0[t, t] -= 1.0
        a0h = a0.astype(np.float32).astype(ml_dtypes.bfloat16).astype(np.float32)
        a0l = (a0 - a0h)
        cb[:, CB["ad"] + g * 128:CB["ad"] + (g + 1) * 128] = ad
        cb[:, CB["ap"] + g * 128:CB["ap"] + (g + 1) * 128] = apv
        cb[:, CB["a0h"] + g * 128:CB["a0h"] + (g + 1) * 128] = a0h
        cb[:, CB["a0l"] + g * 128:CB["a0l"] + (g + 1) * 128] = a0l
    tt = np.arange(T)[None, :]
    nn = np.arange(128)[:, None]
    mc = np.where((16 * nn + 31 <= tt) & (nn < 127), 0.0, NEG)
    cb[:, CB["mcmp"]:CB["mcmp"] + T] = mc
    jj = np.arange(32)[:, None]
    cb[0:32, CB["E"]:CB["E"] + T] = ((tt // 64) == jj)

    cf[:, CF["identf"]:CF["identf"] + 128] = (p == q)
    cf[:, CF["tril"]:CF["tril"] + 128] = (q <= p)
    cf[:, CF["ltri"]:CF["ltri"] + 128] = (p <= q)
    cf[:, CF["onesf"]:CF["onesf"] + 128] = 1.0
    cf[64, CF["row64"]:CF["row64"] + 128] = 1.0
    tpos = (np.arange(NT)[None, :] * 128 + np.arange(128)[:, None])
    cur = tpos // 64
    blk = np.arange(32)[None, None, :]
    forced = ((blk == 0) | (blk == cur[:, :, None]) | (blk == cur[:, :, None] - 1)).astype(np.float32)
    am = np.where(blk <= cur[:, :, None], 1000.0 * forced, -1e30).astype(np.float32)
    cf[:, CF["addmask"]:CF["addmask"] + 512] = am.reshape(128, 512)
    inv = (np.float32(ROPE_THETA) ** (-np.arange(8, dtype=np.float32) * np.float32(2.0) / np.float32(16))).astype(np.float32)
    rp = np.zeros((128, 17, 16), np.float32)
    for sl in range(17):
        pos = (tpos[:, sl] if sl < 16 else (np.arange(128) * 16 + 31)).astype(np.float32)
        ang = (pos[:, None] * inv[None, :]).astype(np.float32)
        rp[:, sl, 0:8] = np.cos(ang.astype(np.float64))
        rp[:, sl, 8:16] = np.sin(ang.astype(np.float64))
    cf[:, CF["rope"]:CF["rope"] + 17 * 16] = rp.reshape(128, -1)
    sel = np.zeros((128, 12, 64), np.float32)
    for k in range(12):
        sel[k, k, :] = 1.0
    cf[:, CF["sel"]:CF["sel"] + 768] = sel.reshape(128, -1)
    return cb.astype(ml_dtypes.bfloat16), cf.astype(np.float32)


PARAM_SHAPES = {
    'ffn1_norm': (2, 1024), 'ffn1_w1': (2, 1024, 2816), 'ffn1_w3': (2, 1024, 2816), 'ffn1_w2': (2, 2816, 1024),
    'mix_norm': (2, 1024), 'w_in': (2, 1024, 2192), 'w_out': (2, 1024, 1024),
    'fox_f_bias': (2, 4), 'fox_q_norm': (2, 64), 'fox_k_norm': (2, 64),
    'gmlp_v_norm': (2, 256), 'gmlp_w_s': (2, 4, 128, 128), 'gmlp_b_s': (2, 4, 128),
    'nsa_q_norm': (2, 64), 'nsa_kc_norm': (2, 64), 'nsa_ks_norm': (2, 64), 'nsa_kw_norm': (2, 64),
    'nsa_cmp_pos_k': (2, 32, 64), 'nsa_cmp_k_w1': (2, 2048, 256), 'nsa_cmp_k_w2': (2, 256, 64),
    'nsa_cmp_pos_v': (2, 32, 64), 'nsa_cmp_v_w1': (2, 2048, 256), 'nsa_cmp_v_w2': (2, 256, 64),
    'nsa_gate_bias': (2, 12), 'pool_w': (2, 4, 64, 64), 'pool_scale': (2, 256),
    'ffn2_norm': (2, 1024), 'ffn2_w1': (2, 1024, 2816), 'ffn2_w3': (2, 1024, 2816), 'ffn2_w2': (2, 2816, 1024),
}

ARENA_BYTES = 132 * 1024


def build(nseq, nlayers, dbg=False, stages=("ffn1", "fox", "gmlp", "pool", "nsa", "ffn2")):
    nc = bass.Bass("TRN2", target_bir_lowering=False)
    x_d = nc.dram_tensor("x", [nseq, T, D], F32, kind="ExternalInput").ap()
    y_d = nc.dram_tensor("y", [nseq, T, D], F32, kind="ExternalOutput").ap()
    dbg_d = nc.dram_tensor("dbg", [6, T, D], F32, kind="ExternalOutput").ap() if dbg else None
    W = {k: nc.dram_tensor(k, list(s), F32, kind="ExternalInput").ap() for k, s in PARAM_SHAPES.items()}
    cb_d = nc.dram_tensor("cb", [128, NCB], BF16, kind="ExternalInput").ap()
    cf_d = nc.dram_tensor("cf", [128, NCF], F32, kind="ExternalInput").ap()

    P = Prog(nc)
    st = contextlib.ExitStack()

    def sb(name, shape, dt):
        return st.enter_context(nc.sbuf_tensor(name, shape, dt))

    X = sb("X", [128, NT, D], F32)
    gb = sb("gb", [128, D], F32)
    arena = sb("arena", [128, ARENA_BYTES // 4], F32)
    identb = sb("identb", [128, 128], BF16)
    ssq = sb("ssq", [128, 16], F32)
    rstd = sb("rstd", [128, 16], F32)
    hb = [sb("hb%d" % i, [128, D], BF16) for i in range(2)]
    dummy = sb("fdummy", [128, 8], F32)
    banks = [st.enter_context(nc.psum_tensor("bank%d" % i, [128, 512], F32)) for i in range(8)]
    pTb = banks[7][:, :].bitcast(BF16)
    junk = banks[6][:, :].bitcast(BF16)

    class Arena:
        def __init__(self):
            self.off = 0

        def reset(self, off=0):
            self.off = off

        def view(self, shape, dt, parts=128):
            esz = 4 if dt == F32 else 2
            n = int(np.prod(shape[1:]))
            nbytes = (n * esz + 3) // 4 * 4
            a = arena[:, self.off // 4:(self.off + nbytes) // 4]
            if dt != F32:
                a = a.bitcast(dt)
            a = a[:, 0:n]
            self.off += nbytes
            assert self.off <= ARENA_BYTES, ("arena overflow", self.off)
            if len(shape) == 3:
                a = a.rearrange("p (a b) -> p a b", b=shape[2])
            elif len(shape) == 4:
                a = a.rearrange("p (a b c) -> p a b c", b=shape[2], c=shape[3])
            return a

    AR = Arena()

    P.dma("sp", identb[:], cb_d[:, CB["ident"]:CB["ident"] + 128], writes=["identb"])

    def norm_T(gain_ap, hT):
        P.dma("sp", gb[:], gain_ap.partition_broadcast(128), writes=["gb"])
        for i in range(NT):
            P.op("act", lambda e, i=i: e.activation(out=hb[i % 2][:], in_=X[:, i, :], func=AF.Square,
                                                   accum_out=ssq[:, i:i + 1]),
                 reads=[("X", i)], writes=[("hb", i % 2), ("ssq", i)])
        P.op("act", lambda e: e.activation(out=rstd[:], in_=ssq[:], func=AF.Sqrt, scale=1.0 / D, bias=EPS_T[:, 0:1]),
             reads=[("ssq", i) for i in range(NT)] + ["epst"], writes=["rstd_s"])
        P.op("dve", lambda e: e.reciprocal(out=rstd[:], in_=rstd[:]), reads=["rstd_s"], writes=["rstd"])
        for i in range(NT):
            b = i % 2
            P.op("dve", lambda e, i=i, b=b: e.scalar_tensor_tensor(out=hb[b][:], in0=X[:, i, :], scalar=rstd[:, i:i + 1],
                                                                in1=gb[:], op0=ALU.mult, op1=ALU.mult),
                 reads=[("X", i), "rstd", "gb"], writes=[("hb", b)])
            for c in range(8):
                P.op("pe", lambda e, b=b, c=c: e.transpose(pTb[:, c * 128:(c + 1) * 128], hb[b][:, c * 128:(c + 1) * 128], identb[:]),
                     reads=[("hb", b), "identb"], writes=["pT"])
            P.op("act", lambda e, i=i: e.activation(out=hT[:, :, i * 128:(i + 1) * 128],
                                                   in_=pTb[:, :].rearrange("p (c t) -> p c t", t=128), func=AF.Copy),
                 reads=["pT"], writes=[("hT", i)])

    EPS_T = sb("epst", [128, 4], F32)
    P.op("dve", lambda e: e.memset(EPS_T[:], EPS), writes=["epst"])

    def ffn(l, which):
        pre = "ffn%d_" % which
        w1_d = W[pre + "w1"][l].rearrange("(k p) f -> p k f", p=128)
        w3_d = W[pre + "w3"][l].rearrange("(k p) f -> p k f", p=128)
        w2_d = W[pre + "w2"][l]
        AR.reset()
        hT = AR.view([128, 8, T], BF16)
        gT = AR.view([128, 6, T], BF16)
        w2b = [AR.view([128, 6, D], BF16) for _ in range(2)]
        w13 = [AR.view([128, 2, 8, 256], BF16) for _ in range(3)]
        sil = [AR.view([128, 512], F32) for _ in range(2)]
        norm_T(W[pre + "norm"][l], hT)
        import os
        lvl = int(os.environ.get("FFN_LEVEL", "2"))
        groups = [(0, 6), (6, 6), (12, 5), (17, 5)]
        if lvl == 0:
            groups = []
        cnt = 0
        oc = 0
        uc = 0
        for g, (c0, n) in enumerate(groups):
            wb = w2b[g % 2]
            P.dma("pool", wb[:, 0:n, :], w2_d[c0 * 128:(c0 + n) * 128, :].rearrange("(c p) f -> p c f", p=128),
                  writes=[("w2b", g % 2)])
            units = [(c0 + u, min(2, n - u)) for u in range(0, n, 2)]
            for (j0, nj) in units:
                slot = uc % 3
                uc += 1
                ws = w13[slot]
                P.dma("pool", ws[:, 0, :, 0:nj * 128], w1_d[:, :, j0 * 128:(j0 + nj) * 128], writes=[("w13a", slot)])
                P.dma("pool", ws[:, 1, :, 0:nj * 128], w3_d[:, :, j0 * 128:(j0 + nj) * 128], writes=[("w13b", slot)])
                for tb in range(4):
                    hreads = [("hT", 4 * tb + qq) for qq in range(4)]
                    for jj in range(nj):
                        jl = j0 + jj - c0
                        r = cnt % 2
                        cnt += 1
                        pa = banks[r]
                        pb = banks[2 + r]
                        for k in range(8):
                            P.op("pe", lambda e, pa=pa, ws=ws, k=k, jj=jj, tb=tb: e.matmul(
                                pa[:, :], lhsT=ws[:, 0, k, jj * 128:(jj + 1) * 128], rhs=hT[:, k, tb * 512:(tb + 1) * 512],
                                start=(k == 0), stop=(k == 7)), reads=[("w13a", slot)] + hreads, writes=[("pa", r)])
                        for k in range(8):
                            P.op("pe", lambda e, pb=pb, ws=ws, k=k, jj=jj, tb=tb: e.matmul(
                                pb[:, :], lhsT=ws[:, 1, k, jj * 128:(jj + 1) * 128], rhs=hT[:, k, tb * 512:(tb + 1) * 512],
                                start=(k == 0), stop=(k == 7)), reads=[("w13b", slot)] + hreads, writes=[("pb", r)])
                        P.op("act", lambda e, pa=pa, r=r: e.activation(out=sil[r][:], in_=pa[:, :], func=AF.Silu),
                             reads=[("pa", r)], writes=[("sil", r)])
                        P.op("dve", lambda e, pb=pb, r=r, jl=jl, tb=tb: e.tensor_tensor(
                            out=gT[:, jl, tb * 512:(tb + 1) * 512], in0=pb[:, :], in1=sil[r][:], op=ALU.mult),
                            reads=[("pb", r), ("sil", r)], writes=[("gT", jl, tb)])
            for i in range(NT if lvl >= 2 else 0):
                for half in range(2):
                    r = oc % 2
                    oc += 1
                    po = banks[4 + r]
                    for jl in range(n):
                        P.op("pe", lambda e, po=po, jl=jl, i=i, half=half, wb=wb: e.matmul(
                            po[:, :], lhsT=gT[:, jl, i * 128:(i + 1) * 128], rhs=wb[:, jl, half * 512:(half + 1) * 512],
                            start=(jl == 0), stop=(jl == n - 1)),
                            reads=[("gT", jl, i // 4), ("w2b", g % 2)], writes=[("po", r)])
                    if os.environ.get("STT_ALT", "0") == "1":
                        P.op("dve", lambda e, po=po, i=i, half=half: e.tensor_tensor(
                            out=X[:, i, half * 512:(half + 1) * 512], in0=po[:, :],
                            in1=X[:, i, half * 512:(half + 1) * 512], op=ALU.add),
                            reads=[("po", r), ("X", i)], writes=[("X", i)])
                    else:
                        P.op("dve", lambda e, po=po, i=i, half=half: e.scalar_tensor_tensor(
                            out=X[:, i, half * 512:(half + 1) * 512], in0=po[:, :], scalar=EPS_T[:, 2:3],
                            in1=X[:, i, half * 512:(half + 1) * 512], op0=ALU.mult, op1=ALU.add),
                            reads=[("po", r), ("X", i), "eps1"], writes=[("X", i)])

    def mixer(l, seq_stages):
        AR.reset()
        hT = AR.view([128, 8, T], BF16)
        wbuf = AR.view([128, 8, 772], BF16)
        post_limit = AR.off
        cbt = AR.view([128, CB["mcmp"]], BF16)
        cft = AR.view([128, NCF], F32)
        OT = AR.view([128, 4, 2, 128], BF16)
        wout = AR.view([128, 4, D], BF16)
        scrA = AR.view([128, 640], F32)
        scrB = AR.view([128, 640], F32)
        scrC = AR.view([128, 640], F32)
        sm = AR.view([128, 64], F32)
        PT = [AR.view([128, 512], BF16) for _ in range(3)]
        base_off = AR.off

        def cbv(name, n=128):
            return cbt[:, CB[name]:CB[name] + n]

        def cfv(name, n=128):
            return cft[:, CF[name]:CF[name] + n]

        P.dma("sp", cbt[:], cb_d[:, 0:CB["mcmp"]], writes=["cbt"])
        P.dma("sp", cft[:], cf_d[:, :], writes=["cft"])
        norm_T(W["mix_norm"][l], hT)
        w_in = W["w_in"][l].rearrange("(k p) f -> p k f", p=128)
        w_out = W["w_out"][l]
        HR = [("hT", i) for i in range(NT)]

        def load_win(c0, n):
            P.dma("pool", wbuf[:, :, 0:n], w_in[:, :, c0:c0 + n], writes=["wbuf"])

        def load_wout(m):
            P.dma("pool", wout[0:64, :, :], w_out[m * 256:(m + 1) * 256, :].rearrange("(c p) f -> p c f", p=64),
                  writes=["wout"])

        def proj_tm(i, col0, ncols, bank, btag, c_in_buf=0):
            for k in range(8):
                P.op("pe", lambda e, k=k: e.matmul(bank[:, 0:ncols], lhsT=hT[:, k, i * 128:(i + 1) * 128],
                                                   rhs=wbuf[:, k, c_in_buf:c_in_buf + ncols], start=(k == 0), stop=(k == 7)),
                     reads=[("hT", i), "wbuf"], writes=[btag])

        def wout_tile(i, buf):
            for half in range(2):
                bk = banks[half]
                for c in range(4):
                    P.op("pe", lambda e, c=c, half=half, bk=bk: e.matmul(
                        bk[:, :], lhsT=OT[0:64, c, buf, :], rhs=wout[0:64, c, half * 512:(half + 1) * 512],
                        start=(c == 0), stop=(c == 3)), reads=[("OT", buf), "wout"], writes=[("b", half)])
                P.op("dve", lambda e, half=half, bk=bk: e.tensor_tensor(
                    out=X[:, i, half * 512:(half + 1) * 512], in0=bk[:, :], in1=X[:, i, half * 512:(half + 1) * 512],
                    op=ALU.add), reads=[("b", half), ("X", i)], writes=[("X", i)])

        def rstd_small(src_ap, dst_ap, n, tagr, tagw):
            npart = dst_ap.shape[0]
            P.op("act", lambda e: e.activation(out=dst_ap, in_=src_ap, func=AF.Sqrt, scale=1.0 / 64, bias=EPS_T[0:npart, 0:1]),
                 reads=[tagr, "epst"], writes=[tagw + "_s"])
            P.op("dve", lambda e: e.reciprocal(out=dst_ap, in_=dst_ap), reads=[tagw + "_s"], writes=[tagw])

        def fox():
            AR.reset(base_off)
            qT2 = AR.view([128, 2, T], BF16)
            kT2 = AR.view([128, 2, T], BF16)
            Vaug = AR.view([128, NT, 4, 128], BF16)
            Bt = AR.view([128, 4, NT, NT], F32)
            zt = AR.view([128, NT, 4], F32)
            ctm = AR.view([128, NT, 4], F32)
            cmb = AR.view([128, NT, 4], F32)
            off = AR.view([128, NT, 4], F32)
            totb = AR.view([128, NT, 4], F32)
            Gqk = AR.view([128, 8, 64], F32)
            fbb = AR.view([128, 4], F32)
            qkb = [AR.view([128, 512], BF16) for _ in range(2)]
            load_win(0, 772)
            load_wout(0)
            for hh in range(4):
                P.dma("sp", Gqk[:, hh, :], W["fox_q_norm"][l].partition_broadcast(128), writes=[("Gqk", hh)])
                P.dma("sp", Gqk[:, 4 + hh, :], W["fox_k_norm"][l].partition_broadcast(128), writes=[("Gqk", 4 + hh)])
            GQ = [("Gqk", hh) for hh in range(8)]
            P.dma("sp", fbb[:], W["fox_f_bias"][l].partition_broadcast(128), writes=["fbb"])
            P.op("dve", lambda e: e.memset(Vaug[:, :, :, 64:128], 1.0), writes=["vones"])
            for i in range(NT):
                proj_tm(i, 0, 512, banks[0], ("b", 0), 0)
                proj_tm(i, 512, 260, banks[1], ("b", 1), 512)
                P.op("act", lambda e: e.activation(out=scrA[:, 0:512], in_=banks[0][:, :], func=AF.Square),
                     reads=[("b", 0)], writes=["scrA"])
                P.op("dve", lambda e: e.tensor_reduce(out=sm[:, 0:8], in_=scrA[:, 0:512].rearrange("p (a b) -> p a b", b=64),
                                                      axis=AX.X, op=ALU.add), reads=["scrA"], writes=["sm"])
                rstd_small(sm[:, 0:8], sm[:, 8:16], 8, "sm", "sm2")
                P.op("dve", lambda e: e.tensor_tensor(out=scrB[:, 0:512].rearrange("p (a b) -> p a b", b=64),
                                                      in0=banks[0][:, :].rearrange("p (a b) -> p a b", b=64),
                                                      in1=sm[:, 8:16].unsqueeze(2).to_broadcast([128, 8, 64]), op=ALU.mult),
                     reads=[("b", 0), "sm2"], writes=["scrB"])
                qb_ = qkb[i % 2]
                P.op("dve", lambda e, qb_=qb_: e.tensor_tensor(out=qb_[:], in0=scrB[:, 0:512],
                                                                in1=Gqk[:, :, :].rearrange("p a b -> p (a b)"), op=ALU.mult),
                     reads=["scrB"] + GQ, writes=[("qkb", i % 2)])
                for c in range(4):
                    P.op("pe", lambda e, c=c, qb_=qb_: e.transpose(pTb[:, c * 128:(c + 1) * 128], qb_[:, c * 128:(c + 1) * 128], identb[:]),
                         reads=[("qkb", i % 2), "identb"], writes=["pT"])
                P.op("act", lambda e, i=i: e.activation(out=qT2[:, :, i * 128:(i + 1) * 128],
                                                       in_=pTb[:, 0:256].rearrange("p (c t) -> p c t", t=128), func=AF.Copy),
                     reads=["pT"], writes=[("qT2", i)])
                P.op("act", lambda e, i=i: e.activation(out=kT2[:, :, i * 128:(i + 1) * 128],
                                                       in_=pTb[:, 256:512].rearrange("p (c t) -> p c t", t=128), func=AF.Copy),
                     reads=["pT"], writes=[("kT2", i)])
                P.op("act", lambda e, i=i: e.activation(out=Vaug[:, i, :, 0:64],
                                                       in_=banks[1][:, 0:256].rearrange("p (a b) -> p a b", b=64), func=AF.Copy),
                     reads=[("b", 1)], writes=[("V", i)])
                P.op("dve", lambda e, i=i: e.tensor_tensor(out=zt[:, i, :], in0=banks[1][:, 256:260], in1=fbb[:], op=ALU.add),
                     reads=[("b", 1), "fbb"], writes=[("zt", i)])
            ZT = [("zt", i) for i in range(NT)]
            ztf = zt[:, :, :].rearrange("p a b -> p (a b)")
            P.op("act", lambda e: e.activation(out=ztf, in_=ztf, func=AF.Exp, scale=-1.0), reads=ZT, writes=["z1"])
            P.op("act", lambda e: e.activation(out=ztf, in_=ztf, func=AF.Ln, bias=EPS_T[:, 1:2]), reads=["z1", "eps1"], writes=["z2"])
            P.op("dve", lambda e: e.tensor_scalar(out=ztf, in0=ztf, scalar1=-1.0, scalar2=None, op0=ALU.mult), reads=["z2"], writes=["logf"])
            P.op("pe", lambda e: e.matmul(banks[0][:, 0:64], lhsT=cfv("ltri"), rhs=ztf, start=True, stop=True),
                 reads=["logf", "cft"], writes=[("b", 0)])
            P.op("pe", lambda e: e.matmul(banks[0][:, 64:128], lhsT=cfv("onesf"), rhs=ztf, start=True, stop=True),
                 reads=["logf", "cft"], writes=[("b", 0)])
            totf = totb[:, :, :].rearrange("p a b -> p (a b)")
            P.op("act", lambda e: e.activation(out=totf, in_=banks[0][:, 64:128], func=AF.Copy), reads=[("b", 0)], writes=["totb"])
            P.op("dve", lambda e: e.memset(off[:, 0, :], 0.0), writes=["off"])
            for j in range(1, NT):
                P.op("dve", lambda e, j=j: e.tensor_tensor(out=off[:, j, :], in0=off[:, j - 1, :], in1=totb[:, j - 1, :], op=ALU.add),
                     reads=["off", "totb"], writes=["off"])
            ctf = ctm[:, :, :].rearrange("p a b -> p (a b)")
            P.op("dve", lambda e: e.tensor_tensor(out=ctf, in0=banks[0][:, 0:64], in1=off[:, :, :].rearrange("p a b -> p (a b)"), op=ALU.add),
                 reads=[("b", 0), "off"], writes=["ctm"])
            P.op("pe", lambda e: e.matmul(banks[1][:, 0:64], lhsT=cfv("row64"), rhs=ctf, start=True, stop=True),
                 reads=["ctm", "cft"], writes=[("b", 1)])
            P.op("act", lambda e: e.activation(out=cmb[:, :, :].rearrange("p a b -> p (a b)"), in_=banks[1][:, 0:64], func=AF.Copy),
                 reads=[("b", 1)], writes=["cmb"])
            for hh in range(4):
                for ks in range(NT):
                    P.op("dve", lambda e, hh=hh, ks=ks: e.tensor_scalar(
                        out=Bt[:, hh, ks, :], in0=cmb[:, :, hh], scalar1=ctm[:, ks, hh:hh + 1], scalar2=None, op0=ALU.subtract),
                        reads=["cmb", "ctm"], writes=[("Bt", hh)])
            sc = 0
            ac = 0
            for j in range(NT):
                buf = j % 2
                for hh in range(4):
                    pr = (hh % 2) * 64
                    pc = hh // 2
                    accb = banks[5 + ac % 2]
                    acct = ("acc", ac % 2)
                    ac += 1
                    for ks in range(j + 1):
                        r = sc % 3
                        sc += 1
                        sb_ = banks[2 + r]
                        P.op("pe", lambda e, sb_=sb_, pr=pr, pc=pc, ks=ks, j=j: e.matmul(
                            sb_[:, 0:128], lhsT=kT2[pr:pr + 64, pc, ks * 128:(ks + 1) * 128],
                            rhs=qT2[pr:pr + 64, pc, j * 128:(j + 1) * 128], start=True, stop=(ks != j)),
                            reads=[("kT2", ks), ("qT2", j)], writes=[("st", r)])
                        if ks == j:
                            P.op("pe", lambda e, sb_=sb_: e.matmul(sb_[:, 0:128], lhsT=identb[:], rhs=cbv("caus"), start=False, stop=True),
                                 reads=["identb", "cbt"], writes=[("st", r)])
                        P.op("act", lambda e, sb_=sb_, r=r, hh=hh, ks=ks, j=j: e.activation(
                            out=PT[r][:, 0:128], in_=sb_[:, 0:128], func=AF.Exp, scale=0.125, bias=Bt[:, hh, ks, j:j + 1]),
                            reads=[("st", r), ("Bt", hh)], writes=[("PT", r)])
                        P.op("pe", lambda e, accb=accb, r=r, hh=hh, ks=ks, j=j: e.matmul(
                            accb[:, 0:128], lhsT=Vaug[:, ks, hh, :], rhs=PT[r][:, 0:128], start=(ks == 0), stop=(ks == j)),
                            reads=[("V", ks), "vones", ("PT", r)], writes=[acct])
                    P.op("act", lambda e, accb=accb: e.activation(out=scrA[0:64, 0:128], in_=accb[64:128, 0:128], func=AF.Copy),
                         reads=[acct], writes=["scrA"])
                    P.op("dve", lambda e: e.reciprocal(out=scrA[0:64, 0:128], in_=scrA[0:64, 0:128]),
                         reads=["scrA"], writes=["scrA"])
                    P.op("dve", lambda e, accb=accb, hh=hh, buf=buf: e.tensor_tensor(
                        out=OT[0:64, hh, buf, :], in0=accb[0:64, 0:128], in1=scrA[0:64, 0:128], op=ALU.mult),
                        reads=[acct, "scrA"], writes=[("OT", buf)])
                wout_tile(j, buf)

        def gmlp():
            AR.reset(base_off)
            uT = AR.view([128, 4, T], BF16)
            vgt = AR.view([128, NT, 256], BF16)
            wsn = AR.view([128, 4, 128], F32)
            wsb = AR.view([128, 4, 128], BF16)
            WT = AR.view([128, 4, 128], BF16)
            Gv = AR.view([128, 256], F32)
            bsf = AR.view([128, 512], F32)
            bsh = AR.view([128, 512], BF16)
            bsl = AR.view([128, 512], BF16)
            load_win(772, 512)
            load_wout(1)
            P.dma("sp", Gv[:], W["gmlp_v_norm"][l].partition_broadcast(128), writes=["Gv"])
            P.dma("sp", wsn[:], W["gmlp_w_s"][l].rearrange("g t s -> t g s"), writes=["wsn"])
            P.dma("sp", bsf[0:1, :], W["gmlp_b_s"][l:l + 1].rearrange("o g t -> o (g t)"), writes=["bsf"])
            P.op("dve", lambda e: e.tensor_copy(out=bsh[0:1, :], in_=bsf[0:1, :]), reads=["bsf"], writes=["bsh"])
            P.op("dve", lambda e: e.tensor_tensor(out=bsl[0:1, :], in0=bsf[0:1, :], in1=bsh[0:1, :], op=ALU.subtract),
                 reads=["bsf", "bsh"], writes=["bsl"])
            P.op("dve", lambda e: e.tensor_tensor(out=wsb[:], in0=wsn[:],
                                                  in1=cfv("tril").unsqueeze(1).to_broadcast([128, 4, 128]), op=ALU.mult),
                 reads=["wsn", "cft"], writes=["wsb"])
            for g in range(4):
                P.op("pe", lambda e, g=g: e.transpose(pTb[:, g * 128:(g + 1) * 128], wsb[:, g, :], identb[:]),
                     reads=["wsb", "identb"], writes=["pT"])
            P.op("act", lambda e: e.activation(out=WT[:], in_=pTb[:, 0:512].rearrange("p (g t) -> p g t", t=128), func=AF.Copy),
                 reads=["pT"], writes=["WT"])
            uc = 0
            for g in range(4):
                for tb in range(4):
                    r = uc % 2
                    uc += 1
                    bk = banks[2 + r]
                    for k in range(8):
                        P.op("pe", lambda e, bk=bk, g=g, tb=tb, k=k: e.matmul(
                            bk[0:64, :], lhsT=wbuf[:, k, g * 64:(g + 1) * 64], rhs=hT[:, k, tb * 512:(tb + 1) * 512],
                            start=(k == 0), stop=(k == 7)), reads=["wbuf"] + HR, writes=[("st", r)])
                    P.op("act", lambda e, bk=bk, g=g, tb=tb: e.activation(out=uT[0:64, g, tb * 512:(tb + 1) * 512], in_=bk[0:64, :],
                                                                         func=AF.Gelu_apprx_tanh), reads=[("st", r)], writes=[("uT", tb)])
            for i in range(NT):
                proj_tm(i, 1028, 256, banks[0], ("b", 0), 256)
                P.op("act", lambda e: e.activation(out=scrA[:, 0:256], in_=banks[0][:, 0:256], func=AF.Gelu_apprx_tanh),
                     reads=[("b", 0)], writes=["scrA"])
                P.op("act", lambda e: e.activation(out=scrB[:, 0:256], in_=scrA[:, 0:256], func=AF.Square),
                     reads=["scrA"], writes=["scrB"])
                P.op("dve", lambda e: e.tensor_reduce(out=sm[:, 0:4], in_=scrB[:, 0:256].rearrange("p (a b) -> p a b", b=64),
                                                      axis=AX.X, op=ALU.add), reads=["scrB"], writes=["sm"])
                rstd_small(sm[:, 0:4], sm[:, 8:12], 4, "sm", "sm2")
                P.op("dve", lambda e: e.tensor_tensor(out=scrB[:, 0:256].rearrange("p (a b) -> p a b", b=64),
                                                      in0=scrA[:, 0:256].rearrange("p (a b) -> p a b", b=64),
                                                      in1=sm[:, 8:12].unsqueeze(2).to_broadcast([128, 4, 64]), op=ALU.mult),
                     reads=["scrA", "sm2", "scrB"], writes=["scrB"])
                P.op("dve", lambda e, i=i: e.tensor_tensor(out=vgt[:, i, :], in0=scrB[:, 0:256], in1=Gv[:], op=ALU.mult),
                     reads=["scrB", "Gv"], writes=[("vgt", i)])
                r = i % 2
                bk = banks[4 + r]
                for g in range(4):
                    P.op("pe", lambda e, bk=bk, g=g, i=i: e.matmul(bk[0:64, g * 128:(g + 1) * 128], lhsT=vgt[:, i, g * 64:(g + 1) * 64],
                                                                  rhs=WT[:, g, :], start=True, stop=False),
                         reads=[("vgt", i), "WT"], writes=[("po", r)])
                    P.op("pe", lambda e, bk=bk, g=g: e.matmul(bk[0:64, g * 128:(g + 1) * 128], lhsT=cbv("ones")[0:1, 0:64],
                                                             rhs=bsh[0:1, g * 128:(g + 1) * 128], start=False, stop=False),
                         reads=["cbt", "bsh"], writes=[("po", r)])
                    P.op("pe", lambda e, bk=bk, g=g: e.matmul(bk[0:64, g * 128:(g + 1) * 128], lhsT=cbv("ones")[0:1, 0:64],
                                                             rhs=bsl[0:1, g * 128:(g + 1) * 128], start=False, stop=True),
                         reads=["cbt", "bsl"], writes=[("po", r)])
                P.op("dve", lambda e, bk=bk, i=i, r=r: e.tensor_tensor(
                    out=OT[0:64, :, r, :], in0=bk[0:64, :].rearrange("p (g t) -> p g t", t=128),
                    in1=uT[0:64, :, i * 128:(i + 1) * 128], op=ALU.mult),
                    reads=[("po", r), ("uT", i // 4)], writes=[("OT", r)])
                wout_tile(i, r)

        def pool():
            AR.reset(base_off)
            zt = AR.view([128, NT, 256], BF16)
            wp = AR.view([128, 4, 64], BF16)
            psc = AR.view([128, 4], F32)
            pl = [AR.view([128, 4, 128], BF16) for _ in range(2)]
            load_win(1936, 256)
            load_wout(3)
            P.dma("pool", wp[0:64, :, :], W["pool_w"][l].rearrange("g d e -> d g e"), writes=["wp"])
            P.dma("sp", psc[0:64, :], W["pool_scale"][l].rearrange("(g e) -> e g", e=64), writes=["psc"],
                  allow_slow_non_contiguous=True)
            for i in range(NT):
                proj_tm(i, 1936, 256, banks[0], ("b", 0), 0)
                P.op("act", lambda e, i=i: e.activation(out=zt[:, i, :], in_=banks[0][:, 0:256], func=AF.Copy),
                     reads=[("b", 0)], writes=[("zt", i)])
                r = i % 2
                bk = banks[2 + r]
                for g in range(4):
                    o_ = bk[0:64, g * 128:(g + 1) * 128]
                    lh = zt[:, i, g * 64:(g + 1) * 64]
                    if i == 0:
                        P.op("pe", lambda e, o_=o_, lh=lh, g=g: e.matmul(o_, lhsT=lh, rhs=cbv("a0h", 512)[:, g * 128:(g + 1) * 128], start=True, stop=False),
                             reads=[("zt", i), "cbt"], writes=[("st", r)])
                        P.op("pe", lambda e, o_=o_, lh=lh, g=g: e.matmul(o_, lhsT=lh, rhs=cbv("a0l", 512)[:, g * 128:(g + 1) * 128], start=False, stop=True),
                             reads=[("zt", i), "cbt"], writes=[("st", r)])
                    else:
                        lp = zt[:, i - 1, g * 64:(g + 1) * 64]
                        P.op("pe", lambda e, o_=o_, lh=lh, g=g: e.matmul(o_, lhsT=lh, rhs=cbv("ad", 512)[:, g * 128:(g + 1) * 128], start=True, stop=False),
                             reads=[("zt", i), "cbt"], writes=[("st", r)])
                        P.op("pe", lambda e, o_=o_, lp=lp, g=g: e.matmul(o_, lhsT=lp, rhs=cbv("ap", 512)[:, g * 128:(g + 1) * 128], start=False, stop=True),
                             reads=[("zt", i - 1), "cbt"], writes=[("st", r)])
                P.op("act", lambda e, bk=bk, r=r: e.activation(out=pl[r][0:64, :, :], in_=bk[0:64, :].rearrange("p (g t) -> p g t", t=128), func=AF.Copy),
                     reads=[("st", r)], writes=[("pl", r)])
                bk2 = banks[4 + r]
                for g in range(4):
                    P.op("pe", lambda e, bk2=bk2, g=g, r=r: e.matmul(bk2[0:64, g * 128:(g + 1) * 128], lhsT=wp[0:64, g, :], rhs=pl[r][0:64, g, :],
                                                                   start=True, stop=True), reads=["wp", ("pl", r)], writes=[("po", r)])
                for g in range(4):
                    P.op("dve", lambda e, bk2=bk2, g=g, r=r: e.tensor_scalar(out=OT[0:64, g, r, :], in0=bk2[0:64, g * 128:(g + 1) * 128],
                                                                            scalar1=psc[0:64, g:g + 1], scalar2=None, op0=ALU.mult),
                         reads=[("po", r), "psc"], writes=[("OT", r)])
                wout_tile(i, r)

        def nsa():
            AR.reset(base_off)
            qT4 = AR.view([128, 4, T], BF16)
            KV3 = AR.view([128, 3, T], BF16)
            vsa = AR.view([128, NT, 128], BF16)
            vwa = AR.view([128, NT, 128], BF16)
            gTt = AR.view([128, T], F32)
            gbias = AR.view([128, 4], F32)
            Gall = AR.view([128, 10, 64], F32)
            NBb = [AR.view([128, 640], BF16) for _ in range(2)]
            rt = [AR.view([128, 4, 8], F32) for _ in range(4)]
            P.op("dve", lambda e: e.memset(wbuf[:, :, 652:704], 0.0), writes=["wbuf"])
            load_win(1284, 652)
            load_wout(2)
            P.op("dve", lambda e: e.memset(Gall[:, :, :], 1.0), writes=["GallI"])
            for hh in range(4):
                P.dma("sp", Gall[:, hh, :], W["nsa_q_norm"][l].partition_broadcast(128), reads=["GallI"], writes=[("Gall", hh)])
            P.dma("sp", Gall[:, 6, :], W["nsa_ks_norm"][l].partition_broadcast(128), reads=["GallI"], writes=[("Gall", 6)])
            P.dma("sp", Gall[:, 8, :], W["nsa_kw_norm"][l].partition_broadcast(128), reads=["GallI"], writes=[("Gall", 8)])
            GA = ["GallI"] + [("Gall", hh) for hh in (0, 1, 2, 3, 6, 8)]
            P.op("dve", lambda e: e.memset(gbias[:, :], 0.0), writes=["gbias0"])
            P.dma("sp", gbias[0:12, 0:1], W["nsa_gate_bias"][l].rearrange("(c o) -> c o", o=1), reads=["gbias0"], writes=["gbias"])
            P.op("dve", lambda e: e.memset(vsa[:, :, 64:128], 1.0), writes=["vsones"])
            P.op("dve", lambda e: e.memset(vwa[:, :, 64:128], 1.0), writes=["vwones"])
            rope = cfv("rope", 17 * 16).rearrange("p (s c) -> p s c", c=16)

            def rope_ops(view, nh, slot, wtag):
                x1 = view[:, :, 0:8]
                x2 = view[:, :, 8:16]
                cosb = rope[:, slot, 0:8].unsqueeze(1).to_broadcast([128, nh, 8])
                sinb = rope[:, slot, 8:16].unsqueeze(1).to_broadcast([128, nh, 8])
                ra, rb_, rc, rd = [t_[:, 0:nh, :] for t_ in rt]
                P.op("dve", lambda e: e.tensor_tensor(out=ra, in0=x1, in1=cosb, op=ALU.mult), reads=[wtag, "cft"], writes=["ra"])
                P.op("dve", lambda e: e.tensor_tensor(out=rb_, in0=x2, in1=sinb, op=ALU.mult), reads=[wtag, "cft"], writes=["rb"])
                P.op("dve", lambda e: e.tensor_tensor(out=rc, in0=x2, in1=cosb, op=ALU.mult), reads=[wtag, "cft"], writes=["rc"])
                P.op("dve", lambda e: e.tensor_tensor(out=rd, in0=x1, in1=sinb, op=ALU.mult), reads=[wtag, "cft"], writes=["rd"])
                P.op("dve", lambda e: e.tensor_tensor(out=x1, in0=ra, in1=rb_, op=ALU.subtract), reads=["ra", "rb", wtag], writes=[wtag])
                P.op("dve", lambda e: e.tensor_tensor(out=x2, in0=rc, in1=rd, op=ALU.add), reads=["rc", "rd", wtag], writes=[wtag])

            for i in range(NT):
                proj_tm(i, 1284, 512, banks[0], ("b", 0), 0)
                proj_tm(i, 1796, 140, banks[1], ("b", 1), 512)
                P.op("act", lambda e: e.activation(out=scrA[:, 0:512], in_=banks[0][:, :], func=AF.Copy), reads=[("b", 0)], writes=["scrA"])
                P.op("act", lambda e: e.activation(out=scrA[:, 512:640], in_=banks[1][:, 0:128], func=AF.Copy), reads=[("b", 1), "scrA"], writes=["scrA"])
                P.op("act", lambda e: e.activation(out=scrB[:, 0:640], in_=scrA[:, 0:640], func=AF.Square), reads=["scrA"], writes=["scrB"])
                P.op("dve", lambda e: e.tensor_reduce(out=sm[:, 0:10], in_=scrB[:, 0:640].rearrange("p (a b) -> p a b", b=64),
                                                      axis=AX.X, op=ALU.add), reads=["scrB"], writes=["sm"])
                rstd_small(sm[:, 0:10], sm[:, 16:26], 10, "sm", "sm2")
                P.op("dve", lambda e: e.tensor_tensor(out=scrB[:, 0:640].rearrange("p (a b) -> p a b", b=64),
                                                      in0=scrA[:, 0:640].rearrange("p (a b) -> p a b", b=64),
                                                      in1=sm[:, 16:26].unsqueeze(2).to_broadcast([128, 10, 64]), op=ALU.mult),
                     reads=["scrA", "sm2", "scrB"], writes=["scrB"])
                P.op("dve", lambda e: e.tensor_tensor(out=scrC[:, 0:640], in0=scrB[:, 0:640],
                                                      in1=Gall[:, :, :].rearrange("p a b -> p (a b)"), op=ALU.mult),
                     reads=["scrB"] + GA, writes=["scrC"])
                for (c0_, c1_) in ((256, 384), (448, 512), (576, 640)):
                    P.op("act", lambda e, c0_=c0_, c1_=c1_: e.activation(out=scrC[:, c0_:c1_], in_=scrA[:, c0_:c1_], func=AF.Copy),
                         reads=["scrA", "scrC"], writes=["scrC"])
                v3 = scrC[:, 0:640].rearrange("p (a b) -> p a b", b=64)
                rope_ops(v3[:, 0:4, :], 4, i, "scrC")
                rope_ops(v3[:, 6:7, :], 1, i, "scrC")
                rope_ops(v3[:, 8:9, :], 1, i, "scrC")
                nb = NBb[i % 2]
                P.op("act", lambda e, nb=nb: e.activation(out=nb[:], in_=scrC[:, 0:640], func=AF.Copy), reads=["scrC"], writes=[("NBb", i % 2)])
                for c in range(5):
                    P.op("pe", lambda e, nb=nb, c=c: e.transpose(pTb[:, c * 128:(c + 1) * 128], nb[:, c * 128:(c + 1) * 128], identb[:]),
                         reads=[("NBb", i % 2), "identb"], writes=["pT"])
                for hh in range(4):
                    pr_ = (hh % 2) * 64
                    if hh % 2 == 0:
                        P.op("act", lambda e, i=i, hh=hh, pr_=pr_: e.activation(out=qT4[0:64, hh, i * 128:(i + 1) * 128],
                                                                               in_=pTb[pr_:pr_ + 64, (hh // 2) * 128:(hh // 2 + 1) * 128], func=AF.Copy),
                             reads=["pT"], writes=[("qT4", i, hh)])
                    else:
                        P.op("act", lambda e, i=i, hh=hh, pr_=pr_: e.activation(out=qT4[0:64, hh, i * 128:(i + 1) * 128],
                                                                               in_=pTb[pr_:pr_ + 64, (hh // 2) * 128:(hh // 2 + 1) * 128], func=AF.Copy),
                             reads=["pT"], writes=[("qT4", i, hh)])
                P.op("act", lambda e, i=i: e.activation(out=KV3[:, :, i * 128:(i + 1) * 128],
                                                       in_=pTb[:, 256:640].rearrange("p (c t) -> p c t", t=128), func=AF.Copy),
                     reads=["pT"], writes=[("KV3", i)])
                P.op("dve", lambda e, nb=nb, i=i: e.tensor_copy(out=vsa[:, i, 0:64], in_=nb[:, 448:512]), reads=[("NBb", i % 2)], writes=[("vsa", i)])
                P.op("dve", lambda e, nb=nb, i=i: e.tensor_copy(out=vwa[:, i, 0:64], in_=nb[:, 576:640]), reads=[("NBb", i % 2)], writes=[("vwa", i)])
            for tb in range(4):
                r = tb % 2
                bk = banks[2 + r]
                for k in range(8):
                    P.op("pe", lambda e, bk=bk, tb=tb, k=k: e.matmul(bk[0:64, :], lhsT=wbuf[:, k, 640:704], rhs=hT[:, k, tb * 512:(tb + 1) * 512],
                                                                   start=(k == 0), stop=(k == 7)), reads=["wbuf"] + HR, writes=[("st", r)])
                P.op("act", lambda e, bk=bk, tb=tb: e.activation(out=gTt[0:64, tb * 512:(tb + 1) * 512], in_=bk[0:64, :], func=AF.Sigmoid,
                                                                bias=gbias[0:64, 0:1]), reads=[("st", r), "gbias", "gbias0"], writes=[("gT", tb)])
            P.fence(dummy[:])
            import os
            nlvl = int(os.environ.get("NSA_LEVEL", "9"))
            if nlvl == 0:
                return
            AR.reset(0)
            kcp = AR.view([128, 32, 128], BF16)
            W1 = AR.view([128, 32, 256], BF16)
            mcmp = AR.view([128, T], BF16)
            Et = AR.view([128, T], BF16)
            W2 = AR.view([128, 2, 2, 64], BF16)
            posn = AR.view([128, 128], F32)
            pos2T = AR.view([128, 32], F32)
            hg = AR.view([128, 4, 128], BF16)
            kcn = AR.view([128, 64], F32)
            kcnb = AR.view([128, 128], BF16)
            kcmpT = AR.view([128, 128], BF16)
            vca = AR.view([128, 128], BF16)
            gkc = AR.view([128, 64], F32)
            MT = [AR.view([128, 128], BF16) for _ in range(2)]
            impt = AR.view([128, 64], F32)
            m8 = AR.view([128, 16], F32)
            Mtm = AR.view([128, 128], BF16)
            scrD = AR.view([128, 512], F32)
            gs = AR.view([128, 512], F32)
            assert AR.off <= post_limit, ("nsa post overflow", AR.off, post_limit)
            P.op("dve", lambda e: e.memset(Mtm[:, :], 0.0), writes=["Mtm0"])
            P.op("dve", lambda e: e.memset(posn[:, :], 0.0), writes=["posn0"])
            P.dma("sp", mcmp[:], cb_d[:, CB["mcmp"]:CB["mcmp"] + T], writes=["mcmp"])
            P.dma("sp", Et[0:32, :], cb_d[0:32, CB["E"]:CB["E"] + T], writes=["Et"])
            P.dma("pool", W1[0:64, :, :], W["nsa_cmp_k_w1"][l].rearrange("(l d) j -> d l j", d=64), writes=["W1k"])
            P.dma("pool", W1[64:128, :, :], W["nsa_cmp_v_w1"][l].rearrange("(l d) j -> d l j", d=64), writes=["W1v"])
            P.dma("pool", W2[:, 0, :, :], W["nsa_cmp_k_w2"][l].rearrange("(c p) d -> p c d", p=128), writes=["W2k"])
            P.dma("pool", W2[:, 1, :, :], W["nsa_cmp_v_w2"][l].rearrange("(c p) d -> p c d", p=128), writes=["W2v"])
            P.dma("sp", posn[0:32, 0:64], W["nsa_cmp_pos_k"][l], reads=["posn0"], writes=["posnk"])
            P.dma("sp", posn[0:32, 64:128], W["nsa_cmp_pos_v"][l], reads=["posn0"], writes=["posnv"])
            P.dma("sp", gkc[:], W["nsa_kc_norm"][l].partition_broadcast(128), writes=["gkc"])
            if nlvl == 10:
                return
            P.op("pe", lambda e: e.transpose(banks[0][:, 0:128], posn[:, :], cfv("identf")),
                 reads=["posnk", "posnv", "posn0", "cft"], writes=[("b", 0)])
            P.op("act", lambda e: e.activation(out=pos2T[:], in_=banks[0][:, 0:32], func=AF.Copy), reads=[("b", 0)], writes=["pos2T"])
            if nlvl == 11:
                return
            kvv = KV3[:, 0, :].rearrange("p (n s) -> p n s", s=16)
            KVR = [("KV3", i) for i in range(NT)]
            P.op("dve", lambda e: e.memset(kcp[:, :, :], 0.0), writes=["kcp0"])
            for ll in range(32):
                src = kvv[:, 0:127, ll] if ll < 16 else kvv[:, 1:128, ll - 16]
                P.op("dve", lambda e, ll=ll, src=src: e.tensor_scalar(
                    out=kcp[:, ll, 0:127], in0=src, scalar1=pos2T[:, ll:ll + 1], scalar2=None, op0=ALU.add),
                    reads=KVR + ["pos2T", "kcp0"], writes=[("kcp", ll)])
            P.op("dve", lambda e: e.memset(hg[:, :, :], 0.0), writes=["hg0"])
            if nlvl == 12:
                return
            nvar = os.environ.get("NSA_VAR", "")
            for kv in range(1 if nvar == "B" else 2):
                for jc in range(2):
                    reg = kv * 2 + jc
                    for ll in range(32):
                        P.op("pe", lambda e, kv=kv, jc=jc, ll=ll, reg=reg: e.matmul(
                            banks[2 + kv][:, jc * 128:jc * 128 + 128], lhsT=W1[kv * 64:(kv + 1) * 64, ll, jc * 128:(jc + 1) * 128],
                            rhs=kcp[kv * 64:(kv + 1) * 64, ll, :], start=(ll == 0), stop=(ll == 31)),
                            reads=["W1k", "W1v", ("kcp", ll), "kcp0"], writes=[("st", kv)])
            for kv in range(2):
                P.op("act", lambda e, kv=kv: e.activation(out=hg[:, 2 * kv:2 * kv + 2, :], in_=banks[2 + kv][:, 0:256].rearrange("p (r n) -> p r n", n=128),
                                                         func=AF.Gelu_apprx_tanh), reads=[("st", kv), "hg0"], writes=[("hg", kv)])
            if nlvl == 13:
                return
            for jc in range(2):
                P.op("pe", lambda e, jc=jc: e.matmul(banks[4][:, 0:64], lhsT=hg[:, jc, :], rhs=W2[:, 0, jc, :],
                                                     start=(jc == 0), stop=(jc == 1)), reads=[("hg", 0), "W2k"], writes=[("st", 2)])
            for jc in range(2):
                P.op("pe", lambda e, jc=jc: e.matmul(banks[4][:, 64:128], lhsT=hg[:, 2 + jc, :], rhs=W2[:, 1, jc, :],
                                                     start=(jc == 0), stop=(jc == 1)), reads=[("hg", 1), "W2v"], writes=[("st", 2)])
            P.op("dve", lambda e: e.memset(vca[:, 0:64], 0.0), writes=["vca0"])
            P.op("dve", lambda e: e.memset(vca[:, 64:128], 1.0), writes=["vca1"])
            P.op("act", lambda e: e.activation(out=vca[:, 0:64], in_=banks[4][:, 64:128], func=AF.Copy),
                 reads=[("st", 2), "vca0"], writes=["vca"])
            if nlvl == 15:
                return
            P.op("dve", lambda e: e.memset(kcn[:], 0.0), writes=["kcn0"])
            P.op("act", lambda e: e.activation(out=scrA[:, 0:64], in_=banks[4][:, 0:64], func=AF.Square, accum_out=sm[:, 0:1]),
                 reads=[("st", 2)], writes=["scrA", "sm"])
            rstd_small(sm[:, 0:1], sm[:, 8:9], 1, "sm", "sm2")
            P.op("dve", lambda e: e.scalar_tensor_tensor(out=kcn[:, :], in0=banks[4][:, 0:64], scalar=sm[:, 8:9], in1=gkc[:, :],
                                                         op0=ALU.mult, op1=ALU.mult), reads=[("st", 2), "sm2", "gkc", "kcn0"], writes=["kcn"])
            if nlvl == 16:
                return
            rope_ops(kcn[:, :].rearrange("p (a b) -> p a b", b=64), 1, 16, "kcn")
            P.op("dve", lambda e: e.memset(kcnb[:, 64:128], 0.0), writes=["kcnb0"])
            P.op("act", lambda e: e.activation(out=kcnb[:, 0:64], in_=kcn[:], func=AF.Copy), reads=["kcn"], writes=["kcnb"])
            P.op("pe", lambda e: e.transpose(pTb[:, 0:128], kcnb[:, :], identb[:]), reads=["kcnb", "kcnb0", "identb"], writes=["pT"])
            P.op("act", lambda e: e.activation(out=kcmpT[0:64, :], in_=pTb[0:64, 0:128], func=AF.Copy), reads=["pT"], writes=["kcmpT"])

            def gate_w(accb, acct, br, j):
                P.op("act", lambda e: e.activation(out=scrA[0:64, 0:512], in_=accb[64:128, :], func=AF.Copy),
                     reads=[acct], writes=["scrA"])
                P.op("dve", lambda e: e.tensor_scalar(out=scrA[0:64, 0:512], in0=scrA[0:64, 0:512], scalar1=1e-30, scalar2=None, op0=ALU.max),
                     reads=["scrA"], writes=["scrA"])
                P.op("dve", lambda e: e.reciprocal(out=scrA[0:64, 0:512], in_=scrA[0:64, 0:512]), reads=["scrA"], writes=["scrA"])
                for hh in range(4):
                    P.op("pe", lambda e, hh=hh: e.matmul(banks[0][0:64, hh * 128:(hh + 1) * 128],
                                                         lhsT=cfv("sel", 768)[0:32, (3 * hh + br) * 64:(3 * hh + br + 1) * 64],
                                                         rhs=gTt[0:32, j * 128:(j + 1) * 128], start=True, stop=True),
                         reads=["cft", ("gT", j // 4)], writes=[("b", 0)])
                P.op("act", lambda e: e.activation(out=gs[0:64, :], in_=banks[0][0:64, :], func=AF.Copy), reads=[("b", 0)], writes=["gs"])
                P.op("dve", lambda e: e.tensor_tensor(out=scrB[0:64, 0:512], in0=scrA[0:64, 0:512], in1=gs[0:64, :], op=ALU.mult),
                     reads=["scrA", "gs"], writes=["gw"])

            sc = 0
            if nlvl == 1:
                return
            for j in range(NT if nlvl >= 3 else 8):
                buf = j % 2
                qrhs = qT4[0:64, :, j * 128:(j + 1) * 128]
                r = sc % 3
                sc += 1
                sb_ = banks[2 + r]
                P.op("pe", lambda e, sb_=sb_, qrhs=qrhs: e.matmul(sb_[:, :], lhsT=kcmpT[0:64, :], rhs=qrhs, start=True, stop=False),
                     reads=["kcmpT"] + [("qT4", j, h_) for h_ in range(4)], writes=[("st", r)])
                P.op("pe", lambda e, sb_=sb_, j=j: e.matmul(sb_[:, :], lhsT=identb[:],
                                                          rhs=mcmp[:, j * 128:(j + 1) * 128].unsqueeze(1).to_broadcast([128, 4, 128]),
                                                          start=False, stop=True), reads=["identb", "mcmp"], writes=[("st", r)])
                P.op("act", lambda e, sb_=sb_, r=r: e.activation(out=PT[r][:], in_=sb_[:, :], func=AF.Exp, scale=0.125),
                     reads=[("st", r)], writes=[("PT", r)])
                accb = banks[5]
                P.op("pe", lambda e, r=r, accb=accb: e.matmul(accb[:, :], lhsT=vca[:, :], rhs=PT[r][:], start=True, stop=True),
                     reads=["vca", "vca1", ("PT", r)], writes=[("acc", 0)])
                if j >= 8:
                    for hh in range(4):
                        P.op("pe", lambda e, r=r, hh=hh: e.matmul(banks[1][:, hh * 33:(hh + 1) * 33], lhsT=PT[r][:, hh * 128:(hh + 1) * 128],
                                                                 rhs=cbv("ov", 64)[:, 0:33], start=True, stop=True),
                             reads=[("PT", r), "cbt"], writes=[("b", 1)])
                gate_w(accb, ("acc", 0), 0, j)
                P.op("dve", lambda e, accb=accb: e.tensor_tensor(out=scrC[0:64, 0:512], in0=accb[0:64, :], in1=scrB[0:64, 0:512], op=ALU.mult),
                     reads=[("acc", 0), "gw"], writes=["scrC"])
                if j >= 8:
                    U = banks[1][:, 0:132].rearrange("p (h c) -> p h c", c=33)
                    P.op("dve", lambda e, U=U: e.tensor_scalar(out=sm[:, 32:36], in0=U[:, :, 32], scalar1=1e-30, scalar2=None, op0=ALU.max),
                         reads=[("b", 1)], writes=["rD"])
                    P.op("dve", lambda e: e.reciprocal(out=sm[:, 32:36], in_=sm[:, 32:36]), reads=["rD"], writes=["rD"])
                    P.op("dve", lambda e, j=j: e.tensor_copy(out=impt[:, 0:32], in_=cfv("addmask", 512)[:, j * 32:(j + 1) * 32]),
                         reads=["cft"], writes=["impt"])
                    for hh in range(4):
                        P.op("dve", lambda e, U=U, hh=hh: e.scalar_tensor_tensor(out=impt[:, 0:32], in0=U[:, hh, 0:32], scalar=sm[:, 32 + hh:33 + hh],
                                                                                in1=impt[:, 0:32], op0=ALU.mult, op1=ALU.add),
                             reads=[("b", 1), "rD", "impt"], writes=["impt"])
                    P.op("dve", lambda e: e.max(out=m8[:, 0:8], in_=impt[:, 0:32]), reads=["impt"], writes=["m8a"])
                    P.op("dve", lambda e: e.match_replace(out=impt[:, 32:64], in_to_replace=m8[:, 0:8], in_values=impt[:, 0:32], imm_value=-3e38),
                         reads=["impt", "m8a"], writes=["impt2"])
                    P.op("dve", lambda e: e.max(out=m8[:, 8:16], in_=impt[:, 32:64]), reads=["impt2"], writes=["m8b"])
                    P.op("dve", lambda e: e.tensor_scalar(out=impt[:, 32:64], in0=impt[:, 0:32], scalar1=m8[:, 15:16], scalar2=None, op0=ALU.is_ge),
                         reads=["impt", "m8b", "impt2"], writes=["selm"])
                    P.op("dve", lambda e: e.tensor_scalar(out=Mtm[:, 0:32], in0=impt[:, 32:64], scalar1=-1.0, scalar2=-NEG, op0=ALU.add, op1=ALU.mult),
                         reads=["selm", "Mtm0"], writes=["Mtm"])
                    P.op("pe", lambda e: e.transpose(pTb[:, 0:128], Mtm[:, :], identb[:]), reads=["Mtm", "Mtm0", "identb"], writes=["pT"])
                    P.op("act", lambda e, buf=buf: e.activation(out=MT[buf][0:32, :], in_=pTb[0:32, 0:128], func=AF.Copy),
                         reads=["pT"], writes=[("MT", buf)])
                for br, kvi, va, vtag, accb, acct in ((2, 2, vwa, "vwa", banks[5], ("acc", 0)), (1, 1, vsa, "vsa", banks[6], ("acc", 1))):
                    k_lo = max(0, j - 4) if br == 2 else 0
                    for ks in range(k_lo, j + 1):
                        r = sc % 3
                        sc += 1
                        sb_ = banks[2 + r]
                        extra = []
                        if ks == j:
                            extra.append("caus")
                        if br == 2 and ks == j - 4:
                            extra.append("winup")
                        if br == 1 and j >= 8:
                            extra.append("sel")
                        P.op("pe", lambda e, sb_=sb_, kvi=kvi, ks=ks, qrhs=qrhs, last=(len(extra) == 0): e.matmul(
                            sb_[:, :], lhsT=KV3[0:64, kvi, ks * 128:(ks + 1) * 128], rhs=qrhs, start=True, stop=last),
                            reads=[("KV3", ks)] + [("qT4", j, h_) for h_ in range(4)], writes=[("st", r)])
                        for xi, kind in enumerate(extra):
                            last = (xi == len(extra) - 1)
                            if kind == "sel":
                                P.op("pe", lambda e, sb_=sb_, ks=ks, buf=buf, last=last: e.matmul(
                                    sb_[:, :], lhsT=Et[0:32, ks * 128:(ks + 1) * 128],
                                    rhs=MT[buf][0:32, :].unsqueeze(1).to_broadcast([32, 4, 128]), start=False, stop=last),
                                    reads=["Et", ("MT", buf)], writes=[("st", r)])
                            else:
                                P.op("pe", lambda e, sb_=sb_, kind=kind, last=last: e.matmul(
                                    sb_[:, :], lhsT=identb[:], rhs=cbv(kind).unsqueeze(1).to_broadcast([128, 4, 128]), start=False, stop=last),
                                    reads=["identb", "cbt"], writes=[("st", r)])
                        P.op("act", lambda e, sb_=sb_, r=r: e.activation(out=PT[r][:], in_=sb_[:, :], func=AF.Exp, scale=0.125),
                             reads=[("st", r)], writes=[("PT", r)])
                        P.op("pe", lambda e, accb=accb, va=va, ks=ks, r=r, first=(ks == k_lo), last2=(ks == j): e.matmul(
                            accb[:, :], lhsT=va[:, ks, :], rhs=PT[r][:], start=first, stop=last2),
                            reads=[(vtag, ks), vtag[0:2] + "ones", ("PT", r)], writes=[acct])
                    gate_w(accb, acct, br, j)
                    if br == 2:
                        P.op("dve", lambda e, accb=accb: e.tensor_tensor(out=scrD[0:64, 0:512], in0=accb[0:64, :], in1=scrB[0:64, 0:512], op=ALU.mult),
                             reads=[acct, "gw"], writes=["scrD"])
                        P.op("dve", lambda e: e.tensor_tensor(out=scrC[0:64, 0:512], in0=scrC[0:64, 0:512], in1=scrD[0:64, 0:512], op=ALU.add),
                             reads=["scrC", "scrD"], writes=["scrC"])
                    else:
                        P.op("dve", lambda e, accb=accb: e.tensor_tensor(out=scrD[0:64, 0:512], in0=accb[0:64, :], in1=scrB[0:64, 0:512], op=ALU.mult),
                             reads=[acct, "gw"], writes=["scrD"])
                        P.op("dve", lambda e, buf=buf: e.tensor_tensor(out=OT[0:64, :, buf, :],
                                                                       in0=scrD[0:64, 0:512].rearrange("p (h t) -> p h t", t=128),
                                                                       in1=scrC[0:64, 0:512].rearrange("p (h t) -> p h t", t=128), op=ALU.add),
                             reads=["scrD", "scrC"], writes=[("OT", buf)])
                wout_tile(j, buf)

        fns = {"fox": fox, "gmlp": gmlp, "pool": pool, "nsa": nsa}
        for sname in ("fox", "gmlp", "pool", "nsa"):
            if sname in seq_stages:
                fns[sname]()
                P.fence(dummy[:])
                yield sname

    P.op("dve", lambda e: e.memset(EPS_T[:, 1:2], 1.0), reads=["epst"], writes=["eps1"])
    P.op("dve", lambda e: e.memset(EPS_T[:, 2:3], 0.5), reads=["epst", "eps1"], writes=["eps1"])
    outs = []
    stage_idx = {"ffn1": 0, "fox": 1, "gmlp": 2, "pool": 3, "nsa": 4, "ffn2": 5}

    def dump(sname):
        if dbg:
            outs.append(P.dma("sp", dbg_d[stage_idx[sname]].rearrange("(i p) d -> p i d", p=128), X[:, :, :],
                              reads=[("X", i) for i in range(NT)], writes=[("dbgout", sname)], sem_key="dbg"))

    for s in range(nseq):
        for h2 in range(2):
            P.dma("sp", X[:, h2 * 8:(h2 + 1) * 8, :], x_d[s, h2 * 1024:(h2 + 1) * 1024, :].rearrange("(i p) d -> p i d", p=128),
                  writes=[("X", i) for i in range(h2 * 8, (h2 + 1) * 8)], sem_key=("xin", h2))
        for l in range(nlayers):
            if "ffn1" in stages:
                ffn(l, 1)
                P.fence(dummy[:])
                dump("ffn1")
            for sname in mixer(l, stages):
                dump(sname)
            if "ffn2" in stages:
                ffn(l, 2)
                P.fence(dummy[:])
                dump("ffn2")
        for h2 in range(2):
            outs.append(P.dma("sp", y_d[s, h2 * 1024:(h2 + 1) * 1024, :].rearrange("(i p) d -> p i d", p=128), X[:, h2 * 8:(h2 + 1) * 8, :],
                              reads=[("X", i) for i in range(h2 * 8, (h2 + 1) * 8)], writes=[("yout", s, h2)], sem_key=("yout", h2)))
    P.emit(final_waits=outs)
    st.close()
    return nc, P


_CACHE = {}


def kernel(**inputs):
    n_cores = 8
    x = np.ascontiguousarray(np.asarray(inputs["x"], dtype=np.float32))
    nseq = x.shape[0] // n_cores
    if "nc" not in _CACHE:
        _CACHE["nc"] = build(nseq, 2)[0]
        _CACHE["consts"] = make_consts()
    nc = _CACHE["nc"]
    cb, cf = _CACHE["consts"]
    params = {k: np.ascontiguousarray(np.asarray(inputs[k], dtype=np.float32)) for k in PARAM_SHAPES}
    in_maps = []
    for c in range(n_cores):
        m = {"x": x[c * nseq:(c + 1) * nseq], "cb": cb, "cf": cf}
        m.update(params)
        in_maps.append(m)
    res = run_bass_kernel_spmd(nc, in_maps, core_ids=list(range(n_cores)))
    return np.concatenate([np.asarray(r["y"]) for r in res.results], axis=0).astype(np.float32)
```

```python
import contextlib
import os
import numpy as np
import ml_dtypes
import concourse.bass as bass
import concourse.mybir as mybir
from concourse.bass_utils import run_bass_kernel_spmd

F32 = mybir.dt.float32
BF16 = mybir.dt.bfloat16
AF = mybir.ActivationFunctionType
ALU = mybir.AluOpType
AX = mybir.AxisListType

T = 2048
D = 1024
DFF = 2816
NIN = 2192
NT = 16
EPS = 1e-6
NEG = -30000.0
ROPE_THETA = 500000.0


class Prog:
    ENG = ("pe", "act", "dve", "pool", "sp")
    EPOCH = 8000

    def __init__(self, nc):
        self.nc = nc
        self.ops = []
        self.last_w = {}
        self.readers = {}
        self.fence_op = None

    def op(self, eng, fn, reads=(), writes=(), dma=False, sem_key=None, extra_deps=()):
        i = len(self.ops)
        deps = set(extra_deps)
        if self.fence_op is not None:
            deps.add(self.fence_op)
        for t in reads:
            w = self.last_w.get(t)
            if w is not None:
                deps.add(w)
        for t in writes:
            w = self.last_w.get(t)
            if w is not None:
                deps.add(w)
            for r in self.readers.get(t, ()):
                deps.add(r)
        for t in reads:
            self.readers.setdefault(t, []).append(i)
        for t in writes:
            self.last_w[t] = i
            self.readers[t] = []
        if dma and sem_key is None:
            sem_key = ("dma",) + tuple(writes)
        self.ops.append(dict(eng=eng, fn=fn, deps=deps, dma=dma, sem_key=sem_key, flag=False))
        return i

    def dma(self, eng, out, in_, reads=(), writes=(), sem_key=None, **kw):
        return self.op(eng, lambda e: e.dma_start(out=out, in_=in_, **kw), reads, writes, dma=True, sem_key=sem_key)

    def fence(self, dummy):
        ops = self.ops
        outstanding = set(self.last_w.values())
        for rs in self.readers.values():
            outstanding.update(rs)
        if self.fence_op is not None:
            outstanding.add(self.fence_op)
        best = {}
        deps = set()
        for d in outstanding:
            o = ops[d]
            if o["dma"]:
                k = ("D", o["sem_key"])
            else:
                k = ("E", o["eng"])
            if k not in best or best[k] < d:
                best[k] = d
        deps = set(best.values())
        self.last_w = {}
        self.readers = {}
        self.fence_op = None
        i = self.op("dve", lambda e: e.memset(dummy, 0.0), extra_deps=deps)
        self.fence_op = i
        return i

    def emit(self, final_waits=()):
        nc = self.nc
        ops = self.ops
        for o in ops:
            if o["eng"] == "pe" and not o["dma"]:
                o["deps"] = {d for d in o["deps"] if not (ops[d]["eng"] == "pe" and not ops[d]["dma"])}
            for d in o["deps"]:
                ops[d]["flag"] = True
        for i in final_waits:
            ops[i]["flag"] = True
        for o in ops:
            if o["dma"]:
                o["flag"] = True
        sem_names = []
        seen = set()
        cnt = {}
        for i, o in enumerate(ops):
            if not o["flag"]:
                continue
            if o["dma"]:
                key = ("D", o["sem_key"])
                cnt[key] = cnt.get(key, 0) + 16
                o["sig"] = (key, cnt[key])
            else:
                ep = cnt.get(("ep", o["eng"]), 0)
                key = ("E", o["eng"], ep)
                cnt[key] = cnt.get(key, 0) + 1
                o["sig"] = (key, cnt[key])
                if cnt[key] >= self.EPOCH:
                    cnt[("ep", o["eng"])] = ep + 1
            if o["sig"][0] not in seen:
                seen.add(o["sig"][0])
                sem_names.append(o["sig"][0])
        self.n_sems = len(sem_names)
        with contextlib.ExitStack() as st:
            sems = {k: st.enter_context(nc.semaphore("s%d" % n)) for n, k in enumerate(sem_names)}
            block = st.enter_context(nc.Block())
            for en in self.ENG:
                mine = [(i, o) for i, o in enumerate(ops) if o["eng"] == en]
                fin = list(final_waits) if en == "sp" else []

                def body(e, mine=mine, fin=fin):
                    waited = {}

                    def wait_for(d):
                        key, val = ops[d]["sig"]
                        if waited.get(key, 0) >= val:
                            return
                        e.wait_ge(sems[key], val)
                        waited[key] = val

                    for i, o in mine:
                        for d in sorted(o["deps"]):
                            wait_for(d)
                        ins = o["fn"](e)
                        if o["flag"]:
                            key, val = o["sig"]
                            ins.then_inc(sems[key], 16 if o["dma"] else 1)
                    for d in fin:
                        wait_for(d)

                dec = {"pe": block.tensor, "act": block.scalar, "dve": block.vector,
                       "pool": block.gpsimd, "sp": block.sync}[en]
                dec(body)
        return self


CB = {}
CF = {}


def _alloc(tab, name, n):
    off = tab.get("_n", 0)
    tab[name] = off
    tab["_n"] = off + n
    return off


for _n, _w in [("ident", 128), ("caus", 128), ("winup", 128), ("ov", 64), ("ones", 128),
               ("ad", 512), ("ap", 512), ("a0h", 512), ("a0l", 512), ("mcmp", 2048), ("E", 2048)]:
    _alloc(CB, _n, _w)
for _n, _w in [("identf", 128), ("tril", 128), ("ltri", 128), ("onesf", 128), ("row64", 128),
               ("addmask", 512), ("rope", 17 * 16), ("sel", 12 * 64)]:
    _alloc(CF, _n, _w)
NCB = CB["_n"]
NCF = CF["_n"]


def make_consts():
    cb = np.zeros((128, NCB), np.float32)
    cf = np.zeros((128, NCF), np.float32)
    p = np.arange(128)[:, None]
    q = np.arange(128)[None, :]
    cb[:, CB["ident"]:CB["ident"] + 128] = (p == q)
    cb[:, CB["caus"]:CB["caus"] + 128] = np.where(p <= q, 0.0, NEG)
    cb[:, CB["winup"]:CB["winup"] + 128] = np.where(p > q, 0.0, NEG)
    ncmp = 127
    cs = np.arange(ncmp) * 16
    ce = cs + 32
    ss = np.arange(32) * 64
    se = ss + 64
    ov = np.clip(np.minimum(ce[:, None], se[None, :]) - np.maximum(cs[:, None], ss[None, :]), 0, None) / 32.0
    cb[0:127, CB["ov"]:CB["ov"] + 32] = ov
    cb[0:127, CB["ov"] + 32] = 1.0
    cb[:, CB["ones"]:CB["ones"] + 128] = 1.0
    sizes = (2, 4, 8, 16)
    for g, wn in enumerate(sizes):
        ad = np.zeros((128, 128)); apv = np.zeros((128, 128)); a0 = np.zeros((128, 128))
        for t in range(128):
            for s in range(t - wn + 1, t + 1):
                if s >= 0:
                    ad[s, t] += 1.0 / wn
                else:
                    apv[128 + s, t] += 1.0 / wn
            ad[t, t] -= 1.0
            cntv = min(t + 1, wn)
            for s in range(max(0, t - wn + 1), t + 1):
                a0[s, t] += 1.0 / cntv
            a0[t, t] -= 1.0
        a0h = a0.astype(np.float32).astype(ml_dtypes.bfloat16).astype(np.float32)
        a0l = (a0 - a0h)
        cb[:, CB["ad"] + g * 128:CB["ad"] + (g + 1) * 128] = ad
        cb[:, CB["ap"] + g * 128:CB["ap"] + (g + 1) * 128] = apv
        cb[:, CB["a0h"] + g * 128:CB["a0h"] + (g + 1) * 128] = a0h
        cb[:, CB["a0l"] + g * 128:CB["a0l"] + (g + 1) * 128] = a0l
    tt = np.arange(T)[None, :]
    nn = np.arange(128)[:, None]
    mc = np.where((16 * nn + 31 <= tt) & (nn < 127), 0.0, NEG)
    cb[:, CB["mcmp"]:CB["mcmp"] + T] = mc
    jj = np.arange(32)[:, None]
    cb[0:32, CB["E"]:CB["E"] + T] = ((tt // 64) == jj)

    cf[:, CF["identf"]:CF["identf"] + 128] = (p == q)
    cf[:, CF["tril"]:CF["tril"] + 128] = (q <= p)
    cf[:, CF["ltri"]:CF["ltri"] + 128] = (p <= q)
    cf[:, CF["onesf"]:CF["onesf"] + 128] = 1.0
    cf[64, CF["row64"]:CF["row64"] + 128] = 1.0
    tpos = (np.arange(NT)[None, :] * 128 + np.arange(128)[:, None])
    cur = tpos // 64
    blk = np.arange(32)[None, None, :]
    forced = ((blk == 0) | (blk == cur[:, :, None]) | (blk == cur[:, :, None] - 1)).astype(np.float32)
    am = np.where(blk <= cur[:, :, None], 1000.0 * forced, -1e30).astype(np.float32)
    cf[:, CF["addmask"]:CF["addmask"] + 512] = am.reshape(128, 512)
    inv = (np.float32(ROPE_THETA) ** (-np.arange(8, dtype=np.float32) * np.float32(2.0) / np.float32(16))).astype(np.float32)
    rp = np.zeros((128, 17, 16), np.float32)
    for sl in range(17):
        pos = (tpos[:, sl] if sl < 16 else (np.arange(128) * 16 + 31)).astype(np.float32)
        ang = (pos[:, None] * inv[None, :]).astype(np.float32)
        rp[:, sl, 0:8] = np.cos(ang.astype(np.float64))
        rp[:, sl, 8:16] = np.sin(ang.astype(np.float64))
    cf[:, CF["rope"]:CF["rope"] + 17 * 16] = rp.reshape(128, -1)
    sel = np.zeros((128, 12, 64), np.float32)
    for k in range(12):
        sel[k, k, :] = 1.0
    cf[:, CF["sel"]:CF["sel"] + 768] = sel.reshape(128, -1)
    return cb.astype(ml_dtypes.bfloat16), cf.astype(np.float32)


PARAM_SHAPES = {
    'ffn1_norm': (2, 1024), 'ffn1_w1': (2, 1024, 2816), 'ffn1_w3': (2, 1024, 2816), 'ffn1_w2': (2, 2816, 1024),
    'mix_norm': (2, 1024), 'w_in': (2, 1024, 2192), 'w_out': (2, 1024, 1024),
    'fox_f_bias': (2, 4), 'fox_q_norm': (2, 64), 'fox_k_norm': (2, 64),
    'gmlp_v_norm': (2, 256), 'gmlp_w_s': (2, 4, 128, 128), 'gmlp_b_s': (2, 4, 128),
    'nsa_q_norm': (2, 64), 'nsa_kc_norm': (2, 64), 'nsa_ks_norm': (2, 64), 'nsa_kw_norm': (2, 64),
    'nsa_cmp_pos_k': (2, 32, 64), 'nsa_cmp_k_w1': (2, 2048, 256), 'nsa_cmp_k_w2': (2, 256, 64),
    'nsa_cmp_pos_v': (2, 32, 64), 'nsa_cmp_v_w1': (2, 2048, 256), 'nsa_cmp_v_w2': (2, 256, 64),
    'nsa_gate_bias': (2, 12), 'pool_w': (2, 4, 64, 64), 'pool_scale': (2, 256),
    'ffn2_norm': (2, 1024), 'ffn2_w1': (2, 1024, 2816), 'ffn2_w3': (2, 1024, 2816), 'ffn2_w2': (2, 2816, 1024),
}

ARENA_BYTES = 132 * 1024


def build(nseq, nlayers, dbg=False, stages=("ffn1", "fox", "gmlp", "pool", "nsa", "ffn2")):
    nc = bass.Bass("TRN2", target_bir_lowering=False)
    x_d = nc.dram_tensor("x", [nseq, T, D], F32, kind="ExternalInput").ap()
    y_d = nc.dram_tensor("y", [nseq, T, D], F32, kind="ExternalOutput").ap()
    dbg_d = nc.dram_tensor("dbg", [6, T, D], F32, kind="ExternalOutput").ap() if dbg else None
    W = {k: nc.dram_tensor(k, list(s), F32, kind="ExternalInput").ap() for k, s in PARAM_SHAPES.items()}
    cb_d = nc.dram_tensor("cb", [128, NCB], BF16, kind="ExternalInput").ap()
    cf_d = nc.dram_tensor("cf", [128, NCF], F32, kind="ExternalInput").ap()

    P = Prog(nc)
    st = contextlib.ExitStack()

    def sb(name, shape, dt):
        return st.enter_context(nc.sbuf_tensor(name, shape, dt))

    X = sb("X", [128, NT, D], F32)
    gb = sb("gb", [128, D], F32)
    arena = sb("arena", [128, ARENA_BYTES // 4], F32)
    identb = sb("identb", [128, 128], BF16)
    ssq = sb("ssq", [128, 16], F32)
    rstd = sb("rstd", [128, 16], F32)
    hb = [sb("hb%d" % i, [128, D], BF16) for i in range(2)]
    dummy = sb("fdummy", [128, 8], F32)
    banks = [st.enter_context(nc.psum_tensor("bank%d" % i, [128, 512], F32)) for i in range(8)]
    pTb = banks[7][:, :].bitcast(BF16)
    junk = banks[6][:, :].bitcast(BF16)

    class Arena:
        def __init__(self):
            self.off = 0

        def reset(self, off=0):
            self.off = off

        def view(self, shape, dt, parts=128):
            esz = 4 if dt == F32 else 2
            n = int(np.prod(shape[1:]))
            nbytes = (n * esz + 3) // 4 * 4
            a = arena[:, self.off // 4:(self.off + nbytes) // 4]
            if dt != F32:
                a = a.bitcast(dt)
            a = a[:, 0:n]
            self.off += nbytes
            assert self.off <= ARENA_BYTES, ("arena overflow", self.off)
            if len(shape) == 3:
                a = a.rearrange("p (a b) -> p a b", b=shape[2])
            elif len(shape) == 4:
                a = a.rearrange("p (a b c) -> p a b c", b=shape[2], c=shape[3])
            return a

    AR = Arena()

    P.dma("sp", identb[:], cb_d[:, CB["ident"]:CB["ident"] + 128], writes=["identb"])

    def norm_T(gain_ap, hT):
        P.dma("sp", gb[:], gain_ap.partition_broadcast(128), writes=["gb"])
        for i in range(NT):
            P.op("act", lambda e, i=i: e.activation(out=hb[i % 2][:], in_=X[:, i, :], func=AF.Square,
                                                   accum_out=ssq[:, i:i + 1]),
                 reads=[("X", i)], writes=[("hb", i % 2), ("ssq", i)])
        P.op("act", lambda e: e.activation(out=rstd[:], in_=ssq[:], func=AF.Sqrt, scale=1.0 / D, bias=EPS_T[:, 0:1]),
             reads=[("ssq", i) for i in range(NT)] + ["epst"], writes=["rstd_s"])
        P.op("dve", lambda e: e.reciprocal(out=rstd[:], in_=rstd[:]), reads=["rstd_s"], writes=["rstd"])
        for i in range(NT):
            b = i % 2
            P.op("dve", lambda e, i=i, b=b: e.scalar_tensor_tensor(out=hb[b][:], in0=X[:, i, :], scalar=rstd[:, i:i + 1],
                                                                in1=gb[:], op0=ALU.mult, op1=ALU.mult),
                 reads=[("X", i), "rstd", "gb"], writes=[("hb", b)])
            for c in range(8):
                P.op("pe", lambda e, b=b, c=c: e.transpose(pTb[:, c * 128:(c + 1) * 128], hb[b][:, c * 128:(c + 1) * 128], identb[:]),
                     reads=[("hb", b), "identb"], writes=["pT"])
            P.op("act", lambda e, i=i: e.activation(out=hT[:, :, i * 128:(i + 1) * 128],
                                                   in_=pTb[:, :].rearrange("p (c t) -> p c t", t=128), func=AF.Copy),
                 reads=["pT"], writes=[("hT", i)])

    EPS_T = sb("epst", [128, 4], F32)
    P.op("dve", lambda e: e.memset(EPS_T[:], EPS), writes=["epst"])

    def ffn(l, which):
        pre = "ffn%d_" % which
        w1_d = W[pre + "w1"][l].rearrange("(k p) f -> p k f", p=128)
        w3_d = W[pre + "w3"][l].rearrange("(k p) f -> p k f", p=128)
        w2_d = W[pre + "w2"][l]
        AR.reset()
        hT = AR.view([128, 8, T], BF16)
        gT = AR.view([128, 6, T], BF16)
        w2b = [AR.view([128, 6, D], BF16) for _ in range(2)]
        w13 = [AR.view([128, 2, 8, 256], BF16) for _ in range(3)]
        sil = [AR.view([128, 512], F32) for _ in range(2)]
        norm_T(W[pre + "norm"][l], hT)
        import os
        lvl = int(os.environ.get("FFN_LEVEL", "2"))
        groups = [(0, 6), (6, 6), (12, 5), (17, 5)]
        if lvl == 0:
            groups = []
        cnt = 0
        oc = 0
        uc = 0
        for g, (c0, n) in enumerate(groups):
            wb = w2b[g % 2]
            P.dma("pool", wb[:, 0:n, :], w2_d[c0 * 128:(c0 + n) * 128, :].rearrange("(c p) f -> p c f", p=128),
                  writes=[("w2b", g % 2)])
            units = [(c0 + u, min(2, n - u)) for u in range(0, n, 2)]
            for (j0, nj) in units:
                slot = uc % 3
                uc += 1
                ws = w13[slot]
                P.dma("pool", ws[:, 0, :, 0:nj * 128], w1_d[:, :, j0 * 128:(j0 + nj) * 128], writes=[("w13a", slot)])
                P.dma("pool", ws[:, 1, :, 0:nj * 128], w3_d[:, :, j0 * 128:(j0 + nj) * 128], writes=[("w13b", slot)])
                for tb in range(4):
                    hreads = [("hT", 4 * tb + qq) for qq in range(4)]
                    for jj in range(nj):
                        jl = j0 + jj - c0
                        r = cnt % 2
                        cnt += 1
                        pa = banks[r]
                        pb = banks[2 + r]
                        for k in range(8):
                            P.op("pe", lambda e, pa=pa, ws=ws, k=k, jj=jj, tb=tb: e.matmul(
                                pa[:, :], lhsT=ws[:, 0, k, jj * 128:(jj + 1) * 128], rhs=hT[:, k, tb * 512:(tb + 1) * 512],
                                start=(k == 0), stop=(k == 7)), reads=[("w13a", slot)] + hreads, writes=[("pa", r)])
                        for k in range(8):
                            P.op("pe", lambda e, pb=pb, ws=ws, k=k, jj=jj, tb=tb: e.matmul(
                                pb[:, :], lhsT=ws[:, 1, k, jj * 128:(jj + 1) * 128], rhs=hT[:, k, tb * 512:(tb + 1) * 512],
                                start=(k == 0), stop=(k == 7)), reads=[("w13b", slot)] + hreads, writes=[("pb", r)])
                        P.op("act", lambda e, pa=pa, r=r: e.activation(out=sil[r][:], in_=pa[:, :], func=AF.Silu),
                             reads=[("pa", r)], writes=[("sil", r)])
                        P.op("dve", lambda e, pb=pb, r=r, jl=jl, tb=tb: e.tensor_tensor(
                            out=gT[:, jl, tb * 512:(tb + 1) * 512], in0=pb[:, :], in1=sil[r][:], op=ALU.mult),
                            reads=[("pb", r), ("sil", r)], writes=[("gT", jl, tb)])
            for i in range(NT if lvl >= 2 else 0):
                for half in range(2):
                    r = oc % 2
                    oc += 1
                    po = banks[4 + r]
                    for jl in range(n):
                        P.op("pe", lambda e, po=po, jl=jl, i=i, half=half, wb=wb: e.matmul(
                            po[:, :], lhsT=gT[:, jl, i * 128:(i + 1) * 128], rhs=wb[:, jl, half * 512:(half + 1) * 512],
                            start=(jl == 0), stop=(jl == n - 1)),
                            reads=[("gT", jl, i // 4), ("w2b", g % 2)], writes=[("po", r)])
                    if os.environ.get("STT_ALT", "0") == "1":
                        P.op("dve", lambda e, po=po, i=i, half=half: e.tensor_tensor(
                            out=X[:, i, half * 512:(half + 1) * 512], in0=po[:, :],
                            in1=X[:, i, half * 512:(half + 1) * 512], op=ALU.add),
                            reads=[("po", r), ("X", i)], writes=[("X", i)])
                    else:
                        P.op("dve", lambda e, po=po, i=i, half=half: e.scalar_tensor_tensor(
                            out=X[:, i, half * 512:(half + 1) * 512], in0=po[:, :], scalar=EPS_T[:, 2:3],
                            in1=X[:, i, half * 512:(half + 1) * 512], op0=ALU.mult, op1=ALU.add),
                            reads=[("po", r), ("X", i), "eps1"], writes=[("X", i)])

    def mixer(l, seq_stages):
        AR.reset()
        hT = AR.view([128, 8, T], BF16)
        wbuf = AR.view([128, 8, 772], BF16)
        post_limit = AR.off
        cbt = AR.view([128, CB["mcmp"]], BF16)
        cft = AR.view([128, NCF], F32)
        OT = AR.view([128, 4, 2, 128], BF16)
        wout = AR.view([128, 4, D], BF16)
        scrA = AR.view([128, 640], F32)
        scrB = AR.view([128, 640], F32)
        scrC = AR.view([128, 640], F32)
        sm = AR.view([128, 64], F32)
        PT = [AR.view([128, 512], BF16) for _ in range(3)]
        base_off = AR.off

        def cbv(name, n=128):
            return cbt[:, CB[name]:CB[name] + n]

        def cfv(name, n=128):
            return cft[:, CF[name]:CF[name] + n]

        P.dma("sp", cbt[:], cb_d[:, 0:CB["mcmp"]], writes=["cbt"])
        P.dma("sp", cft[:], cf_d[:, :], writes=["cft"])
        norm_T(W["mix_norm"][l], hT)
        w_in = W["w_in"][l].rearrange("(k p) f -> p k f", p=128)
        w_out = W["w_out"][l]
        HR = [("hT", i) for i in range(NT)]

        def load_win(c0, n):
            P.dma("pool", wbuf[:, :, 0:n], w_in[:, :, c0:c0 + n], writes=["wbuf"])

        def load_wout(m):
            P.dma("pool", wout[0:64, :, :], w_out[m * 256:(m + 1) * 256, :].rearrange("(c p) f -> p c f", p=64),
                  writes=["wout"])

        def proj_tm(i, col0, ncols, bank, btag, c_in_buf=0):
            for k in range(8):
                P.op("pe", lambda e, k=k: e.matmul(bank[:, 0:ncols], lhsT=hT[:, k, i * 128:(i + 1) * 128],
                                                   rhs=wbuf[:, k, c_in_buf:c_in_buf + ncols], start=(k == 0), stop=(k == 7)),
                     reads=[("hT", i), "wbuf"], writes=[btag])

        def wout_tile(i, buf, bsel=(0, 1)):
            for half in range(2):
                bk = banks[bsel[half]]
                for c in range(4):
                    P.op("pe", lambda e, c=c, half=half, bk=bk: e.matmul(
                        bk[:, :], lhsT=OT[0:64, c, buf, :], rhs=wout[0:64, c, half * 512:(half + 1) * 512],
                        start=(c == 0), stop=(c == 3)), reads=[("OT", buf), ("OT", buf, c), "wout"], writes=[("b", bsel[half])])
                P.op("dve", lambda e, half=half, bk=bk: e.tensor_tensor(
                    out=X[:, i, half * 512:(half + 1) * 512], in0=bk[:, :], in1=X[:, i, half * 512:(half + 1) * 512],
                    op=ALU.add), reads=[("b", bsel[half]), ("X", i)], writes=[("X", i)])

        def rstd_small(src_ap, dst_ap, n, tagr, tagw):
            npart = dst_ap.shape[0]
            P.op("act", lambda e: e.activation(out=dst_ap, in_=src_ap, func=AF.Sqrt, scale=1.0 / 64, bias=EPS_T[0:npart, 0:1]),
                 reads=[tagr, "epst"], writes=[tagw + "_s"])
            P.op("dve", lambda e: e.reciprocal(out=dst_ap, in_=dst_ap), reads=[tagw + "_s"], writes=[tagw])

        def fox():
            AR.reset(base_off)
            qT2 = AR.view([128, 2, T], BF16)
            kT2 = AR.view([128, 2, T], BF16)
            Vaug = AR.view([128, NT, 4, 128], BF16)
            Bt = AR.view([128, 4, NT, NT], F32)
            zt = AR.view([128, NT, 4], F32)
            ctm = AR.view([128, NT, 4], F32)
            cmb = AR.view([128, NT, 4], F32)
            off = AR.view([128, NT, 4], F32)
            totb = AR.view([128, NT, 4], F32)
            Gqk = AR.view([128, 8, 64], F32)
            fbb = AR.view([128, 4], F32)
            qkb = [AR.view([128, 512], BF16) for _ in range(2)]
            load_win(0, 772)
            load_wout(0)
            for hh in range(4):
                P.dma("sp", Gqk[:, hh, :], W["fox_q_norm"][l].partition_broadcast(128), writes=[("Gqk", hh)])
                P.dma("sp", Gqk[:, 4 + hh, :], W["fox_k_norm"][l].partition_broadcast(128), writes=[("Gqk", 4 + hh)])
            GQ = [("Gqk", hh) for hh in range(8)]
            P.dma("sp", fbb[:], W["fox_f_bias"][l].partition_broadcast(128), writes=["fbb"])
            P.op("dve", lambda e: e.memset(Vaug[:, :, :, 64:128], 1.0), writes=["vones"])
            for i in range(NT):
                proj_tm(i, 0, 512, banks[0], ("b", 0), 0)
                proj_tm(i, 512, 260, banks[1], ("b", 1), 512)
                P.op("act", lambda e: e.activation(out=scrA[:, 0:512], in_=banks[0][:, :], func=AF.Square),
                     reads=[("b", 0)], writes=["scrA"])
                P.op("dve", lambda e: e.tensor_reduce(out=sm[:, 0:8], in_=scrA[:, 0:512].rearrange("p (a b) -> p a b", b=64),
                                                      axis=AX.X, op=ALU.add), reads=["scrA"], writes=["sm"])
                rstd_small(sm[:, 0:8], sm[:, 8:16], 8, "sm", "sm2")
                P.op("dve", lambda e: e.tensor_tensor(out=scrB[:, 0:512].rearrange("p (a b) -> p a b", b=64),
                                                      in0=banks[0][:, :].rearrange("p (a b) -> p a b", b=64),
                                                      in1=sm[:, 8:16].unsqueeze(2).to_broadcast([128, 8, 64]), op=ALU.mult),
                     reads=[("b", 0), "sm2"], writes=["scrB"])
                qb_ = qkb[i % 2]
                P.op("dve", lambda e, qb_=qb_: e.tensor_tensor(out=qb_[:], in0=scrB[:, 0:512],
                                                                in1=Gqk[:, :, :].rearrange("p a b -> p (a b)"), op=ALU.mult),
                     reads=["scrB"] + GQ, writes=[("qkb", i % 2)])
                for c in range(4):
                    P.op("pe", lambda e, c=c, qb_=qb_: e.transpose(pTb[:, c * 128:(c + 1) * 128], qb_[:, c * 128:(c + 1) * 128], identb[:]),
                         reads=[("qkb", i % 2), "identb"], writes=["pT"])
                P.op("act", lambda e, i=i: e.activation(out=qT2[:, :, i * 128:(i + 1) * 128],
                                                       in_=pTb[:, 0:256].rearrange("p (c t) -> p c t", t=128), func=AF.Copy),
                     reads=["pT"], writes=[("qT2", i)])
                P.op("act", lambda e, i=i: e.activation(out=kT2[:, :, i * 128:(i + 1) * 128],
                                                       in_=pTb[:, 256:512].rearrange("p (c t) -> p c t", t=128), func=AF.Copy),
                     reads=["pT"], writes=[("kT2", i)])
                P.op("act", lambda e, i=i: e.activation(out=Vaug[:, i, :, 0:64],
                                                       in_=banks[1][:, 0:256].rearrange("p (a b) -> p a b", b=64), func=AF.Copy),
                     reads=[("b", 1)], writes=[("V", i)])
                P.op("dve", lambda e, i=i: e.tensor_tensor(out=zt[:, i, :], in0=banks[1][:, 256:260], in1=fbb[:], op=ALU.add),
                     reads=[("b", 1), "fbb"], writes=[("zt", i)])
            ZT = [("zt", i) for i in range(NT)]
            ztf = zt[:, :, :].rearrange("p a b -> p (a b)")
            P.op("act", lambda e: e.activation(out=ztf, in_=ztf, func=AF.Exp, scale=-1.0), reads=ZT, writes=["z1"])
            P.op("act", lambda e: e.activation(out=ztf, in_=ztf, func=AF.Ln, bias=EPS_T[:, 1:2]), reads=["z1", "eps1"], writes=["z2"])
            P.op("dve", lambda e: e.tensor_scalar(out=ztf, in0=ztf, scalar1=-1.0, scalar2=None, op0=ALU.mult), reads=["z2"], writes=["logf"])
            P.op("pe", lambda e: e.matmul(banks[0][:, 0:64], lhsT=cfv("ltri"), rhs=ztf, start=True, stop=True),
                 reads=["logf", "cft"], writes=[("b", 0)])
            P.op("pe", lambda e: e.matmul(banks[0][:, 64:128], lhsT=cfv("onesf"), rhs=ztf, start=True, stop=True),
                 reads=["logf", "cft"], writes=[("b", 0)])
            totf = totb[:, :, :].rearrange("p a b -> p (a b)")
            P.op("act", lambda e: e.activation(out=totf, in_=banks[0][:, 64:128], func=AF.Copy), reads=[("b", 0)], writes=["totb"])
            P.op("dve", lambda e: e.memset(off[:, 0, :], 0.0), writes=["off"])
            for j in range(1, NT):
                P.op("dve", lambda e, j=j: e.tensor_tensor(out=off[:, j, :], in0=off[:, j - 1, :], in1=totb[:, j - 1, :], op=ALU.add),
                     reads=["off", "totb"], writes=["off"])
            ctf = ctm[:, :, :].rearrange("p a b -> p (a b)")
            P.op("dve", lambda e: e.tensor_tensor(out=ctf, in0=banks[0][:, 0:64], in1=off[:, :, :].rearrange("p a b -> p (a b)"), op=ALU.add),
                 reads=[("b", 0), "off"], writes=["ctm"])
            P.op("pe", lambda e: e.matmul(banks[1][:, 0:64], lhsT=cfv("row64"), rhs=ctf, start=True, stop=True),
                 reads=["ctm", "cft"], writes=[("b", 1)])
            P.op("act", lambda e: e.activation(out=cmb[:, :, :].rearrange("p a b -> p (a b)"), in_=banks[1][:, 0:64], func=AF.Copy),
                 reads=[("b", 1)], writes=["cmb"])
            for hh in range(4):
                for ks in range(NT):
                    P.op("dve", lambda e, hh=hh, ks=ks: e.tensor_scalar(
                        out=Bt[:, hh, ks, :], in0=cmb[:, :, hh], scalar1=ctm[:, ks, hh:hh + 1], scalar2=None, op0=ALU.subtract),
                        reads=["cmb", "ctm"], writes=[("Bt", hh)])
            DLA = 3
            rdn = AR.view([128, 4, 128], F32)
            blocks = []
            for j in range(NT):
                for hh in range(4):
                    for ks in range(j + 1):
                        blocks.append((j, hh, ks, len(blocks) % 4))
            PTs = [PT[0][:, 0:128], PT[1][:, 0:128], PT[2][:, 0:128], PT[0][:, 256:384]]
            st_banks = [2, 3, 4, 1]

            def st_ap(s_):
                return banks[st_banks[s_]][:, 0:128]

            def acc_ap(g_):
                sl = g_ % 2
                return banks[5 + sl][:, 0:128], ("acc", sl)

            def emit_S(idx):
                j, hh, ks, s_ = blocks[idx]
                pr = (hh % 2) * 64
                pc = hh // 2
                sb_ = st_ap(s_)
                P.op("pe", lambda e: e.matmul(sb_, lhsT=kT2[pr:pr + 64, pc, ks * 128:(ks + 1) * 128],
                                              rhs=qT2[pr:pr + 64, pc, j * 128:(j + 1) * 128], start=True, stop=(ks != j)),
                     reads=[("kT2", ks), ("qT2", j)], writes=[("st", s_)])
                if ks == j:
                    P.op("pe", lambda e: e.matmul(sb_, lhsT=identb[:], rhs=cbv("caus"), start=False, stop=True),
                         reads=["identb", "cbt"], writes=[("st", s_)])
                P.op("act", lambda e: e.activation(out=PTs[s_], in_=sb_, func=AF.Exp, scale=0.125, bias=Bt[:, hh, ks, j:j + 1]),
                     reads=[("st", s_), ("Bt", hh)], writes=[("PT", s_)])

            def emit_PV(idx):
                j, hh, ks, s_ = blocks[idx]
                g_ = j * 4 + hh
                accb, acct = acc_ap(g_)
                buf = j % 2
                P.op("pe", lambda e: e.matmul(accb, lhsT=Vaug[:, ks, hh, :], rhs=PTs[s_], start=(ks == 0), stop=(ks == j)),
                     reads=[("V", ks), "vones", ("PT", s_)], writes=[acct])
                if ks == j:
                    rs_ = g_ % 4
                    P.op("act", lambda e: e.activation(out=rdn[0:64, rs_, :], in_=accb[64:128, :], func=AF.Copy),
                         reads=[acct], writes=[("rdn", rs_)])
                    P.op("dve", lambda e: e.reciprocal(out=rdn[0:64, rs_, :], in_=rdn[0:64, rs_, :]),
                         reads=[("rdn", rs_)], writes=[("rdn", rs_)])
                    P.op("dve", lambda e: e.tensor_tensor(out=OT[0:64, hh, buf, :], in0=accb[0:64, :], in1=rdn[0:64, rs_, :], op=ALU.mult),
                         reads=[acct, ("rdn", rs_)], writes=[("OT", buf, hh)])
                    if hh == 3:
                        wout_tile(j, buf, (0, 0))

            for idx in range(len(blocks) + DLA):
                if idx < len(blocks):
                    emit_S(idx)
                if idx >= DLA:
                    emit_PV(idx - DLA)

        def gmlp():
            AR.reset(base_off)
            uT = AR.view([128, 4, T], BF16)
            vgt = AR.view([128, NT, 256], BF16)
            wsn = AR.view([128, 4, 128], F32)
            wsb = AR.view([128, 4, 128], BF16)
            WT = AR.view([128, 4, 128], BF16)
            Gv = AR.view([128, 256], F32)
            bsf = AR.view([128, 512], F32)
            bsh = AR.view([128, 512], BF16)
            bsl = AR.view([128, 512], BF16)
            load_win(772, 512)
            load_wout(1)
            P.dma("sp", Gv[:], W["gmlp_v_norm"][l].partition_broadcast(128), writes=["Gv"])
            P.dma("sp", wsn[:], W["gmlp_w_s"][l].rearrange("g t s -> t g s"), writes=["wsn"])
            P.dma("sp", bsf[0:1, :], W["gmlp_b_s"][l:l + 1].rearrange("o g t -> o (g t)"), writes=["bsf"])
            P.op("dve", lambda e: e.tensor_copy(out=bsh[0:1, :], in_=bsf[0:1, :]), reads=["bsf"], writes=["bsh"])
            P.op("dve", lambda e: e.tensor_tensor(out=bsl[0:1, :], in0=bsf[0:1, :], in1=bsh[0:1, :], op=ALU.subtract),
                 reads=["bsf", "bsh"], writes=["bsl"])
            P.op("dve", lambda e: e.tensor_tensor(out=wsb[:], in0=wsn[:],
                                                  in1=cfv("tril").unsqueeze(1).to_broadcast([128, 4, 128]), op=ALU.mult),
                 reads=["wsn", "cft"], writes=["wsb"])
            for g in range(4):
                P.op("pe", lambda e, g=g: e.transpose(pTb[:, g * 128:(g + 1) * 128], wsb[:, g, :], identb[:]),
                     reads=["wsb", "identb"], writes=["pT"])
            P.op("act", lambda e: e.activation(out=WT[:], in_=pTb[:, 0:512].rearrange("p (g t) -> p g t", t=128), func=AF.Copy),
                 reads=["pT"], writes=["WT"])
            uc = 0
            for g in range(4):
                for tb in range(4):
                    r = uc % 2
                    uc += 1
                    bk = banks[2 + r]
                    for k in range(8):
                        P.op("pe", lambda e, bk=bk, g=g, tb=tb, k=k: e.matmul(
                            bk[0:64, :], lhsT=wbuf[:, k, g * 64:(g + 1) * 64], rhs=hT[:, k, tb * 512:(tb + 1) * 512],
                            start=(k == 0), stop=(k == 7)), reads=["wbuf"] + HR, writes=[("st", r)])
                    P.op("act", lambda e, bk=bk, g=g, tb=tb: e.activation(out=uT[0:64, g, tb * 512:(tb + 1) * 512], in_=bk[0:64, :],
                                                                         func=AF.Gelu_apprx_tanh), reads=[("st", r)], writes=[("uT", tb)])
            for i in range(NT):
                proj_tm(i, 1028, 256, banks[0], ("b", 0), 256)
                P.op("act", lambda e: e.activation(out=scrA[:, 0:256], in_=banks[0][:, 0:256], func=AF.Gelu_apprx_tanh),
                     reads=[("b", 0)], writes=["scrA"])
                P.op("act", lambda e: e.activation(out=scrB[:, 0:256], in_=scrA[:, 0:256], func=AF.Square),
                     reads=["scrA"], writes=["scrB"])
                P.op("dve", lambda e: e.tensor_reduce(out=sm[:, 0:4], in_=scrB[:, 0:256].rearrange("p (a b) -> p a b", b=64),
                                                      axis=AX.X, op=ALU.add), reads=["scrB"], writes=["sm"])
                rstd_small(sm[:, 0:4], sm[:, 8:12], 4, "sm", "sm2")
                P.op("dve", lambda e: e.tensor_tensor(out=scrB[:, 0:256].rearrange("p (a b) -> p a b", b=64),
                                                      in0=scrA[:, 0:256].rearrange("p (a b) -> p a b", b=64),
                                                      in1=sm[:, 8:12].unsqueeze(2).to_broadcast([128, 4, 64]), op=ALU.mult),
                     reads=["scrA", "sm2", "scrB"], writes=["scrB"])
                P.op("dve", lambda e, i=i: e.tensor_tensor(out=vgt[:, i, :], in0=scrB[:, 0:256], in1=Gv[:], op=ALU.mult),
                     reads=["scrB", "Gv"], writes=[("vgt", i)])
                r = i % 2
                bk = banks[4 + r]
                for g in range(4):
                    P.op("pe", lambda e, bk=bk, g=g, i=i: e.matmul(bk[0:64, g * 128:(g + 1) * 128], lhsT=vgt[:, i, g * 64:(g + 1) * 64],
                                                                  rhs=WT[:, g, :], start=True, stop=False),
                         reads=[("vgt", i), "WT"], writes=[("po", r)])
                    P.op("pe", lambda e, bk=bk, g=g: e.matmul(bk[0:64, g * 128:(g + 1) * 128], lhsT=cbv("ones")[0:1, 0:64],
                                                             rhs=bsh[0:1, g * 128:(g + 1) * 128], start=False, stop=False),
                         reads=["cbt", "bsh"], writes=[("po", r)])
                    P.op("pe", lambda e, bk=bk, g=g: e.matmul(bk[0:64, g * 128:(g + 1) * 128], lhsT=cbv("ones")[0:1, 0:64],
                                                             rhs=bsl[0:1, g * 128:(g + 1) * 128], start=False, stop=True),
                         reads=["cbt", "bsl"], writes=[("po", r)])
                P.op("dve", lambda e, bk=bk, i=i, r=r: e.tensor_tensor(
                    out=OT[0:64, :, r, :], in0=bk[0:64, :].rearrange("p (g t) -> p g t", t=128),
                    in1=uT[0:64, :, i * 128:(i + 1) * 128], op=ALU.mult),
                    reads=[("po", r), ("uT", i // 4)], writes=[("OT", r)])
                wout_tile(i, r)

        def pool():
            AR.reset(base_off)
            zt = AR.view([128, NT, 256], BF16)
            wp = AR.view([128, 4, 64], BF16)
            psc = AR.view([128, 4], F32)
            pl = [AR.view([128, 4, 128], BF16) for _ in range(2)]
            load_win(1936, 256)
            load_wout(3)
            P.dma("pool", wp[0:64, :, :], W["pool_w"][l].rearrange("g d e -> d g e"), writes=["wp"])
            P.dma("sp", psc[0:64, :], W["pool_scale"][l].rearrange("(g e) -> e g", e=64), writes=["psc"],
                  allow_slow_non_contiguous=True)
            for i in range(NT):
                proj_tm(i, 1936, 256, banks[0], ("b", 0), 0)
                P.op("act", lambda e, i=i: e.activation(out=zt[:, i, :], in_=banks[0][:, 0:256], func=AF.Copy),
                     reads=[("b", 0)], writes=[("zt", i)])
                r = i % 2
                bk = banks[2 + r]
                for g in range(4):
                    o_ = bk[0:64, g * 128:(g + 1) * 128]
                    lh = zt[:, i, g * 64:(g + 1) * 64]
                    if i == 0:
                        P.op("pe", lambda e, o_=o_, lh=lh, g=g: e.matmul(o_, lhsT=lh, rhs=cbv("a0h", 512)[:, g * 128:(g + 1) * 128], start=True, stop=False),
                             reads=[("zt", i), "cbt"], writes=[("st", r)])
                        P.op("pe", lambda e, o_=o_, lh=lh, g=g: e.matmul(o_, lhsT=lh, rhs=cbv("a0l", 512)[:, g * 128:(g + 1) * 128], start=False, stop=True),
                             reads=[("zt", i), "cbt"], writes=[("st", r)])
                    else:
                        lp = zt[:, i - 1, g * 64:(g + 1) * 64]
                        P.op("pe", lambda e, o_=o_, lh=lh, g=g: e.matmul(o_, lhsT=lh, rhs=cbv("ad", 512)[:, g * 128:(g + 1) * 128], start=True, stop=False),
                             reads=[("zt", i), "cbt"], writes=[("st", r)])
                        P.op("pe", lambda e, o_=o_, lp=lp, g=g: e.matmul(o_, lhsT=lp, rhs=cbv("ap", 512)[:, g * 128:(g + 1) * 128], start=False, stop=True),
                             reads=[("zt", i - 1), "cbt"], writes=[("st", r)])
                P.op("act", lambda e, bk=bk, r=r: e.activation(out=pl[r][0:64, :, :], in_=bk[0:64, :].rearrange("p (g t) -> p g t", t=128), func=AF.Copy),
                     reads=[("st", r)], writes=[("pl", r)])
                bk2 = banks[4 + r]
                for g in range(4):
                    P.op("pe", lambda e, bk2=bk2, g=g, r=r: e.matmul(bk2[0:64, g * 128:(g + 1) * 128], lhsT=wp[0:64, g, :], rhs=pl[r][0:64, g, :],
                                                                   start=True, stop=True), reads=["wp", ("pl", r)], writes=[("po", r)])
                for g in range(4):
                    P.op("dve", lambda e, bk2=bk2, g=g, r=r: e.tensor_scalar(out=OT[0:64, g, r, :], in0=bk2[0:64, g * 128:(g + 1) * 128],
                                                                            scalar1=psc[0:64, g:g + 1], scalar2=None, op0=ALU.mult),
                         reads=[("po", r), "psc"], writes=[("OT", r)])
                wout_tile(i, r)

        def nsa():
            AR.reset(base_off)
            qT4 = AR.view([128, 4, T], BF16)
            KV3 = AR.view([128, 3, T], BF16)
            vsa = AR.view([128, NT, 128], BF16)
            vwa = AR.view([128, NT, 128], BF16)
            gTt = AR.view([128, T], F32)
            gbias = AR.view([128, 4], F32)
            Gall = AR.view([128, 10, 64], F32)
            NBb = [AR.view([128, 640], BF16) for _ in range(2)]
            rt = [AR.view([128, 4, 8], F32) for _ in range(4)]
            P.op("dve", lambda e: e.memset(wbuf[:, :, 652:704], 0.0), writes=["wbuf"])
            load_win(1284, 652)
            load_wout(2)
            P.op("dve", lambda e: e.memset(Gall[:, :, :], 1.0), writes=["GallI"])
            for hh in range(4):
                P.dma("sp", Gall[:, hh, :], W["nsa_q_norm"][l].partition_broadcast(128), reads=["GallI"], writes=[("Gall", hh)])
            P.dma("sp", Gall[:, 6, :], W["nsa_ks_norm"][l].partition_broadcast(128), reads=["GallI"], writes=[("Gall", 6)])
            P.dma("sp", Gall[:, 8, :], W["nsa_kw_norm"][l].partition_broadcast(128), reads=["GallI"], writes=[("Gall", 8)])
            GA = ["GallI"] + [("Gall", hh) for hh in (0, 1, 2, 3, 6, 8)]
            P.op("dve", lambda e: e.memset(gbias[:, :], 0.0), writes=["gbias0"])
            P.dma("sp", gbias[0:12, 0:1], W["nsa_gate_bias"][l].rearrange("(c o) -> c o", o=1), reads=["gbias0"], writes=["gbias"])
            P.op("dve", lambda e: e.memset(vsa[:, :, 64:128], 1.0), writes=["vsones"])
            P.op("dve", lambda e: e.memset(vwa[:, :, 64:128], 1.0), writes=["vwones"])
            rope = cfv("rope", 17 * 16).rearrange("p (s c) -> p s c", c=16)

            def rope_ops(view, nh, slot, wtag):
                x1 = view[:, :, 0:8]
                x2 = view[:, :, 8:16]
                cosb = rope[:, slot, 0:8].unsqueeze(1).to_broadcast([128, nh, 8])
                sinb = rope[:, slot, 8:16].unsqueeze(1).to_broadcast([128, nh, 8])
                ra, rb_, rc, rd = [t_[:, 0:nh, :] for t_ in rt]
                P.op("dve", lambda e: e.tensor_tensor(out=ra, in0=x1, in1=cosb, op=ALU.mult), reads=[wtag, "cft"], writes=["ra"])
                P.op("dve", lambda e: e.tensor_tensor(out=rb_, in0=x2, in1=sinb, op=ALU.mult), reads=[wtag, "cft"], writes=["rb"])
                P.op("dve", lambda e: e.tensor_tensor(out=rc, in0=x2, in1=cosb, op=ALU.mult), reads=[wtag, "cft"], writes=["rc"])
                P.op("dve", lambda e: e.tensor_tensor(out=rd, in0=x1, in1=sinb, op=ALU.mult), reads=[wtag, "cft"], writes=["rd"])
                P.op("dve", lambda e: e.tensor_tensor(out=x1, in0=ra, in1=rb_, op=ALU.subtract), reads=["ra", "rb", wtag], writes=[wtag])
                P.op("dve", lambda e: e.tensor_tensor(out=x2, in0=rc, in1=rd, op=ALU.add), reads=["rc", "rd", wtag], writes=[wtag])

            for i in range(NT):
                proj_tm(i, 1284, 512, banks[0], ("b", 0), 0)
                proj_tm(i, 1796, 140, banks[1], ("b", 1), 512)
                P.op("act", lambda e: e.activation(out=scrA[:, 0:512], in_=banks[0][:, :], func=AF.Copy), reads=[("b", 0)], writes=["scrA"])
                P.op("act", lambda e: e.activation(out=scrA[:, 512:640], in_=banks[1][:, 0:128], func=AF.Copy), reads=[("b", 1), "scrA"], writes=["scrA"])
                P.op("act", lambda e: e.activation(out=scrB[:, 0:640], in_=scrA[:, 0:640], func=AF.Square), reads=["scrA"], writes=["scrB"])
                P.op("dve", lambda e: e.tensor_reduce(out=sm[:, 0:10], in_=scrB[:, 0:640].rearrange("p (a b) -> p a b", b=64),
                                                      axis=AX.X, op=ALU.add), reads=["scrB"], writes=["sm"])
                rstd_small(sm[:, 0:10], sm[:, 16:26], 10, "sm", "sm2")
                P.op("dve", lambda e: e.tensor_tensor(out=scrB[:, 0:640].rearrange("p (a b) -> p a b", b=64),
                                                      in0=scrA[:, 0:640].rearrange("p (a b) -> p a b", b=64),
                                                      in1=sm[:, 16:26].unsqueeze(2).to_broadcast([128, 10, 64]), op=ALU.mult),
                     reads=["scrA", "sm2", "scrB"], writes=["scrB"])
                P.op("dve", lambda e: e.tensor_tensor(out=scrC[:, 0:640], in0=scrB[:, 0:640],
                                                      in1=Gall[:, :, :].rearrange("p a b -> p (a b)"), op=ALU.mult),
                     reads=["scrB"] + GA, writes=["scrC"])
                for (c0_, c1_) in ((256, 384), (448, 512), (576, 640)):
                    P.op("act", lambda e, c0_=c0_, c1_=c1_: e.activation(out=scrC[:, c0_:c1_], in_=scrA[:, c0_:c1_], func=AF.Copy),
                         reads=["scrA", "scrC"], writes=["scrC"])
                v3 = scrC[:, 0:640].rearrange("p (a b) -> p a b", b=64)
                rope_ops(v3[:, 0:4, :], 4, i, "scrC")
                rope_ops(v3[:, 6:7, :], 1, i, "scrC")
                rope_ops(v3[:, 8:9, :], 1, i, "scrC")
                nb = NBb[i % 2]
                P.op("act", lambda e, nb=nb: e.activation(out=nb[:], in_=scrC[:, 0:640], func=AF.Copy), reads=["scrC"], writes=[("NBb", i % 2)])
                for c in range(5):
                    P.op("pe", lambda e, nb=nb, c=c: e.transpose(pTb[:, c * 128:(c + 1) * 128], nb[:, c * 128:(c + 1) * 128], identb[:]),
                         reads=[("NBb", i % 2), "identb"], writes=["pT"])
                for hh in range(4):
                    pr_ = (hh % 2) * 64
                    if hh % 2 == 0:
                        P.op("act", lambda e, i=i, hh=hh, pr_=pr_: e.activation(out=qT4[0:64, hh, i * 128:(i + 1) * 128],
                                                                               in_=pTb[pr_:pr_ + 64, (hh // 2) * 128:(hh // 2 + 1) * 128], func=AF.Copy),
                             reads=["pT"], writes=[("qT4", i, hh)])
                    else:
                        P.op("act", lambda e, i=i, hh=hh, pr_=pr_: e.activation(out=qT4[0:64, hh, i * 128:(i + 1) * 128],
                                                                               in_=pTb[pr_:pr_ + 64, (hh // 2) * 128:(hh // 2 + 1) * 128], func=AF.Copy),
                             reads=["pT"], writes=[("qT4", i, hh)])
                P.op("act", lambda e, i=i: e.activation(out=KV3[:, :, i * 128:(i + 1) * 128],
                                                       in_=pTb[:, 256:640].rearrange("p (c t) -> p c t", t=128), func=AF.Copy),
                     reads=["pT"], writes=[("KV3", i)])
                P.op("dve", lambda e, nb=nb, i=i: e.tensor_copy(out=vsa[:, i, 0:64], in_=nb[:, 448:512]), reads=[("NBb", i % 2)], writes=[("vsa", i)])
                P.op("dve", lambda e, nb=nb, i=i: e.tensor_copy(out=vwa[:, i, 0:64], in_=nb[:, 576:640]), reads=[("NBb", i % 2)], writes=[("vwa", i)])
            for tb in range(4):
                r = tb % 2
                bk = banks[2 + r]
                for k in range(8):
                    P.op("pe", lambda e, bk=bk, tb=tb, k=k: e.matmul(bk[0:64, :], lhsT=wbuf[:, k, 640:704], rhs=hT[:, k, tb * 512:(tb + 1) * 512],
                                                                   start=(k == 0), stop=(k == 7)), reads=["wbuf"] + HR, writes=[("st", r)])
                P.op("act", lambda e, bk=bk, tb=tb: e.activation(out=gTt[0:64, tb * 512:(tb + 1) * 512], in_=bk[0:64, :], func=AF.Sigmoid,
                                                                bias=gbias[0:64, 0:1]), reads=[("st", r), "gbias", "gbias0"], writes=[("gT", tb)])
            P.fence(dummy[:])
            import os
            nlvl = int(os.environ.get("NSA_LEVEL", "9"))
            if nlvl == 0:
                return
            AR.reset(0)
            kcp = AR.view([128, 32, 128], BF16)
            W1 = AR.view([128, 32, 256], BF16)
            mcmp = AR.view([128, T], BF16)
            Et = AR.view([128, T], BF16)
            W2 = AR.view([128, 2, 2, 64], BF16)
            posn = AR.view([128, 128], F32)
            pos2T = AR.view([128, 32], F32)
            hg = AR.view([128, 4, 128], BF16)
            kcn = AR.view([128, 64], F32)
            kcnb = AR.view([128, 128], BF16)
            kcmpT = AR.view([128, 128], BF16)
            vca = AR.view([128, 128], BF16)
            gkc = AR.view([128, 64], F32)
            MT = [AR.view([128, 128], BF16) for _ in range(2)]
            impt = AR.view([128, 64], F32)
            m8 = AR.view([128, 16], F32)
            Mtm = AR.view([128, 128], BF16)
            scrD = AR.view([128, 512], F32)
            gs = AR.view([128, 512], F32)
            assert AR.off <= post_limit, ("nsa post overflow", AR.off, post_limit)
            P.op("dve", lambda e: e.memset(Mtm[:, :], 0.0), writes=["Mtm0"])
            P.op("dve", lambda e: e.memset(posn[:, :], 0.0), writes=["posn0"])
            P.dma("sp", mcmp[:], cb_d[:, CB["mcmp"]:CB["mcmp"] + T], writes=["mcmp"])
            P.dma("sp", Et[0:32, :], cb_d[0:32, CB["E"]:CB["E"] + T], writes=["Et"])
            P.dma("pool", W1[0:64, :, :], W["nsa_cmp_k_w1"][l].rearrange("(l d) j -> d l j", d=64), writes=["W1k"])
            P.dma("pool", W1[64:128, :, :], W["nsa_cmp_v_w1"][l].rearrange("(l d) j -> d l j", d=64), writes=["W1v"])
            P.dma("pool", W2[:, 0, :, :], W["nsa_cmp_k_w2"][l].rearrange("(c p) d -> p c d", p=128), writes=["W2k"])
            P.dma("pool", W2[:, 1, :, :], W["nsa_cmp_v_w2"][l].rearrange("(c p) d -> p c d", p=128), writes=["W2v"])
            P.dma("sp", posn[0:32, 0:64], W["nsa_cmp_pos_k"][l], reads=["posn0"], writes=["posnk"])
            P.dma("sp", posn[0:32, 64:128], W["nsa_cmp_pos_v"][l], reads=["posn0"], writes=["posnv"])
            P.dma("sp", gkc[:], W["nsa_kc_norm"][l].partition_broadcast(128), writes=["gkc"])
            if nlvl == 10:
                return
            P.op("pe", lambda e: e.transpose(banks[0][:, 0:128], posn[:, :], cfv("identf")),
                 reads=["posnk", "posnv", "posn0", "cft"], writes=[("b", 0)])
            P.op("act", lambda e: e.activation(out=pos2T[:], in_=banks[0][:, 0:32], func=AF.Copy), reads=[("b", 0)], writes=["pos2T"])
            if nlvl == 11:
                return
            kvv = KV3[:, 0, :].rearrange("p (n s) -> p n s", s=16)
            KVR = [("KV3", i) for i in range(NT)]
            P.op("dve", lambda e: e.memset(kcp[:, :, :], 0.0), writes=["kcp0"])
            for ll in range(32):
                src = kvv[:, 0:127, ll] if ll < 16 else kvv[:, 1:128, ll - 16]
                P.op("dve", lambda e, ll=ll, src=src: e.tensor_scalar(
                    out=kcp[:, ll, 0:127], in0=src, scalar1=pos2T[:, ll:ll + 1], scalar2=None, op0=ALU.add),
                    reads=KVR + ["pos2T", "kcp0"], writes=[("kcp", ll)])
            P.op("dve", lambda e: e.memset(hg[:, :, :], 0.0), writes=["hg0"])
            if nlvl == 12:
                return
            nvar = os.environ.get("NSA_VAR", "")
            for kv in range(1 if nvar == "B" else 2):
                for jc in range(2):
                    reg = kv * 2 + jc
                    for ll in range(32):
                        P.op("pe", lambda e, kv=kv, jc=jc, ll=ll, reg=reg: e.matmul(
                            banks[2 + kv][:, jc * 128:jc * 128 + 128], lhsT=W1[kv * 64:(kv + 1) * 64, ll, jc * 128:(jc + 1) * 128],
                            rhs=kcp[kv * 64:(kv + 1) * 64, ll, :], start=(ll == 0), stop=(ll == 31)),
                            reads=["W1k", "W1v", ("kcp", ll), "kcp0"], writes=[("st", kv)])
            for kv in range(2):
                P.op("act", lambda e, kv=kv: e.activation(out=hg[:, 2 * kv:2 * kv + 2, :], in_=banks[2 + kv][:, 0:256].rearrange("p (r n) -> p r n", n=128),
                                                         func=AF.Gelu_apprx_tanh), reads=[("st", kv), "hg0"], writes=[("hg", kv)])
            if nlvl == 13:
                return
            for jc in range(2):
                P.op("pe", lambda e, jc=jc: e.matmul(banks[4][:, 0:64], lhsT=hg[:, jc, :], rhs=W2[:, 0, jc, :],
                                                     start=(jc == 0), stop=(jc == 1)), reads=[("hg", 0), "W2k"], writes=[("st", 2)])
            for jc in range(2):
                P.op("pe", lambda e, jc=jc: e.matmul(banks[4][:, 64:128], lhsT=hg[:, 2 + jc, :], rhs=W2[:, 1, jc, :],
                                                     start=(jc == 0), stop=(jc == 1)), reads=[("hg", 1), "W2v"], writes=[("st", 2)])
            P.op("dve", lambda e: e.memset(vca[:, 0:64], 0.0), writes=["vca0"])
            P.op("dve", lambda e: e.memset(vca[:, 64:128], 1.0), writes=["vca1"])
            P.op("act", lambda e: e.activation(out=vca[:, 0:64], in_=banks[4][:, 64:128], func=AF.Copy),
                 reads=[("st", 2), "vca0"], writes=["vca"])
            if nlvl == 15:
                return
            P.op("dve", lambda e: e.memset(kcn[:], 0.0), writes=["kcn0"])
            P.op("act", lambda e: e.activation(out=scrA[:, 0:64], in_=banks[4][:, 0:64], func=AF.Square, accum_out=sm[:, 0:1]),
                 reads=[("st", 2)], writes=["scrA", "sm"])
            rstd_small(sm[:, 0:1], sm[:, 8:9], 1, "sm", "sm2")
            P.op("dve", lambda e: e.scalar_tensor_tensor(out=kcn[:, :], in0=banks[4][:, 0:64], scalar=sm[:, 8:9], in1=gkc[:, :],
                                                         op0=ALU.mult, op1=ALU.mult), reads=[("st", 2), "sm2", "gkc", "kcn0"], writes=["kcn"])
            if nlvl == 16:
                return
            rope_ops(kcn[:, :].rearrange("p (a b) -> p a b", b=64), 1, 16, "kcn")
            P.op("dve", lambda e: e.memset(kcnb[:, 64:128], 0.0), writes=["kcnb0"])
            P.op("act", lambda e: e.activation(out=kcnb[:, 0:64], in_=kcn[:], func=AF.Copy), reads=["kcn"], writes=["kcnb"])
            P.op("pe", lambda e: e.transpose(pTb[:, 0:128], kcnb[:, :], identb[:]), reads=["kcnb", "kcnb0", "identb"], writes=["pT"])
            P.op("act", lambda e: e.activation(out=kcmpT[0:64, :], in_=pTb[0:64, 0:128], func=AF.Copy), reads=["pT"], writes=["kcmpT"])

            def gate_w(accb, acct, br, j):
                P.op("act", lambda e: e.activation(out=scrA[0:64, 0:512], in_=accb[64:128, :], func=AF.Copy),
                     reads=[acct], writes=["scrA"])
                P.op("dve", lambda e: e.tensor_scalar(out=scrA[0:64, 0:512], in0=scrA[0:64, 0:512], scalar1=1e-30, scalar2=None, op0=ALU.max),
                     reads=["scrA"], writes=["scrA"])
                P.op("dve", lambda e: e.reciprocal(out=scrA[0:64, 0:512], in_=scrA[0:64, 0:512]), reads=["scrA"], writes=["scrA"])
                for hh in range(4):
                    P.op("pe", lambda e, hh=hh: e.matmul(banks[0][0:64, hh * 128:(hh + 1) * 128],
                                                         lhsT=cfv("sel", 768)[0:32, (3 * hh + br) * 64:(3 * hh + br + 1) * 64],
                                                         rhs=gTt[0:32, j * 128:(j + 1) * 128], start=True, stop=True),
                         reads=["cft", ("gT", j // 4)], writes=[("b", 0)])
                P.op("act", lambda e: e.activation(out=gs[0:64, :], in_=banks[0][0:64, :], func=AF.Copy), reads=[("b", 0)], writes=["gs"])
                P.op("dve", lambda e: e.tensor_tensor(out=scrB[0:64, 0:512], in0=scrA[0:64, 0:512], in1=gs[0:64, :], op=ALU.mult),
                     reads=["scrA", "gs"], writes=["gw"])

            sc = 0
            if nlvl == 1:
                return
            for j in range(NT if nlvl >= 3 else 8):
                buf = j % 2
                qrhs = qT4[0:64, :, j * 128:(j + 1) * 128]
                r = sc % 3
                sc += 1
                sb_ = banks[2 + r]
                P.op("pe", lambda e, sb_=sb_, qrhs=qrhs: e.matmul(sb_[:, :], lhsT=kcmpT[0:64, :], rhs=qrhs, start=True, stop=False),
                     reads=["kcmpT"] + [("qT4", j, h_) for h_ in range(4)], writes=[("st", r)])
                P.op("pe", lambda e, sb_=sb_, j=j: e.matmul(sb_[:, :], lhsT=identb[:],
                                                          rhs=mcmp[:, j * 128:(j + 1) * 128].unsqueeze(1).to_broadcast([128, 4, 128]),
                                                          start=False, stop=True), reads=["identb", "mcmp"], writes=[("st", r)])
                P.op("act", lambda e, sb_=sb_, r=r: e.activation(out=PT[r][:], in_=sb_[:, :], func=AF.Exp, scale=0.125),
                     reads=[("st", r)], writes=[("PT", r)])
                accb = banks[5]
                P.op("pe", lambda e, r=r, accb=accb: e.matmul(accb[:, :], lhsT=vca[:, :], rhs=PT[r][:], start=True, stop=True),
                     reads=["vca", "vca1", ("PT", r)], writes=[("acc", 0)])
                if j >= 8:
                    for hh in range(4):
                        P.op("pe", lambda e, r=r, hh=hh: e.matmul(banks[1][:, hh * 33:(hh + 1) * 33], lhsT=PT[r][:, hh * 128:(hh + 1) * 128],
                                                                 rhs=cbv("ov", 64)[:, 0:33], start=True, stop=True),
                             reads=[("PT", r), "cbt"], writes=[("b", 1)])
                gate_w(accb, ("acc", 0), 0, j)
                P.op("dve", lambda e, accb=accb: e.tensor_tensor(out=scrC[0:64, 0:512], in0=accb[0:64, :], in1=scrB[0:64, 0:512], op=ALU.mult),
                     reads=[("acc", 0), "gw"], writes=["scrC"])
                if j >= 8:
                    U = banks[1][:, 0:132].rearrange("p (h c) -> p h c", c=33)
                    P.op("dve", lambda e, U=U: e.tensor_scalar(out=sm[:, 32:36], in0=U[:, :, 32], scalar1=1e-30, scalar2=None, op0=ALU.max),
                         reads=[("b", 1)], writes=["rD"])
                    P.op("dve", lambda e: e.reciprocal(out=sm[:, 32:36], in_=sm[:, 32:36]), reads=["rD"], writes=["rD"])
                    P.op("dve", lambda e, j=j: e.tensor_copy(out=impt[:, 0:32], in_=cfv("addmask", 512)[:, j * 32:(j + 1) * 32]),
                         reads=["cft"], writes=["impt"])
                    for hh in range(4):
                        P.op("dve", lambda e, U=U, hh=hh: e.scalar_tensor_tensor(out=impt[:, 0:32], in0=U[:, hh, 0:32], scalar=sm[:, 32 + hh:33 + hh],
                                                                                in1=impt[:, 0:32], op0=ALU.mult, op1=ALU.add),
                             reads=[("b", 1), "rD", "impt"], writes=["impt"])
                    P.op("dve", lambda e: e.max(out=m8[:, 0:8], in_=impt[:, 0:32]), reads=["impt"], writes=["m8a"])
                    P.op("dve", lambda e: e.match_replace(out=impt[:, 32:64], in_to_replace=m8[:, 0:8], in_values=impt[:, 0:32], imm_value=-3e38),
                         reads=["impt", "m8a"], writes=["impt2"])
                    P.op("dve", lambda e: e.max(out=m8[:, 8:16], in_=impt[:, 32:64]), reads=["impt2"], writes=["m8b"])
                    P.op("dve", lambda e: e.tensor_scalar(out=impt[:, 32:64], in0=impt[:, 0:32], scalar1=m8[:, 15:16], scalar2=None, op0=ALU.is_ge),
                         reads=["impt", "m8b", "impt2"], writes=["selm"])
                    P.op("dve", lambda e: e.tensor_scalar(out=Mtm[:, 0:32], in0=impt[:, 32:64], scalar1=-1.0, scalar2=-NEG, op0=ALU.add, op1=ALU.mult),
                         reads=["selm", "Mtm0"], writes=["Mtm"])
                    P.op("pe", lambda e: e.transpose(pTb[:, 0:128], Mtm[:, :], identb[:]), reads=["Mtm", "Mtm0", "identb"], writes=["pT"])
                    P.op("act", lambda e, buf=buf: e.activation(out=MT[buf][0:32, :], in_=pTb[0:32, 0:128], func=AF.Copy),
                         reads=["pT"], writes=[("MT", buf)])
                for br, kvi, va, vtag, accb, acct in ((2, 2, vwa, "vwa", banks[5], ("acc", 0)), (1, 1, vsa, "vsa", banks[6], ("acc", 1))):
                    k_lo = max(0, j - 4) if br == 2 else 0
                    for ks in range(k_lo, j + 1):
                        r = sc % 3
                        sc += 1
                        sb_ = banks[2 + r]
                        extra = []
                        if ks == j:
                            extra.append("caus")
                        if br == 2 and ks == j - 4:
                            extra.append("winup")
                        if br == 1 and j >= 8:
                            extra.append("sel")
                        P.op("pe", lambda e, sb_=sb_, kvi=kvi, ks=ks, qrhs=qrhs, last=(len(extra) == 0): e.matmul(
                            sb_[:, :], lhsT=KV3[0:64, kvi, ks * 128:(ks + 1) * 128], rhs=qrhs, start=True, stop=last),
                            reads=[("KV3", ks)] + [("qT4", j, h_) for h_ in range(4)], writes=[("st", r)])
                        for xi, kind in enumerate(extra):
                            last = (xi == len(extra) - 1)
                            if kind == "sel":
                                P.op("pe", lambda e, sb_=sb_, ks=ks, buf=buf, last=last: e.matmul(
                                    sb_[:, :], lhsT=Et[0:32, ks * 128:(ks + 1) * 128],
                                    rhs=MT[buf][0:32, :].unsqueeze(1).to_broadcast([32, 4, 128]), start=False, stop=last),
                                    reads=["Et", ("MT", buf)], writes=[("st", r)])
                            else:
                                P.op("pe", lambda e, sb_=sb_, kind=kind, last=last: e.matmul(
                                    sb_[:, :], lhsT=identb[:], rhs=cbv(kind).unsqueeze(1).to_broadcast([128, 4, 128]), start=False, stop=last),
                                    reads=["identb", "cbt"], writes=[("st", r)])
                        P.op("act", lambda e, sb_=sb_, r=r: e.activation(out=PT[r][:], in_=sb_[:, :], func=AF.Exp, scale=0.125),
                             reads=[("st", r)], writes=[("PT", r)])
                        P.op("pe", lambda e, accb=accb, va=va, ks=ks, r=r, first=(ks == k_lo), last2=(ks == j): e.matmul(
                            accb[:, :], lhsT=va[:, ks, :], rhs=PT[r][:], start=first, stop=last2),
                            reads=[(vtag, ks), vtag[0:2] + "ones", ("PT", r)], writes=[acct])
                    gate_w(accb, acct, br, j)
                    if br == 2:
                        P.op("dve", lambda e, accb=accb: e.tensor_tensor(out=scrD[0:64, 0:512], in0=accb[0:64, :], in1=scrB[0:64, 0:512], op=ALU.mult),
                             reads=[acct, "gw"], writes=["scrD"])
                        P.op("dve", lambda e: e.tensor_tensor(out=scrC[0:64, 0:512], in0=scrC[0:64, 0:512], in1=scrD[0:64, 0:512], op=ALU.add),
                             reads=["scrC", "scrD"], writes=["scrC"])
                    else:
                        P.op("dve", lambda e, accb=accb: e.tensor_tensor(out=scrD[0:64, 0:512], in0=accb[0:64, :], in1=scrB[0:64, 0:512], op=ALU.mult),
                             reads=[acct, "gw"], writes=["scrD"])
                        P.op("dve", lambda e, buf=buf: e.tensor_tensor(out=OT[0:64, :, buf, :],
                                                                       in0=scrD[0:64, 0:512].rearrange("p (h t) -> p h t", t=128),
                                                                       in1=scrC[0:64, 0:512].rearrange("p (h t) -> p h t", t=128), op=ALU.add),
                             reads=["scrD", "scrC"], writes=[("OT", buf)])
                wout_tile(j, buf)

        fns = {"fox": fox, "gmlp": gmlp, "pool": pool, "nsa": nsa}
        for sname in ("fox", "gmlp", "pool", "nsa"):
            if sname in seq_stages:
                fns[sname]()
                P.fence(dummy[:])
                yield sname

    P.op("dve", lambda e: e.memset(EPS_T[:, 1:2], 1.0), reads=["epst"], writes=["eps1"])
    P.op("dve", lambda e: e.memset(EPS_T[:, 2:3], 0.5), reads=["epst", "eps1"], writes=["eps1"])
    outs = []
    stage_idx = {"ffn1": 0, "fox": 1, "gmlp": 2, "pool": 3, "nsa": 4, "ffn2": 5}

    def dump(sname):
        if dbg:
            outs.append(P.dma("sp", dbg_d[stage_idx[sname]].rearrange("(i p) d -> p i d", p=128), X[:, :, :],
                              reads=[("X", i) for i in range(NT)], writes=[("dbgout", sname)], sem_key="dbg"))

    for s in range(nseq):
        for h2 in range(2):
            P.dma("sp", X[:, h2 * 8:(h2 + 1) * 8, :], x_d[s, h2 * 1024:(h2 + 1) * 1024, :].rearrange("(i p) d -> p i d", p=128),
                  writes=[("X", i) for i in range(h2 * 8, (h2 + 1) * 8)], sem_key=("xin", h2))
        for l in range(nlayers):
            if "ffn1" in stages:
                ffn(l, 1)
                P.fence(dummy[:])
                dump("ffn1")
            for sname in mixer(l, stages):
                dump(sname)
            if "ffn2" in stages:
                ffn(l, 2)
                P.fence(dummy[:])
                dump("ffn2")
        for h2 in range(2):
            outs.append(P.dma("sp", y_d[s, h2 * 1024:(h2 + 1) * 1024, :].rearrange("(i p) d -> p i d", p=128), X[:, h2 * 8:(h2 + 1) * 8, :],
                              reads=[("X", i) for i in range(h2 * 8, (h2 + 1) * 8)], writes=[("yout", s, h2)], sem_key=("yout", h2)))
    P.emit(final_waits=outs)
    st.close()
    return nc, P


_CACHE = {}


def kernel(**inputs):
    n_cores = 8
    x = np.ascontiguousarray(np.asarray(inputs["x"], dtype=np.float32))
    nseq = x.shape[0] // n_cores
    if "nc" not in _CACHE:
        _CACHE["nc"] = build(nseq, 2)[0]
        _CACHE["consts"] = make_consts()
    nc = _CACHE["nc"]
    cb, cf = _CACHE["consts"]
    params = {k: np.ascontiguousarray(np.asarray(inputs[k], dtype=np.float32)) for k in PARAM_SHAPES}
    in_maps = []
    for c in range(n_cores):
        m = {"x": x[c * nseq:(c + 1) * nseq], "cb": cb, "cf": cf}
        m.update(params)
        in_maps.append(m)
    res = run_bass_kernel_spmd(nc, in_maps, core_ids=list(range(n_cores)))
    return np.concatenate([np.asarray(r["y"]) for r in res.results], axis=0).astype(np.float32)
```

```python
import contextlib
import os
import numpy as np
import ml_dtypes
import concourse.bass as bass
import concourse.mybir as mybir
from concourse.bass_utils import run_bass_kernel_spmd

F32 = mybir.dt.float32
BF16 = mybir.dt.bfloat16
AF = mybir.ActivationFunctionType
ALU = mybir.AluOpType
AX = mybir.AxisListType

T = 2048
D = 1024
DFF = 2816
NIN = 2192
NT = 16
EPS = 1e-6
NEG = -30000.0
ROPE_THETA = 500000.0


class Prog:
    ENG = ("pe", "act", "dve", "pool", "sp")
    EPOCH = 8000

    def __init__(self, nc):
        self.nc = nc
        self.ops = []
        self.last_w = {}
        self.readers = {}
        self.fence_op = None

    def op(self, eng, fn, reads=(), writes=(), dma=False, sem_key=None, extra_deps=()):
        i = len(self.ops)
        deps = set(extra_deps)
        if self.fence_op is not None:
            deps.add(self.fence_op)
        for t in reads:
            w = self.last_w.get(t)
            if w is not None:
                deps.add(w)
        for t in writes:
            w = self.last_w.get(t)
            if w is not None:
                deps.add(w)
            for r in self.readers.get(t, ()):
                deps.add(r)
        for t in reads:
            self.readers.setdefault(t, []).append(i)
        for t in writes:
            self.last_w[t] = i
            self.readers[t] = []
        if dma and sem_key is None:
            sem_key = ("dma",) + tuple(writes)
        self.ops.append(dict(eng=eng, fn=fn, deps=deps, dma=dma, sem_key=sem_key, flag=False))
        return i

    def dma(self, eng, out, in_, reads=(), writes=(), sem_key=None, **kw):
        return self.op(eng, lambda e: e.dma_start(out=out, in_=in_, **kw), reads, writes, dma=True, sem_key=sem_key)

    def fence(self, dummy):
        ops = self.ops
        outstanding = set(self.last_w.values())
        for rs in self.readers.values():
            outstanding.update(rs)
        if self.fence_op is not None:
            outstanding.add(self.fence_op)
        best = {}
        deps = set()
        for d in outstanding:
            o = ops[d]
            if o["dma"]:
                k = ("D", o["sem_key"])
            else:
                k = ("E", o["eng"])
            if k not in best or best[k] < d:
                best[k] = d
        deps = set(best.values())
        self.last_w = {}
        self.readers = {}
        self.fence_op = None
        i = self.op("dve", lambda e: e.memset(dummy, 0.0), extra_deps=deps)
        self.fence_op = i
        return i

    def emit(self, final_waits=()):
        nc = self.nc
        ops = self.ops
        for o in ops:
            if o["eng"] == "pe" and not o["dma"]:
                o["deps"] = {d for d in o["deps"] if not (ops[d]["eng"] == "pe" and not ops[d]["dma"])}
            for d in o["deps"]:
                ops[d]["flag"] = True
        for i in final_waits:
            ops[i]["flag"] = True
        for o in ops:
            if o["dma"]:
                o["flag"] = True
        sem_names = []
        seen = set()
        cnt = {}
        for i, o in enumerate(ops):
            if not o["flag"]:
                continue
            if o["dma"]:
                key = ("D", o["sem_key"])
                cnt[key] = cnt.get(key, 0) + 16
                o["sig"] = (key, cnt[key])
            else:
                ep = cnt.get(("ep", o["eng"]), 0)
                key = ("E", o["eng"], ep)
                cnt[key] = cnt.get(key, 0) + 1
                o["sig"] = (key, cnt[key])
                if cnt[key] >= self.EPOCH:
                    cnt[("ep", o["eng"])] = ep + 1
            if o["sig"][0] not in seen:
                seen.add(o["sig"][0])
                sem_names.append(o["sig"][0])
        self.n_sems = len(sem_names)
        with contextlib.ExitStack() as st:
            sems = {k: st.enter_context(nc.semaphore("s%d" % n)) for n, k in enumerate(sem_names)}
            block = st.enter_context(nc.Block())
            for en in self.ENG:
                mine = [(i, o) for i, o in enumerate(ops) if o["eng"] == en]
                fin = list(final_waits) if en == "sp" else []

                def body(e, mine=mine, fin=fin):
                    waited = {}

                    def wait_for(d):
                        key, val = ops[d]["sig"]
                        if waited.get(key, 0) >= val:
                            return
                        e.wait_ge(sems[key], val)
                        waited[key] = val

                    for i, o in mine:
                        for d in sorted(o["deps"]):
                            wait_for(d)
                        ins = o["fn"](e)
                        if o["flag"]:
                            key, val = o["sig"]
                            ins.then_inc(sems[key], 16 if o["dma"] else 1)
                    for d in fin:
                        wait_for(d)

                dec = {"pe": block.tensor, "act": block.scalar, "dve": block.vector,
                       "pool": block.gpsimd, "sp": block.sync}[en]
                dec(body)
        return self


class Rec:
    def __init__(self):
        self.items = []

    def op(self, eng, fn, reads=(), writes=(), **kw):
        self.items.append((eng, fn, list(reads), list(writes), kw))

    def dma(self, eng, out, in_, reads=(), writes=(), sem_key=None, **kw):
        self.items.append((eng, lambda e: e.dma_start(out=out, in_=in_, **kw), list(reads), list(writes),
                           dict(dma=True, sem_key=sem_key)))


def interleave(P, recs):
    n = max(len(r.items) for r in recs)
    for k in range(n):
        for r in recs:
            if k < len(r.items):
                eng, fn, rd, wr, kw = r.items[k]
                P.op(eng, fn, rd, wr, **kw)


CB = {}
CF = {}


def _alloc(tab, name, n):
    off = tab.get("_n", 0)
    tab[name] = off
    tab["_n"] = off + n
    return off


for _n, _w in [("ident", 128), ("caus", 128), ("winup", 128), ("ov", 64), ("ones", 128),
               ("ad", 512), ("ap", 512), ("a0h", 512), ("a0l", 512), ("mcmp", 2048), ("E", 2048)]:
    _alloc(CB, _n, _w)
for _n, _w in [("identf", 128), ("tril", 128), ("ltri", 128), ("onesf", 128), ("row64", 128),
               ("addmask", 512), ("rope", 17 * 16), ("sel", 12 * 64)]:
    _alloc(CF, _n, _w)
NCB = CB["_n"]
NCF = CF["_n"]


def make_consts():
    cb = np.zeros((128, NCB), np.float32)
    cf = np.zeros((128, NCF), np.float32)
    p = np.arange(128)[:, None]
    q = np.arange(128)[None, :]
    cb[:, CB["ident"]:CB["ident"] + 128] = (p == q)
    cb[:, CB["caus"]:CB["caus"] + 128] = np.where(p <= q, 0.0, NEG)
    cb[:, CB["winup"]:CB["winup"] + 128] = np.where(p > q, 0.0, NEG)
    ncmp = 127
    cs = np.arange(ncmp) * 16
    ce = cs + 32
    ss = np.arange(32) * 64
    se = ss + 64
    ov = np.clip(np.minimum(ce[:, None], se[None, :]) - np.maximum(cs[:, None], ss[None, :]), 0, None) / 32.0
    cb[0:127, CB["ov"]:CB["ov"] + 32] = ov
    cb[0:127, CB["ov"] + 32] = 1.0
    cb[:, CB["ones"]:CB["ones"] + 128] = 1.0
    sizes = (2, 4, 8, 16)
    for g, wn in enumerate(sizes):
        ad = np.zeros((128, 128)); apv = np.zeros((128, 128)); a0 = np.zeros((128, 128))
        for t in range(128):
            for s in range(t - wn + 1, t + 1):
                if s >= 0:
                    ad[s, t] += 1.0 / wn
                else:
                    apv[128 + s, t] += 1.0 / wn
            ad[t, t] -= 1.0
            cntv = min(t + 1, wn)
            for s in range(max(0, t - wn + 1), t + 1):
                a0[s, t] += 1.0 / cntv
            a0[t, t] -= 1.0
        a0h = a0.astype(np.float32).astype(ml_dtypes.bfloat16).astype(np.float32)
        a0l = (a0 - a0h)
        cb[:, CB["ad"] + g * 128:CB["ad"] + (g + 1) * 128] = ad
        cb[:, CB["ap"] + g * 128:CB["ap"] + (g + 1) * 128] = apv
        cb[:, CB["a0h"] + g * 128:CB["a0h"] + (g + 1) * 128] = a0h
        cb[:, CB["a0l"] + g * 128:CB["a0l"] + (g + 1) * 128] = a0l
    tt = np.arange(T)[None, :]
    nn = np.arange(128)[:, None]
    mc = np.where((16 * nn + 31 <= tt) & (nn < 127), 0.0, NEG)
    cb[:, CB["mcmp"]:CB["mcmp"] + T] = mc
    jj = np.arange(32)[:, None]
    cb[0:32, CB["E"]:CB["E"] + T] = ((tt // 64) == jj)

    cf[:, CF["identf"]:CF["identf"] + 128] = (p == q)
    cf[:, CF["tril"]:CF["tril"] + 128] = (q <= p)
    cf[:, CF["ltri"]:CF["ltri"] + 128] = (p <= q)
    cf[:, CF["onesf"]:CF["onesf"] + 128] = 1.0
    cf[64, CF["row64"]:CF["row64"] + 128] = 1.0
    tpos = (np.arange(NT)[None, :] * 128 + np.arange(128)[:, None])
    cur = tpos // 64
    blk = np.arange(32)[None, None, :]
    forced = ((blk == 0) | (blk == cur[:, :, None]) | (blk == cur[:, :, None] - 1)).astype(np.float32)
    am = np.where(blk <= cur[:, :, None], 1000.0 * forced, -1e30).astype(np.float32)
    cf[:, CF["addmask"]:CF["addmask"] + 512] = am.reshape(128, 512)
    inv = (np.float32(ROPE_THETA) ** (-np.arange(8, dtype=np.float32) * np.float32(2.0) / np.float32(16))).astype(np.float32)
    rp = np.zeros((128, 17, 16), np.float32)
    for sl in range(17):
        pos = (tpos[:, sl] if sl < 16 else (np.arange(128) * 16 + 31)).astype(np.float32)
        ang = (pos[:, None] * inv[None, :]).astype(np.float32)
        rp[:, sl, 0:8] = np.cos(ang.astype(np.float64))
        rp[:, sl, 8:16] = np.sin(ang.astype(np.float64))
    cf[:, CF["rope"]:CF["rope"] + 17 * 16] = rp.reshape(128, -1)
    sel = np.zeros((128, 12, 64), np.float32)
    for k in range(12):
        sel[k, k, :] = 1.0
    cf[:, CF["sel"]:CF["sel"] + 768] = sel.reshape(128, -1)
    return cb.astype(ml_dtypes.bfloat16), cf.astype(np.float32)


PARAM_SHAPES = {
    'ffn1_norm': (2, 1024), 'ffn1_w1': (2, 1024, 2816), 'ffn1_w3': (2, 1024, 2816), 'ffn1_w2': (2, 2816, 1024),
    'mix_norm': (2, 1024), 'w_in': (2, 1024, 2192), 'w_out': (2, 1024, 1024),
    'fox_f_bias': (2, 4), 'fox_q_norm': (2, 64), 'fox_k_norm': (2, 64),
    'gmlp_v_norm': (2, 256), 'gmlp_w_s': (2, 4, 128, 128), 'gmlp_b_s': (2, 4, 128),
    'nsa_q_norm': (2, 64), 'nsa_kc_norm': (2, 64), 'nsa_ks_norm': (2, 64), 'nsa_kw_norm': (2, 64),
    'nsa_cmp_pos_k': (2, 32, 64), 'nsa_cmp_k_w1': (2, 2048, 256), 'nsa_cmp_k_w2': (2, 256, 64),
    'nsa_cmp_pos_v': (2, 32, 64), 'nsa_cmp_v_w1': (2, 2048, 256), 'nsa_cmp_v_w2': (2, 256, 64),
    'nsa_gate_bias': (2, 12), 'pool_w': (2, 4, 64, 64), 'pool_scale': (2, 256),
    'ffn2_norm': (2, 1024), 'ffn2_w1': (2, 1024, 2816), 'ffn2_w3': (2, 1024, 2816), 'ffn2_w2': (2, 2816, 1024),
}

ARENA_BYTES = 134 * 1024


def build(nseq, nlayers, dbg=False, stages=("ffn1", "fox", "gmlp", "pool", "nsa", "ffn2")):
    nc = bass.Bass("TRN2", target_bir_lowering=False)
    x_d = nc.dram_tensor("x", [nseq, T, D], F32, kind="ExternalInput").ap()
    y_d = nc.dram_tensor("y", [nseq, T, D], F32, kind="ExternalOutput").ap()
    dbg_d = nc.dram_tensor("dbg", [6, T, D], F32, kind="ExternalOutput").ap() if dbg else None
    W = {k: nc.dram_tensor(k, list(s), F32, kind="ExternalInput").ap() for k, s in PARAM_SHAPES.items()}
    cb_d = nc.dram_tensor("cb", [128, NCB], BF16, kind="ExternalInput").ap()
    cf_d = nc.dram_tensor("cf", [128, NCF], F32, kind="ExternalInput").ap()

    P = Prog(nc)
    st = contextlib.ExitStack()

    def sb(name, shape, dt):
        return st.enter_context(nc.sbuf_tensor(name, shape, dt))

    X = sb("X", [128, NT, D], F32)
    gb = sb("gb", [128, D], F32)
    arena = sb("arena", [128, ARENA_BYTES // 4], F32)
    identb = sb("identb", [128, 128], BF16)
    ssq = sb("ssq", [128, 16], F32)
    rstd = sb("rstd", [128, 16], F32)
    hb = [sb("hb%d" % i, [128, D], BF16) for i in range(2)]
    dummy = sb("fdummy", [128, 8], F32)
    banks = [st.enter_context(nc.psum_tensor("bank%d" % i, [128, 512], F32)) for i in range(8)]
    pTb = banks[7][:, :].bitcast(BF16)
    junk = banks[6][:, :].bitcast(BF16)

    class Arena:
        def __init__(self):
            self.off = 0

        def reset(self, off=0):
            self.off = off

        def view(self, shape, dt, parts=128):
            esz = 4 if dt == F32 else 2
            n = int(np.prod(shape[1:]))
            nbytes = (n * esz + 3) // 4 * 4
            a = arena[:, self.off // 4:(self.off + nbytes) // 4]
            if dt != F32:
                a = a.bitcast(dt)
            a = a[:, 0:n]
            self.off += nbytes
            assert self.off <= ARENA_BYTES, ("arena overflow", self.off)
            if len(shape) == 3:
                a = a.rearrange("p (a b) -> p a b", b=shape[2])
            elif len(shape) == 4:
                a = a.rearrange("p (a b c) -> p a b c", b=shape[2], c=shape[3])
            return a

    AR = Arena()

    P.dma("sp", identb[:], cb_d[:, CB["ident"]:CB["ident"] + 128], writes=["identb"])

    def norm_T(gain_ap, hT):
        P.dma("sp", gb[:], gain_ap.partition_broadcast(128), writes=["gb"])
        for i in range(NT):
            P.op("act", lambda e, i=i: e.activation(out=hb[i % 2][:], in_=X[:, i, :], func=AF.Square,
                                                   accum_out=ssq[:, i:i + 1]),
                 reads=[("X", i)], writes=[("hb", i % 2), ("ssq", i)])
        P.op("act", lambda e: e.activation(out=rstd[:], in_=ssq[:], func=AF.Sqrt, scale=1.0 / D, bias=EPS_T[:, 0:1]),
             reads=[("ssq", i) for i in range(NT)] + ["epst"], writes=["rstd_s"])
        P.op("dve", lambda e: e.reciprocal(out=rstd[:], in_=rstd[:]), reads=["rstd_s"], writes=["rstd"])
        for i in range(NT):
            b = i % 2
            P.op("dve", lambda e, i=i, b=b: e.scalar_tensor_tensor(out=hb[b][:], in0=X[:, i, :], scalar=rstd[:, i:i + 1],
                                                                in1=gb[:], op0=ALU.mult, op1=ALU.mult),
                 reads=[("X", i), "rstd", "gb"], writes=[("hb", b)])
            for c in range(8):
                P.op("pe", lambda e, b=b, c=c: e.transpose(pTb[:, c * 128:(c + 1) * 128], hb[b][:, c * 128:(c + 1) * 128], identb[:]),
                     reads=[("hb", b), "identb"], writes=[("bk", 7)])
            P.op("act", lambda e, i=i: e.activation(out=hT[:, :, i * 128:(i + 1) * 128],
                                                   in_=pTb[:, :].rearrange("p (c t) -> p c t", t=128), func=AF.Copy),
                 reads=[("bk", 7)], writes=[("hT", i)])

    EPS_T = sb("epst", [128, 4], F32)
    P.op("dve", lambda e: e.memset(EPS_T[:], EPS), writes=["epst"])

    def ffn(l, which):
        pre = "ffn%d_" % which
        w1_d = W[pre + "w1"][l].rearrange("(k p) f -> p k f", p=128)
        w3_d = W[pre + "w3"][l].rearrange("(k p) f -> p k f", p=128)
        w2_d = W[pre + "w2"][l]
        AR.reset()
        hT = AR.view([128, 8, T], BF16)
        gT = AR.view([128, 6, T], BF16)
        w2b = [AR.view([128, 6, D], BF16) for _ in range(2)]
        w13 = [AR.view([128, 2, 8, 256], BF16) for _ in range(3)]
        sil = [AR.view([128, 512], F32) for _ in range(2)]
        norm_T(W[pre + "norm"][l], hT)
        import os
        lvl = int(os.environ.get("FFN_LEVEL", "2"))
        groups = [(0, 6), (6, 6), (12, 5), (17, 5)]
        if lvl == 0:
            groups = []
        cnt = 0
        oc = 0
        uc = 0
        for g, (c0, n) in enumerate(groups):
            wb = w2b[g % 2]
            P.dma("pool", wb[:, 0:n, :], w2_d[c0 * 128:(c0 + n) * 128, :].rearrange("(c p) f -> p c f", p=128),
                  writes=[("w2b", g % 2)])
            units = [(c0 + u, min(2, n - u)) for u in range(0, n, 2)]
            for (j0, nj) in units:
                slot = uc % 3
                uc += 1
                ws = w13[slot]
                P.dma("pool", ws[:, 0, :, 0:nj * 128], w1_d[:, :, j0 * 128:(j0 + nj) * 128], writes=[("w13a", slot)])
                P.dma("pool", ws[:, 1, :, 0:nj * 128], w3_d[:, :, j0 * 128:(j0 + nj) * 128], writes=[("w13b", slot)])
                for tb in range(4):
                    hreads = [("hT", 4 * tb + qq) for qq in range(4)]
                    for jj in range(nj):
                        jl = j0 + jj - c0
                        r = cnt % 2
                        cnt += 1
                        pa = banks[r]
                        pb = banks[2 + r]
                        for k in range(8):
                            P.op("pe", lambda e, pa=pa, ws=ws, k=k, jj=jj, tb=tb: e.matmul(
                                pa[:, :], lhsT=ws[:, 0, k, jj * 128:(jj + 1) * 128], rhs=hT[:, k, tb * 512:(tb + 1) * 512],
                                start=(k == 0), stop=(k == 7)), reads=[("w13a", slot)] + hreads, writes=[("pa", r)])
                        for k in range(8):
                            P.op("pe", lambda e, pb=pb, ws=ws, k=k, jj=jj, tb=tb: e.matmul(
                                pb[:, :], lhsT=ws[:, 1, k, jj * 128:(jj + 1) * 128], rhs=hT[:, k, tb * 512:(tb + 1) * 512],
                                start=(k == 0), stop=(k == 7)), reads=[("w13b", slot)] + hreads, writes=[("pb", r)])
                        P.op("act", lambda e, pa=pa, r=r: e.activation(out=sil[r][:], in_=pa[:, :], func=AF.Silu),
                             reads=[("pa", r)], writes=[("sil", r)])
                        P.op("dve", lambda e, pb=pb, r=r, jl=jl, tb=tb: e.tensor_tensor(
                            out=gT[:, jl, tb * 512:(tb + 1) * 512], in0=pb[:, :], in1=sil[r][:], op=ALU.mult),
                            reads=[("pb", r), ("sil", r)], writes=[("gT", jl, tb)])
            for i in range(NT if lvl >= 2 else 0):
                for half in range(2):
                    r = oc % 2
                    oc += 1
                    po = banks[4 + r]
                    for jl in range(n):
                        P.op("pe", lambda e, po=po, jl=jl, i=i, half=half, wb=wb: e.matmul(
                            po[:, :], lhsT=gT[:, jl, i * 128:(i + 1) * 128], rhs=wb[:, jl, half * 512:(half + 1) * 512],
                            start=(jl == 0), stop=(jl == n - 1)),
                            reads=[("gT", jl, i // 4), ("w2b", g % 2)], writes=[("po", r)])
                    if os.environ.get("STT_ALT", "0") == "1":
                        P.op("dve", lambda e, po=po, i=i, half=half: e.tensor_tensor(
                            out=X[:, i, half * 512:(half + 1) * 512], in0=po[:, :],
                            in1=X[:, i, half * 512:(half + 1) * 512], op=ALU.add),
                            reads=[("po", r), ("X", i)], writes=[("X", i)])
                    else:
                        P.op("dve", lambda e, po=po, i=i, half=half: e.scalar_tensor_tensor(
                            out=X[:, i, half * 512:(half + 1) * 512], in0=po[:, :], scalar=EPS_T[:, 2:3],
                            in1=X[:, i, half * 512:(half + 1) * 512], op0=ALU.mult, op1=ALU.add),
                            reads=[("po", r), ("X", i), "eps1"], writes=[("X", i)])

    def mixer(l, seq_stages):
        AR.reset()
        hT = AR.view([128, 8, T], BF16)
        wbuf = AR.view([128, 8, 772], BF16)
        post_limit = AR.off
        cbt = AR.view([128, CB["mcmp"]], BF16)
        cft = AR.view([128, NCF], F32)
        OT = AR.view([128, 4, 2, 128], BF16)
        wout = AR.view([128, 4, D], BF16)
        scrA = AR.view([128, 640], F32)
        scrB = AR.view([128, 640], F32)
        scrC = AR.view([128, 640], F32)
        sm = AR.view([128, 64], F32)
        scrA1 = AR.view([128, 640], F32)
        scrB1 = AR.view([128, 640], F32)
        sm1 = AR.view([128, 64], F32)
        SCR = [dict(A=scrA, B=scrB, sm=sm, b0=0, b1=1, pT=7), dict(A=scrA1, B=scrB1, sm=sm1, b0=2, b1=3, pT=4)]
        Q = [P]

        def run_pairs(body, n=NT):
            for m0 in range(0, n, 2):
                recs = []
                for i_ in (m0, m0 + 1):
                    r_ = Rec()
                    Q[0] = r_
                    body(i_)
                    Q[0] = P
                    recs.append(r_)
                interleave(P, recs)
        PT = [AR.view([128, 512], BF16) for _ in range(3)]
        base_off = AR.off

        def cbv(name, n=128):
            return cbt[:, CB[name]:CB[name] + n]

        def cfv(name, n=128):
            return cft[:, CF[name]:CF[name] + n]

        Q[0].dma("sp", cbt[:], cb_d[:, 0:CB["mcmp"]], writes=["cbt"])
        Q[0].dma("sp", cft[:], cf_d[:, :], writes=["cft"])
        norm_T(W["mix_norm"][l], hT)
        w_in = W["w_in"][l].rearrange("(k p) f -> p k f", p=128)
        w_out = W["w_out"][l]
        HR = [("hT", i) for i in range(NT)]

        def load_win(c0, n):
            Q[0].dma("pool", wbuf[:, :, 0:n], w_in[:, :, c0:c0 + n], writes=["wbuf"])

        def load_wout(m):
            Q[0].dma("pool", wout[0:64, :, :], w_out[m * 256:(m + 1) * 256, :].rearrange("(c p) f -> p c f", p=64),
                  writes=["wout"])

        def proj_tm(i, col0, ncols, bank, btag, c_in_buf=0):
            for k in range(8):
                Q[0].op("pe", lambda e, k=k: e.matmul(bank[:, 0:ncols], lhsT=hT[:, k, i * 128:(i + 1) * 128],
                                                   rhs=wbuf[:, k, c_in_buf:c_in_buf + ncols], start=(k == 0), stop=(k == 7)),
                     reads=[("hT", i), "wbuf"], writes=[btag])

        def wout_tile(i, buf, bsel=(0, 1)):
            for half in range(2):
                bk = banks[bsel[half]]
                for c in range(4):
                    Q[0].op("pe", lambda e, c=c, half=half, bk=bk: e.matmul(
                        bk[:, :], lhsT=OT[0:64, c, buf, :], rhs=wout[0:64, c, half * 512:(half + 1) * 512],
                        start=(c == 0), stop=(c == 3)), reads=[("OT", buf), ("OT", buf, c), "wout"], writes=[("bk", bsel[half])])
                Q[0].op("dve", lambda e, half=half, bk=bk: e.tensor_tensor(
                    out=X[:, i, half * 512:(half + 1) * 512], in0=bk[:, :], in1=X[:, i, half * 512:(half + 1) * 512],
                    op=ALU.add), reads=[("bk", bsel[half]), ("X", i)], writes=[("X", i)])

        def rstd_small(src_ap, dst_ap, n, tagr, tagw):
            npart = dst_ap.shape[0]
            Q[0].op("act", lambda e: e.activation(out=dst_ap, in_=src_ap, func=AF.Sqrt, scale=1.0 / 64, bias=EPS_T[0:npart, 0:1]),
                 reads=[tagr, "epst"], writes=[(tagw, "_s")])
            Q[0].op("dve", lambda e: e.reciprocal(out=dst_ap, in_=dst_ap), reads=[(tagw, "_s")], writes=[tagw])

        def fox():
            AR.reset(base_off)
            qT2 = AR.view([128, 2, T], BF16)
            kT2 = AR.view([128, 2, T], BF16)
            Vaug = AR.view([128, NT, 4, 128], BF16)
            Bt = AR.view([128, 4, NT, NT], F32)
            zt = AR.view([128, NT, 4], F32)
            ctm = AR.view([128, NT, 4], F32)
            cmb = AR.view([128, NT, 4], F32)
            off = AR.view([128, NT, 4], F32)
            totb = AR.view([128, NT, 4], F32)
            Gqk = AR.view([128, 8, 64], F32)
            fbb = AR.view([128, 4], F32)
            qkb = [AR.view([128, 512], BF16) for _ in range(2)]
            load_win(0, 772)
            load_wout(0)
            for hh in range(4):
                Q[0].dma("sp", Gqk[:, hh, :], W["fox_q_norm"][l].partition_broadcast(128), writes=[("Gqk", hh)])
                Q[0].dma("sp", Gqk[:, 4 + hh, :], W["fox_k_norm"][l].partition_broadcast(128), writes=[("Gqk", 4 + hh)])
            GQ = [("Gqk", hh) for hh in range(8)]
            Q[0].dma("sp", fbb[:], W["fox_f_bias"][l].partition_broadcast(128), writes=["fbb"])
            Q[0].op("dve", lambda e: e.memset(Vaug[:, :, :, 64:128], 1.0), writes=["vones"])
            def fox_body(i):
                p = i % 2
                S = SCR[p]
                A, B, smp = S["A"], S["B"], S["sm"]
                b0, b1, pTi = S["b0"], S["b1"], S["pT"]
                bk0, bk1 = banks[b0], banks[b1]
                pTv = banks[pTi][:, :].bitcast(BF16)
                proj_tm(i, 0, 512, bk0, ("bk", b0), 0)
                proj_tm(i, 512, 260, bk1, ("bk", b1), 512)
                Q[0].op("act", lambda e: e.activation(out=A[:, 0:512], in_=bk0[:, :], func=AF.Square),
                        reads=[("bk", b0)], writes=[("sA", p)])
                Q[0].op("dve", lambda e: e.tensor_reduce(out=smp[:, 0:8], in_=A[:, 0:512].rearrange("p (a b) -> p a b", b=64),
                                                         axis=AX.X, op=ALU.add), reads=[("sA", p)], writes=[("sm", p)])
                rstd_small(smp[:, 0:8], smp[:, 8:16], 8, ("sm", p), ("sm2", p))
                Q[0].op("dve", lambda e: e.tensor_tensor(out=B[:, 0:512].rearrange("p (a b) -> p a b", b=64),
                                                         in0=bk0[:, :].rearrange("p (a b) -> p a b", b=64),
                                                         in1=smp[:, 8:16].unsqueeze(2).to_broadcast([128, 8, 64]), op=ALU.mult),
                        reads=[("bk", b0), ("sm2", p)], writes=[("sB", p)])
                qb_ = qkb[p]
                Q[0].op("dve", lambda e: e.tensor_tensor(out=qb_[:], in0=B[:, 0:512],
                                                         in1=Gqk[:, :, :].rearrange("p a b -> p (a b)"), op=ALU.mult),
                        reads=[("sB", p)] + GQ, writes=[("qkb", p)])
                for c in range(4):
                    Q[0].op("pe", lambda e, c=c: e.transpose(pTv[:, c * 128:(c + 1) * 128], qb_[:, c * 128:(c + 1) * 128], identb[:]),
                            reads=[("qkb", p), "identb"], writes=[("bk", pTi)])
                Q[0].op("act", lambda e: e.activation(out=qT2[:, :, i * 128:(i + 1) * 128],
                                                      in_=pTv[:, 0:256].rearrange("p (c t) -> p c t", t=128), func=AF.Copy),
                        reads=[("bk", pTi)], writes=[("qT2", i)])
                Q[0].op("act", lambda e: e.activation(out=kT2[:, :, i * 128:(i + 1) * 128],
                                                      in_=pTv[:, 256:512].rearrange("p (c t) -> p c t", t=128), func=AF.Copy),
                        reads=[("bk", pTi)], writes=[("kT2", i)])
                Q[0].op("act", lambda e: e.activation(out=Vaug[:, i, :, 0:64],
                                                      in_=bk1[:, 0:256].rearrange("p (a b) -> p a b", b=64), func=AF.Copy),
                        reads=[("bk", b1)], writes=[("V", i)])
                Q[0].op("dve", lambda e: e.tensor_tensor(out=zt[:, i, :], in0=bk1[:, 256:260], in1=fbb[:], op=ALU.add),
                        reads=[("bk", b1), "fbb"], writes=[("zt", i)])

            run_pairs(fox_body)
            ZT = [("zt", i) for i in range(NT)]
            ztf = zt[:, :, :].rearrange("p a b -> p (a b)")
            Q[0].op("act", lambda e: e.activation(out=ztf, in_=ztf, func=AF.Exp, scale=-1.0), reads=ZT, writes=["z1"])
            Q[0].op("act", lambda e: e.activation(out=ztf, in_=ztf, func=AF.Ln, bias=EPS_T[:, 1:2]), reads=["z1", "eps1"], writes=["z2"])
            Q[0].op("dve", lambda e: e.tensor_scalar(out=ztf, in0=ztf, scalar1=-1.0, scalar2=None, op0=ALU.mult), reads=["z2"], writes=["logf"])
            Q[0].op("pe", lambda e: e.matmul(banks[0][:, 0:64], lhsT=cfv("ltri"), rhs=ztf, start=True, stop=True),
                 reads=["logf", "cft"], writes=[("bk", 0)])
            Q[0].op("pe", lambda e: e.matmul(banks[0][:, 64:128], lhsT=cfv("onesf"), rhs=ztf, start=True, stop=True),
                 reads=["logf", "cft"], writes=[("bk", 0)])
            totf = totb[:, :, :].rearrange("p a b -> p (a b)")
            Q[0].op("act", lambda e: e.activation(out=totf, in_=banks[0][:, 64:128], func=AF.Copy), reads=[("bk", 0)], writes=["totb"])
            Q[0].op("dve", lambda e: e.memset(off[:, 0, :], 0.0), writes=["off"])
            for j in range(1, NT):
                Q[0].op("dve", lambda e, j=j: e.tensor_tensor(out=off[:, j, :], in0=off[:, j - 1, :], in1=totb[:, j - 1, :], op=ALU.add),
                     reads=["off", "totb"], writes=["off"])
            ctf = ctm[:, :, :].rearrange("p a b -> p (a b)")
            Q[0].op("dve", lambda e: e.tensor_tensor(out=ctf, in0=banks[0][:, 0:64], in1=off[:, :, :].rearrange("p a b -> p (a b)"), op=ALU.add),
                 reads=[("bk", 0), "off"], writes=["ctm"])
            Q[0].op("pe", lambda e: e.matmul(banks[1][:, 0:64], lhsT=cfv("row64"), rhs=ctf, start=True, stop=True),
                 reads=["ctm", "cft"], writes=[("bk", 1)])
            Q[0].op("act", lambda e: e.activation(out=cmb[:, :, :].rearrange("p a b -> p (a b)"), in_=banks[1][:, 0:64], func=AF.Copy),
                 reads=[("bk", 1)], writes=["cmb"])
            for hh in range(4):
                for ks in range(NT):
                    Q[0].op("dve", lambda e, hh=hh, ks=ks: e.tensor_scalar(
                        out=Bt[:, hh, ks, :], in0=cmb[:, :, hh], scalar1=ctm[:, ks, hh:hh + 1], scalar2=None, op0=ALU.subtract),
                        reads=["cmb", "ctm"], writes=[("Bt", hh)])
            DLA = 3
            rdn = AR.view([128, 4, 128], F32)
            blocks = []
            for j in range(NT):
                for hh in range(4):
                    for ks in range(j + 1):
                        blocks.append((j, hh, ks, len(blocks) % 4))
            PTs = [PT[0][:, 0:128], PT[1][:, 0:128], PT[2][:, 0:128], PT[0][:, 256:384]]
            st_banks = [2, 3, 4, 1]

            def st_ap(s_):
                return banks[st_banks[s_]][:, 0:128]

            def acc_ap(g_):
                sl = g_ % 2
                return banks[5 + sl][:, 0:128], ("bk", 5 + sl)

            def emit_S(idx):
                j, hh, ks, s_ = blocks[idx]
                pr = (hh % 2) * 64
                pc = hh // 2
                sb_ = st_ap(s_)
                Q[0].op("pe", lambda e: e.matmul(sb_, lhsT=kT2[pr:pr + 64, pc, ks * 128:(ks + 1) * 128],
                                              rhs=qT2[pr:pr + 64, pc, j * 128:(j + 1) * 128], start=True, stop=(ks != j)),
                     reads=[("kT2", ks), ("qT2", j)], writes=[("bk", st_banks[s_])])
                if ks == j:
                    Q[0].op("pe", lambda e: e.matmul(sb_, lhsT=identb[:], rhs=cbv("caus"), start=False, stop=True),
                         reads=["identb", "cbt"], writes=[("bk", st_banks[s_])])
                Q[0].op("act", lambda e: e.activation(out=PTs[s_], in_=sb_, func=AF.Exp, scale=0.125, bias=Bt[:, hh, ks, j:j + 1]),
                     reads=[("bk", st_banks[s_]), ("Bt", hh)], writes=[("PT", s_)])

            def emit_PV(idx):
                j, hh, ks, s_ = blocks[idx]
                g_ = j * 4 + hh
                accb, acct = acc_ap(g_)
                buf = j % 2
                Q[0].op("pe", lambda e: e.matmul(accb, lhsT=Vaug[:, ks, hh, :], rhs=PTs[s_], start=(ks == 0), stop=(ks == j)),
                     reads=[("V", ks), "vones", ("PT", s_)], writes=[acct])
                if ks == j:
                    rs_ = g_ % 4
                    Q[0].op("act", lambda e: e.activation(out=rdn[0:64, rs_, :], in_=accb[64:128, :], func=AF.Copy),
                         reads=[acct], writes=[("rdn", rs_)])
                    Q[0].op("dve", lambda e: e.reciprocal(out=rdn[0:64, rs_, :], in_=rdn[0:64, rs_, :]),
                         reads=[("rdn", rs_)], writes=[("rdn", rs_)])
                    Q[0].op("dve", lambda e: e.tensor_tensor(out=OT[0:64, hh, buf, :], in0=accb[0:64, :], in1=rdn[0:64, rs_, :], op=ALU.mult),
                         reads=[acct, ("rdn", rs_)], writes=[("OT", buf, hh)])
                    if hh == 3:
                        wout_tile(j, buf, (0, 0))

            for idx in range(len(blocks) + DLA):
                if idx < len(blocks):
                    emit_S(idx)
                if idx >= DLA:
                    emit_PV(idx - DLA)

        def gmlp():
            AR.reset(base_off)
            uT = AR.view([128, 4, T], BF16)
            vgt = AR.view([128, NT, 256], BF16)
            wsn = AR.view([128, 4, 128], F32)
            wsb = AR.view([128, 4, 128], BF16)
            WT = AR.view([128, 4, 128], BF16)
            Gv = AR.view([128, 256], F32)
            bsf = AR.view([128, 512], F32)
            bsh = AR.view([128, 512], BF16)
            bsl = AR.view([128, 512], BF16)
            load_win(772, 512)
            load_wout(1)
            Q[0].dma("sp", Gv[:], W["gmlp_v_norm"][l].partition_broadcast(128), writes=["Gv"])
            Q[0].dma("sp", wsn[:], W["gmlp_w_s"][l].rearrange("g t s -> t g s"), writes=["wsn"])
            Q[0].dma("sp", bsf[0:1, :], W["gmlp_b_s"][l:l + 1].rearrange("o g t -> o (g t)"), writes=["bsf"])
            Q[0].op("dve", lambda e: e.tensor_copy(out=bsh[0:1, :], in_=bsf[0:1, :]), reads=["bsf"], writes=["bsh"])
            Q[0].op("dve", lambda e: e.tensor_tensor(out=bsl[0:1, :], in0=bsf[0:1, :], in1=bsh[0:1, :], op=ALU.subtract),
                 reads=["bsf", "bsh"], writes=["bsl"])
            Q[0].op("dve", lambda e: e.tensor_tensor(out=wsb[:], in0=wsn[:],
                                                  in1=cfv("tril").unsqueeze(1).to_broadcast([128, 4, 128]), op=ALU.mult),
                 reads=["wsn", "cft"], writes=["wsb"])
            for g in range(4):
                Q[0].op("pe", lambda e, g=g: e.transpose(pTb[:, g * 128:(g + 1) * 128], wsb[:, g, :], identb[:]),
                     reads=["wsb", "identb"], writes=[("bk", 7)])
            Q[0].op("act", lambda e: e.activation(out=WT[:], in_=pTb[:, 0:512].rearrange("p (g t) -> p g t", t=128), func=AF.Copy),
                 reads=[("bk", 7)], writes=["WT"])
            uc = 0
            for g in range(4):
                for tb in range(4):
                    r = uc % 2
                    uc += 1
                    bk = banks[2 + r]
                    for k in range(8):
                        Q[0].op("pe", lambda e, bk=bk, g=g, tb=tb, k=k: e.matmul(
                            bk[0:64, :], lhsT=wbuf[:, k, g * 64:(g + 1) * 64], rhs=hT[:, k, tb * 512:(tb + 1) * 512],
                            start=(k == 0), stop=(k == 7)), reads=["wbuf"] + HR, writes=[("bk", 2 + r)])
                    Q[0].op("act", lambda e, bk=bk, g=g, tb=tb: e.activation(out=uT[0:64, g, tb * 512:(tb + 1) * 512], in_=bk[0:64, :],
                                                                         func=AF.Gelu_apprx_tanh), reads=[("bk", 2 + r)], writes=[("uT", tb)])
            for i in range(NT):
                proj_tm(i, 1028, 256, banks[0], ("bk", 0), 256)
                Q[0].op("act", lambda e: e.activation(out=scrA[:, 0:256], in_=banks[0][:, 0:256], func=AF.Gelu_apprx_tanh),
                     reads=[("bk", 0)], writes=["scrA"])
                Q[0].op("act", lambda e: e.activation(out=scrB[:, 0:256], in_=scrA[:, 0:256], func=AF.Square),
                     reads=["scrA"], writes=["scrB"])
                Q[0].op("dve", lambda e: e.tensor_reduce(out=sm[:, 0:4], in_=scrB[:, 0:256].rearrange("p (a b) -> p a b", b=64),
                                                      axis=AX.X, op=ALU.add), reads=["scrB"], writes=["sm"])
                rstd_small(sm[:, 0:4], sm[:, 8:12], 4, "sm", "sm2")
                Q[0].op("dve", lambda e: e.tensor_tensor(out=scrB[:, 0:256].rearrange("p (a b) -> p a b", b=64),
                                                      in0=scrA[:, 0:256].rearrange("p (a b) -> p a b", b=64),
                                                      in1=sm[:, 8:12].unsqueeze(2).to_broadcast([128, 4, 64]), op=ALU.mult),
                     reads=["scrA", "sm2", "scrB"], writes=["scrB"])
                Q[0].op("dve", lambda e, i=i: e.tensor_tensor(out=vgt[:, i, :], in0=scrB[:, 0:256], in1=Gv[:], op=ALU.mult),
                     reads=["scrB", "Gv"], writes=[("vgt", i)])
                r = i % 2
                bk = banks[4 + r]
                for g in range(4):
                    Q[0].op("pe", lambda e, bk=bk, g=g, i=i: e.matmul(bk[0:64, g * 128:(g + 1) * 128], lhsT=vgt[:, i, g * 64:(g + 1) * 64],
                                                                  rhs=WT[:, g, :], start=True, stop=False),
                         reads=[("vgt", i), "WT"], writes=[("bk", 4 + r)])
                    Q[0].op("pe", lambda e, bk=bk, g=g: e.matmul(bk[0:64, g * 128:(g + 1) * 128], lhsT=cbv("ones")[0:1, 0:64],
                                                             rhs=bsh[0:1, g * 128:(g + 1) * 128], start=False, stop=False),
                         reads=["cbt", "bsh"], writes=[("bk", 4 + r)])
                    Q[0].op("pe", lambda e, bk=bk, g=g: e.matmul(bk[0:64, g * 128:(g + 1) * 128], lhsT=cbv("ones")[0:1, 0:64],
                                                             rhs=bsl[0:1, g * 128:(g + 1) * 128], start=False, stop=True),
                         reads=["cbt", "bsl"], writes=[("bk", 4 + r)])
                Q[0].op("dve", lambda e, bk=bk, i=i, r=r: e.tensor_tensor(
                    out=OT[0:64, :, r, :], in0=bk[0:64, :].rearrange("p (g t) -> p g t", t=128),
                    in1=uT[0:64, :, i * 128:(i + 1) * 128], op=ALU.mult),
                    reads=[("bk", 4 + r), ("uT", i // 4)], writes=[("OT", r)])
                wout_tile(i, r)

        def pool():
            AR.reset(base_off)
            zt = AR.view([128, NT, 256], BF16)
            wp = AR.view([128, 4, 64], BF16)
            psc = AR.view([128, 4], F32)
            pl = [AR.view([128, 4, 128], BF16) for _ in range(2)]
            load_win(1936, 256)
            load_wout(3)
            Q[0].dma("pool", wp[0:64, :, :], W["pool_w"][l].rearrange("g d e -> d g e"), writes=["wp"])
            Q[0].dma("sp", psc[0:64, :], W["pool_scale"][l].rearrange("(g e) -> e g", e=64), writes=["psc"],
                  allow_slow_non_contiguous=True)
            for i in range(NT):
                proj_tm(i, 1936, 256, banks[0], ("bk", 0), 0)
                Q[0].op("act", lambda e, i=i: e.activation(out=zt[:, i, :], in_=banks[0][:, 0:256], func=AF.Copy),
                     reads=[("bk", 0)], writes=[("zt", i)])
                r = i % 2
                bk = banks[2 + r]
                for g in range(4):
                    o_ = bk[0:64, g * 128:(g + 1) * 128]
                    lh = zt[:, i, g * 64:(g + 1) * 64]
                    if i == 0:
                        Q[0].op("pe", lambda e, o_=o_, lh=lh, g=g: e.matmul(o_, lhsT=lh, rhs=cbv("a0h", 512)[:, g * 128:(g + 1) * 128], start=True, stop=False),
                             reads=[("zt", i), "cbt"], writes=[("bk", 2 + r)])
                        Q[0].op("pe", lambda e, o_=o_, lh=lh, g=g: e.matmul(o_, lhsT=lh, rhs=cbv("a0l", 512)[:, g * 128:(g + 1) * 128], start=False, stop=True),
                             reads=[("zt", i), "cbt"], writes=[("bk", 2 + r)])
                    else:
                        lp = zt[:, i - 1, g * 64:(g + 1) * 64]
                        Q[0].op("pe", lambda e, o_=o_, lh=lh, g=g: e.matmul(o_, lhsT=lh, rhs=cbv("ad", 512)[:, g * 128:(g + 1) * 128], start=True, stop=False),
                             reads=[("zt", i), "cbt"], writes=[("bk", 2 + r)])
                        Q[0].op("pe", lambda e, o_=o_, lp=lp, g=g: e.matmul(o_, lhsT=lp, rhs=cbv("ap", 512)[:, g * 128:(g + 1) * 128], start=False, stop=True),
                             reads=[("zt", i - 1), "cbt"], writes=[("bk", 2 + r)])
                Q[0].op("act", lambda e, bk=bk, r=r: e.activation(out=pl[r][0:64, :, :], in_=bk[0:64, :].rearrange("p (g t) -> p g t", t=128), func=AF.Copy),
                     reads=[("bk", 2 + r)], writes=[("pl", r)])
                bk2 = banks[4 + r]
                for g in range(4):
                    Q[0].op("pe", lambda e, bk2=bk2, g=g, r=r: e.matmul(bk2[0:64, g * 128:(g + 1) * 128], lhsT=wp[0:64, g, :], rhs=pl[r][0:64, g, :],
                                                                   start=True, stop=True), reads=["wp", ("pl", r)], writes=[("bk", 4 + r)])
                for g in range(4):
                    Q[0].op("dve", lambda e, bk2=bk2, g=g, r=r: e.tensor_scalar(out=OT[0:64, g, r, :], in0=bk2[0:64, g * 128:(g + 1) * 128],
                                                                            scalar1=psc[0:64, g:g + 1], scalar2=None, op0=ALU.mult),
                         reads=[("bk", 4 + r), "psc"], writes=[("OT", r)])
                wout_tile(i, r)

        def nsa():
            AR.reset(base_off)
            qT4 = AR.view([128, 4, T], BF16)
            KV3 = AR.view([128, 3, T], BF16)
            vsa = AR.view([128, NT, 128], BF16)
            vwa = AR.view([128, NT, 128], BF16)
            ghl = AR.view([128, T], BF16)
            gsc = [AR.view([128, 512], F32) for _ in range(2)]
            gbias = AR.view([128, 4], F32)
            Gall = AR.view([128, 10, 64], F32)
            NBb = [AR.view([128, 640], BF16) for _ in range(2)]
            rts = [[AR.view([128, 4, 8], F32) for _ in range(4)] for _ in range(2)]
            Q[0].op("dve", lambda e: e.memset(wbuf[:, :, 652:704], 0.0), writes=["wbuf"])
            load_win(1284, 652)
            load_wout(2)
            Q[0].op("dve", lambda e: e.memset(Gall[:, :, :], 1.0), writes=["GallI"])
            for hh in range(4):
                Q[0].dma("sp", Gall[:, hh, :], W["nsa_q_norm"][l].partition_broadcast(128), reads=["GallI"], writes=[("Gall", hh)])
            Q[0].dma("sp", Gall[:, 6, :], W["nsa_ks_norm"][l].partition_broadcast(128), reads=["GallI"], writes=[("Gall", 6)])
            Q[0].dma("sp", Gall[:, 8, :], W["nsa_kw_norm"][l].partition_broadcast(128), reads=["GallI"], writes=[("Gall", 8)])
            GA = ["GallI"] + [("Gall", hh) for hh in (0, 1, 2, 3, 6, 8)]
            Q[0].op("dve", lambda e: e.memset(gbias[:, :], 0.0), writes=["gbias0"])
            Q[0].dma("sp", gbias[0:12, 0:1], W["nsa_gate_bias"][l].rearrange("(c o) -> c o", o=1), reads=["gbias0"], writes=["gbias"])
            Q[0].op("dve", lambda e: e.memset(vsa[:, :, 64:128], 1.0), writes=["vsones"])
            Q[0].op("dve", lambda e: e.memset(vwa[:, :, 64:128], 1.0), writes=["vwones"])
            rope = cfv("rope", 17 * 16).rearrange("p (s c) -> p s c", c=16)

            def rope_ops(view, nh, slot, wtag, p=0):
                x1 = view[:, :, 0:8]
                x2 = view[:, :, 8:16]
                cosb = rope[:, slot, 0:8].unsqueeze(1).to_broadcast([128, nh, 8])
                sinb = rope[:, slot, 8:16].unsqueeze(1).to_broadcast([128, nh, 8])
                ra, rb_, rc, rd = [t_[:, 0:nh, :] for t_ in rts[p]]
                Q[0].op("dve", lambda e: e.tensor_tensor(out=ra, in0=x1, in1=cosb, op=ALU.mult), reads=[wtag, "cft"], writes=[("ra", p)])
                Q[0].op("dve", lambda e: e.tensor_tensor(out=rb_, in0=x2, in1=sinb, op=ALU.mult), reads=[wtag, "cft"], writes=[("rb", p)])
                Q[0].op("dve", lambda e: e.tensor_tensor(out=rc, in0=x2, in1=cosb, op=ALU.mult), reads=[wtag, "cft"], writes=[("rc", p)])
                Q[0].op("dve", lambda e: e.tensor_tensor(out=rd, in0=x1, in1=sinb, op=ALU.mult), reads=[wtag, "cft"], writes=[("rd", p)])
                Q[0].op("dve", lambda e: e.tensor_tensor(out=x1, in0=ra, in1=rb_, op=ALU.subtract), reads=[("ra", p), ("rb", p), wtag], writes=[wtag])
                Q[0].op("dve", lambda e: e.tensor_tensor(out=x2, in0=rc, in1=rd, op=ALU.add), reads=[("rc", p), ("rd", p), wtag], writes=[wtag])

            def nsa_body(i):
                p = i % 2
                S = SCR[p]
                A, B, smp = S["A"], S["B"], S["sm"]
                b0, b1, pTi = S["b0"], S["b1"], S["pT"]
                bk0, bk1 = banks[b0], banks[b1]
                pTv = banks[pTi][:, :].bitcast(BF16)
                proj_tm(i, 1284, 512, bk0, ("bk", b0), 0)
                proj_tm(i, 1796, 140, bk1, ("bk", b1), 512)
                Q[0].op("act", lambda e: e.activation(out=A[:, 0:512], in_=bk0[:, :], func=AF.Copy), reads=[("bk", b0)], writes=[("sA", p)])
                Q[0].op("act", lambda e: e.activation(out=A[:, 512:640], in_=bk1[:, 0:128], func=AF.Copy), reads=[("bk", b1), ("sA", p)], writes=[("sA", p)])
                Q[0].op("act", lambda e: e.activation(out=B[:, 0:640], in_=A[:, 0:640], func=AF.Square), reads=[("sA", p)], writes=[("sB", p)])
                Q[0].op("dve", lambda e: e.tensor_reduce(out=smp[:, 0:10], in_=B[:, 0:640].rearrange("p (a b) -> p a b", b=64),
                                                         axis=AX.X, op=ALU.add), reads=[("sB", p)], writes=[("sm", p)])
                rstd_small(smp[:, 0:10], smp[:, 16:26], 10, ("sm", p), ("sm2", p))
                Q[0].op("dve", lambda e: e.tensor_tensor(out=B[:, 0:640].rearrange("p (a b) -> p a b", b=64),
                                                         in0=A[:, 0:640].rearrange("p (a b) -> p a b", b=64),
                                                         in1=smp[:, 16:26].unsqueeze(2).to_broadcast([128, 10, 64]), op=ALU.mult),
                        reads=[("sA", p), ("sm2", p), ("sB", p)], writes=[("sB", p)])
                Q[0].op("dve", lambda e: e.tensor_tensor(out=B[:, 0:640], in0=B[:, 0:640],
                                                         in1=Gall[:, :, :].rearrange("p a b -> p (a b)"), op=ALU.mult),
                        reads=[("sB", p)] + GA, writes=[("sB", p)])
                for (c0_, c1_) in ((256, 384), (448, 512), (576, 640)):
                    Q[0].op("act", lambda e, c0_=c0_, c1_=c1_: e.activation(out=B[:, c0_:c1_], in_=A[:, c0_:c1_], func=AF.Copy),
                            reads=[("sA", p), ("sB", p)], writes=[("sB", p)])
                v3 = B[:, 0:640].rearrange("p (a b) -> p a b", b=64)
                rope_ops(v3[:, 0:4, :], 4, i, ("sB", p), p)
                rope_ops(v3[:, 6:7, :], 1, i, ("sB", p), p)
                rope_ops(v3[:, 8:9, :], 1, i, ("sB", p), p)
                nb = NBb[p]
                Q[0].op("act", lambda e: e.activation(out=nb[:], in_=B[:, 0:640], func=AF.Copy), reads=[("sB", p)], writes=[("NBb", p)])
                for c in range(5):
                    Q[0].op("pe", lambda e, c=c: e.transpose(pTv[:, c * 128:(c + 1) * 128], nb[:, c * 128:(c + 1) * 128], identb[:]),
                            reads=[("NBb", p), "identb"], writes=[("bk", pTi)])
                for hh in range(4):
                    pr_ = (hh % 2) * 64
                    Q[0].op("act", lambda e, hh=hh, pr_=pr_: e.activation(out=qT4[0:64, hh, i * 128:(i + 1) * 128],
                                                                          in_=pTv[pr_:pr_ + 64, (hh // 2) * 128:(hh // 2 + 1) * 128], func=AF.Copy),
                            reads=[("bk", pTi)], writes=[("qT4", i, hh)])
                Q[0].op("act", lambda e: e.activation(out=KV3[:, :, i * 128:(i + 1) * 128],
                                                      in_=pTv[:, 256:640].rearrange("p (c t) -> p c t", t=128), func=AF.Copy),
                        reads=[("bk", pTi)], writes=[("KV3", i)])
                Q[0].op("dve", lambda e: e.tensor_copy(out=vsa[:, i, 0:64], in_=nb[:, 448:512]), reads=[("NBb", p)], writes=[("vsa", i)])
                Q[0].op("dve", lambda e: e.tensor_copy(out=vwa[:, i, 0:64], in_=nb[:, 576:640]), reads=[("NBb", p)], writes=[("vwa", i)])

            run_pairs(nsa_body)
            for tb in range(4):
                r = tb % 2
                bk = banks[5 + r]
                gsc_ = gsc[r]
                for k in range(8):
                    Q[0].op("pe", lambda e, bk=bk, tb=tb, k=k: e.matmul(bk[0:64, :], lhsT=wbuf[:, k, 640:704], rhs=hT[:, k, tb * 512:(tb + 1) * 512],
                                                                   start=(k == 0), stop=(k == 7)), reads=["wbuf"] + HR, writes=[("bk", 5 + r)])
                Q[0].op("act", lambda e, bk=bk, gsc_=gsc_: e.activation(out=gsc_[0:64, :], in_=bk[0:64, :], func=AF.Sigmoid,
                                                                    bias=gbias[0:64, 0:1]), reads=[("bk", 5 + r), "gbias", "gbias0"], writes=[("gsc", r)])
                Q[0].op("dve", lambda e, gsc_=gsc_, tb=tb: e.tensor_copy(out=ghl[0:32, tb * 512:(tb + 1) * 512], in_=gsc_[0:32, :]),
                        reads=[("gsc", r)], writes=[("ghi", tb)])
                Q[0].op("dve", lambda e, gsc_=gsc_, tb=tb: e.tensor_tensor(out=ghl[32:64, tb * 512:(tb + 1) * 512], in0=gsc_[0:32, :],
                                                                       in1=ghl[0:32, tb * 512:(tb + 1) * 512], op=ALU.subtract),
                        reads=[("gsc", r), ("ghi", tb)], writes=[("gT", tb)])
            P.fence(dummy[:])
            import os
            nlvl = int(os.environ.get("NSA_LEVEL", "9"))
            if nlvl == 0:
                return
            AR.reset(0)
            kcp = AR.view([128, 32, 128], BF16)
            W1 = AR.view([128, 32, 256], BF16)
            mcmp = AR.view([128, T], BF16)
            Et = AR.view([128, T], BF16)
            W2 = AR.view([128, 2, 2, 64], BF16)
            posn = AR.view([128, 128], F32)
            pos2T = AR.view([128, 32], F32)
            hg = AR.view([128, 4, 128], BF16)
            kcn = AR.view([128, 64], F32)
            kcnb = AR.view([128, 128], BF16)
            kcmpT = AR.view([128, 128], BF16)
            vca = AR.view([128, 128], BF16)
            gkc = AR.view([128, 64], F32)
            MT = [AR.view([128, 128], BF16) for _ in range(2)]
            impt = AR.view([128, 64], F32)
            m8 = AR.view([128, 16], F32)
            Mtm = AR.view([128, 128], BF16)
            scrD = AR.view([128, 512], F32)
            Usb_ = AR.view([128, 132], F32)
            assert AR.off <= post_limit, ("nsa post overflow", AR.off, post_limit)
            Q[0].op("dve", lambda e: e.memset(Mtm[:, :], 0.0), writes=["Mtm0"])
            Q[0].op("dve", lambda e: e.memset(posn[:, :], 0.0), writes=["posn0"])
            Q[0].dma("sp", mcmp[:], cb_d[:, CB["mcmp"]:CB["mcmp"] + T], writes=["mcmp"])
            Q[0].dma("sp", Et[0:32, :], cb_d[0:32, CB["E"]:CB["E"] + T], writes=["Et"])
            Q[0].dma("pool", W1[0:64, :, :], W["nsa_cmp_k_w1"][l].rearrange("(l d) j -> d l j", d=64), writes=["W1k"])
            Q[0].dma("pool", W1[64:128, :, :], W["nsa_cmp_v_w1"][l].rearrange("(l d) j -> d l j", d=64), writes=["W1v"])
            Q[0].dma("pool", W2[:, 0, :, :], W["nsa_cmp_k_w2"][l].rearrange("(c p) d -> p c d", p=128), writes=["W2k"])
            Q[0].dma("pool", W2[:, 1, :, :], W["nsa_cmp_v_w2"][l].rearrange("(c p) d -> p c d", p=128), writes=["W2v"])
            Q[0].dma("sp", posn[0:32, 0:64], W["nsa_cmp_pos_k"][l], reads=["posn0"], writes=["posnk"])
            Q[0].dma("sp", posn[0:32, 64:128], W["nsa_cmp_pos_v"][l], reads=["posn0"], writes=["posnv"])
            Q[0].dma("sp", gkc[:], W["nsa_kc_norm"][l].partition_broadcast(128), writes=["gkc"])
            if nlvl == 10:
                return
            Q[0].op("pe", lambda e: e.transpose(banks[0][:, 0:128], posn[:, :], cfv("identf")),
                 reads=["posnk", "posnv", "posn0", "cft"], writes=[("bk", 0)])
            Q[0].op("act", lambda e: e.activation(out=pos2T[:], in_=banks[0][:, 0:32], func=AF.Copy), reads=[("bk", 0)], writes=["pos2T"])
            if nlvl == 11:
                return
            kvv = KV3[:, 0, :].rearrange("p (n s) -> p n s", s=16)
            KVR = [("KV3", i) for i in range(NT)]
            Q[0].op("dve", lambda e: e.memset(kcp[:, :, :], 0.0), writes=["kcp0"])
            for ll in range(32):
                src = kvv[:, 0:127, ll] if ll < 16 else kvv[:, 1:128, ll - 16]
                Q[0].op("dve", lambda e, ll=ll, src=src: e.tensor_scalar(
                    out=kcp[:, ll, 0:127], in0=src, scalar1=pos2T[:, ll:ll + 1], scalar2=None, op0=ALU.add),
                    reads=KVR + ["pos2T", "kcp0"], writes=[("kcp", ll)])
            Q[0].op("dve", lambda e: e.memset(hg[:, :, :], 0.0), writes=["hg0"])
            if nlvl == 12:
                return
            nvar = os.environ.get("NSA_VAR", "")
            for kv in range(1 if nvar == "B" else 2):
                for jc in range(2):
                    reg = kv * 2 + jc
                    for ll in range(32):
                        Q[0].op("pe", lambda e, kv=kv, jc=jc, ll=ll, reg=reg: e.matmul(
                            banks[2 + kv][:, jc * 128:jc * 128 + 128], lhsT=W1[kv * 64:(kv + 1) * 64, ll, jc * 128:(jc + 1) * 128],
                            rhs=kcp[kv * 64:(kv + 1) * 64, ll, :], start=(ll == 0), stop=(ll == 31)),
                            reads=["W1k", "W1v", ("kcp", ll), "kcp0"], writes=[("bk", 2 + kv)])
            for kv in range(2):
                Q[0].op("act", lambda e, kv=kv: e.activation(out=hg[:, 2 * kv:2 * kv + 2, :], in_=banks[2 + kv][:, 0:256].rearrange("p (r n) -> p r n", n=128),
                                                         func=AF.Gelu_apprx_tanh), reads=[("bk", 2 + kv), "hg0"], writes=[("hg", kv)])
            if nlvl == 13:
                return
            for jc in range(2):
                Q[0].op("pe", lambda e, jc=jc: e.matmul(banks[4][:, 0:64], lhsT=hg[:, jc, :], rhs=W2[:, 0, jc, :],
                                                     start=(jc == 0), stop=(jc == 1)), reads=[("hg", 0), "W2k"], writes=[("bk", 4)])
            for jc in range(2):
                Q[0].op("pe", lambda e, jc=jc: e.matmul(banks[4][:, 64:128], lhsT=hg[:, 2 + jc, :], rhs=W2[:, 1, jc, :],
                                                     start=(jc == 0), stop=(jc == 1)), reads=[("hg", 1), "W2v"], writes=[("bk", 4)])
            Q[0].op("dve", lambda e: e.memset(vca[:, 0:64], 0.0), writes=["vca0"])
            Q[0].op("dve", lambda e: e.memset(vca[:, 64:128], 1.0), writes=["vca1"])
            Q[0].op("act", lambda e: e.activation(out=vca[:, 0:64], in_=banks[4][:, 64:128], func=AF.Copy),
                 reads=[("bk", 4), "vca0"], writes=["vca"])
            if nlvl == 15:
                return
            Q[0].op("dve", lambda e: e.memset(kcn[:], 0.0), writes=["kcn0"])
            Q[0].op("act", lambda e: e.activation(out=scrA[:, 0:64], in_=banks[4][:, 0:64], func=AF.Square, accum_out=sm[:, 0:1]),
                 reads=[("bk", 4)], writes=["scrA", "sm"])
            rstd_small(sm[:, 0:1], sm[:, 8:9], 1, "sm", "sm2")
            Q[0].op("dve", lambda e: e.scalar_tensor_tensor(out=kcn[:, :], in0=banks[4][:, 0:64], scalar=sm[:, 8:9], in1=gkc[:, :],
                                                         op0=ALU.mult, op1=ALU.mult), reads=[("bk", 4), "sm2", "gkc", "kcn0"], writes=["kcn"])
            if nlvl == 16:
                return
            rope_ops(kcn[:, :].rearrange("p (a b) -> p a b", b=64), 1, 16, "kcn")
            Q[0].op("dve", lambda e: e.memset(kcnb[:, 64:128], 0.0), writes=["kcnb0"])
            Q[0].op("act", lambda e: e.activation(out=kcnb[:, 0:64], in_=kcn[:], func=AF.Copy), reads=["kcn"], writes=["kcnb"])
            Q[0].op("pe", lambda e: e.transpose(pTb[:, 0:128], kcnb[:, :], identb[:]), reads=["kcnb", "kcnb0", "identb"], writes=[("bk", 7)])
            Q[0].op("act", lambda e: e.activation(out=kcmpT[0:64, :], in_=pTb[0:64, 0:128], func=AF.Copy), reads=[("bk", 7)], writes=["kcmpT"])

            selb2 = scrB1[:, 0:384].bitcast(BF16)
            PT3 = scrB1[:, 384:640].bitcast(BF16)
            scrCc = [scrC, scrA1]
            Usb = Usb_
            Q[0].op("dve", lambda e: e.tensor_copy(out=selb2[0:32, :], in_=cfv("sel", 768)[0:32, :]), reads=["cft"], writes=["selb2a"])
            Q[0].op("dve", lambda e: e.tensor_copy(out=selb2[32:64, :], in_=cfv("sel", 768)[0:32, :]), reads=["cft"], writes=["selb2b"])
            PTn = PT + [PT3]
            b0bf = banks[0][:, :].bitcast(BF16)
            GT = [("gT", tb_) for tb_ in range(4)]

            def gate_w(accb, acct, br, j):
                Q[0].op("act", lambda e: e.activation(out=scrA[0:64, 0:512], in_=accb[64:128, :], func=AF.Ln, bias=EPS_T[0:64, 3:4]),
                        reads=[acct, "eps1"], writes=["scrA"])
                Q[0].op("act", lambda e: e.activation(out=scrA[0:64, 0:512], in_=scrA[0:64, 0:512], func=AF.Exp, scale=-1.0),
                        reads=["scrA"], writes=["scrA"])
                for hh in range(4):
                    Q[0].op("pe", lambda e, hh=hh: e.matmul(banks[0][0:64, hh * 128:(hh + 1) * 128],
                                                            lhsT=selb2[0:64, (3 * hh + br) * 64:(3 * hh + br + 1) * 64],
                                                            rhs=ghl[0:64, j * 128:(j + 1) * 128], start=True, stop=True),
                            reads=["selb2a", "selb2b", ("gT", j // 4)], writes=[("bk", 0)])
                Q[0].op("dve", lambda e: e.tensor_tensor(out=scrB[0:64, 0:512], in0=banks[0][0:64, :], in1=scrA[0:64, 0:512], op=ALU.mult),
                        reads=["scrA", ("bk", 0)], writes=["gw"])

            seq = [("cmp", 0)]
            for j in range(NT):
                if j + 1 < NT:
                    seq.append(("cmp", j + 1))
                seq.append(("win", j))
                seq.append(("sel", j))
            blocks = []
            for kind, j in seq:
                if kind == "cmp":
                    blocks.append((kind, j, None, True, True))
                else:
                    k_lo = max(0, j - 4) if kind == "win" else 0
                    for ks in range(k_lo, j + 1):
                        blocks.append((kind, j, ks, ks == k_lo, ks == j))
            st_banks = [1, 2, 3, 4]
            DLA = 3
            acc_of = {"cmp": 7, "win": 5, "sel": 6}

            def emit_S(idx):
                kind, j, ks, first, last = blocks[idx]
                sl = idx % 4
                bi = st_banks[sl]
                sb_ = banks[bi]
                qrhs = qT4[0:64, :, j * 128:(j + 1) * 128]
                qreads = [("qT4", j, h_) for h_ in range(4)]
                if kind == "cmp":
                    Q[0].op("pe", lambda e: e.matmul(sb_[:, :], lhsT=kcmpT[0:64, :], rhs=qrhs, start=True, stop=False),
                            reads=["kcmpT"] + qreads, writes=[("bk", bi)])
                    Q[0].op("pe", lambda e: e.matmul(sb_[:, :], lhsT=identb[:],
                                                     rhs=mcmp[:, j * 128:(j + 1) * 128].unsqueeze(1).to_broadcast([128, 4, 128]),
                                                     start=False, stop=True), reads=["identb", "mcmp"], writes=[("bk", bi)])
                else:
                    kvi = 2 if kind == "win" else 1
                    extra = []
                    if ks == j:
                        extra.append("caus")
                    if kind == "win" and ks == j - 4:
                        extra.append("winup")
                    if kind == "sel" and j >= 8:
                        extra.append("sel")
                    Q[0].op("pe", lambda e: e.matmul(sb_[:, :], lhsT=KV3[0:64, kvi, ks * 128:(ks + 1) * 128], rhs=qrhs,
                                                     start=True, stop=(len(extra) == 0)),
                            reads=[("KV3", ks)] + qreads, writes=[("bk", bi)])
                    for xi, kind2 in enumerate(extra):
                        lastx = (xi == len(extra) - 1)
                        if kind2 == "sel":
                            Q[0].op("pe", lambda e, lastx=lastx: e.matmul(
                                sb_[:, :], lhsT=Et[0:32, ks * 128:(ks + 1) * 128],
                                rhs=MT[j % 2][0:32, :].unsqueeze(1).to_broadcast([32, 4, 128]), start=False, stop=lastx),
                                reads=["Et", ("MT", j % 2)], writes=[("bk", bi)])
                        else:
                            Q[0].op("pe", lambda e, kind2=kind2, lastx=lastx: e.matmul(
                                sb_[:, :], lhsT=identb[:], rhs=cbv(kind2).unsqueeze(1).to_broadcast([128, 4, 128]), start=False, stop=lastx),
                                reads=["identb", "cbt"], writes=[("bk", bi)])
                Q[0].op("act", lambda e: e.activation(out=PTn[sl][:], in_=sb_[:, :], func=AF.Exp, scale=0.125),
                        reads=[("bk", bi)], writes=[("PT", sl)])

            def emit_PV(idx):
                kind, j, ks, first, last = blocks[idx]
                sl = idx % 4
                ai = acc_of[kind]
                accb = banks[ai]
                acct = ("bk", ai)
                buf = j % 2
                cc = scrCc[j % 2]
                cct = ("scrCc", j % 2)
                if kind == "cmp":
                    Q[0].op("pe", lambda e: e.matmul(accb[:, :], lhsT=vca[:, :], rhs=PTn[sl][:], start=True, stop=True),
                            reads=["vca", "vca1", ("PT", sl)], writes=[acct])
                    if j >= 8:
                        for hh in range(4):
                            Q[0].op("pe", lambda e, hh=hh: e.matmul(banks[0][:, hh * 33:(hh + 1) * 33], lhsT=PTn[sl][:, hh * 128:(hh + 1) * 128],
                                                                    rhs=cbv("ov", 64)[:, 0:33], start=True, stop=True),
                                    reads=[("PT", sl), "cbt"], writes=[("bk", 0)])
                        Q[0].op("act", lambda e: e.activation(out=Usb[:, 0:132], in_=banks[0][:, 0:132], func=AF.Copy),
                                reads=[("bk", 0)], writes=["Usb"])
                    gate_w(accb, acct, 0, j)
                    Q[0].op("dve", lambda e: e.tensor_tensor(out=cc[0:64, 0:512], in0=accb[0:64, :], in1=scrB[0:64, 0:512], op=ALU.mult),
                            reads=[acct, "gw"], writes=[cct])
                    if j >= 8:
                        U = Usb[:, 0:132].rearrange("p (h c) -> p h c", c=33)
                        Q[0].op("dve", lambda e: e.tensor_scalar(out=sm[:, 32:36], in0=U[:, :, 32], scalar1=1e-30, scalar2=None, op0=ALU.max),
                                reads=["Usb"], writes=["rD"])
                        Q[0].op("dve", lambda e: e.reciprocal(out=sm[:, 32:36], in_=sm[:, 32:36]), reads=["rD"], writes=["rD"])
                        Q[0].op("dve", lambda e: e.tensor_copy(out=impt[:, 0:32], in_=cfv("addmask", 512)[:, j * 32:(j + 1) * 32]),
                                reads=["cft"], writes=["impt"])
                        for hh in range(4):
                            Q[0].op("dve", lambda e, hh=hh: e.scalar_tensor_tensor(out=impt[:, 0:32], in0=U[:, hh, 0:32], scalar=sm[:, 32 + hh:33 + hh],
                                                                                   in1=impt[:, 0:32], op0=ALU.mult, op1=ALU.add),
                                    reads=["Usb", "rD", "impt"], writes=["impt"])
                        Q[0].op("dve", lambda e: e.max(out=m8[:, 0:8], in_=impt[:, 0:32]), reads=["impt"], writes=["m8a"])
                        Q[0].op("dve", lambda e: e.match_replace(out=impt[:, 32:64], in_to_replace=m8[:, 0:8], in_values=impt[:, 0:32], imm_value=-3e38),
                                reads=["impt", "m8a"], writes=["impt2"])
                        Q[0].op("dve", lambda e: e.max(out=m8[:, 8:16], in_=impt[:, 32:64]), reads=["impt2"], writes=["m8b"])
                        Q[0].op("dve", lambda e: e.tensor_scalar(out=impt[:, 32:64], in0=impt[:, 0:32], scalar1=m8[:, 15:16], scalar2=None, op0=ALU.is_ge),
                                reads=["impt", "m8b", "impt2"], writes=["selm"])
                        Q[0].op("dve", lambda e: e.tensor_scalar(out=Mtm[:, 0:32], in0=impt[:, 32:64], scalar1=-1.0, scalar2=-NEG, op0=ALU.add, op1=ALU.mult),
                                reads=["selm", "Mtm0"], writes=["Mtm"])
                        Q[0].op("pe", lambda e: e.transpose(b0bf[:, 0:128], Mtm[:, :], identb[:]), reads=["Mtm", "Mtm0", "identb"], writes=[("bk", 0)])
                        Q[0].op("act", lambda e: e.activation(out=MT[j % 2][0:32, :], in_=b0bf[0:32, 0:128], func=AF.Copy),
                                reads=[("bk", 0)], writes=[("MT", j % 2)])
                    return
                va, vtag = (vwa, "vwa") if kind == "win" else (vsa, "vsa")
                Q[0].op("pe", lambda e: e.matmul(accb[:, :], lhsT=va[:, ks, :], rhs=PTn[sl][:], start=first, stop=last),
                        reads=[(vtag, ks), vtag[0:2] + "ones", ("PT", sl)], writes=[acct])
                if not last:
                    return
                gate_w(accb, acct, 2 if kind == "win" else 1, j)
                Q[0].op("dve", lambda e: e.tensor_tensor(out=scrD[0:64, 0:512], in0=accb[0:64, :], in1=scrB[0:64, 0:512], op=ALU.mult),
                        reads=[acct, "gw"], writes=["scrD"])
                if kind == "win":
                    Q[0].op("dve", lambda e: e.tensor_tensor(out=cc[0:64, 0:512], in0=cc[0:64, 0:512], in1=scrD[0:64, 0:512], op=ALU.add),
                            reads=[cct, "scrD"], writes=[cct])
                else:
                    Q[0].op("dve", lambda e: e.tensor_tensor(out=OT[0:64, :, buf, :],
                                                             in0=scrD[0:64, 0:512].rearrange("p (h t) -> p h t", t=128),
                                                             in1=cc[0:64, 0:512].rearrange("p (h t) -> p h t", t=128), op=ALU.add),
                            reads=["scrD", cct], writes=[("OT", buf)])
                    wout_tile(j, buf, (0, 0))

            if nlvl == 1:
                return
            for idx in range(len(blocks) + DLA):
                if idx < len(blocks):
                    emit_S(idx)
                if idx >= DLA:
                    emit_PV(idx - DLA)

        fns = {"fox": fox, "gmlp": gmlp, "pool": pool, "nsa": nsa}
        for sname in ("fox", "gmlp", "pool", "nsa"):
            if sname in seq_stages:
                fns[sname]()
                P.fence(dummy[:])
                yield sname

    P.op("dve", lambda e: e.memset(EPS_T[:, 1:2], 1.0), reads=["epst"], writes=["eps1"])
    P.op("dve", lambda e: e.memset(EPS_T[:, 2:3], 0.5), reads=["epst", "eps1"], writes=["eps1"])
    P.op("dve", lambda e: e.memset(EPS_T[:, 3:4], 1e-18), reads=["epst", "eps1"], writes=["eps1"])
    outs = []
    stage_idx = {"ffn1": 0, "fox": 1, "gmlp": 2, "pool": 3, "nsa": 4, "ffn2": 5}

    def dump(sname):
        if dbg:
            outs.append(P.dma("sp", dbg_d[stage_idx[sname]].rearrange("(i p) d -> p i d", p=128), X[:, :, :],
                              reads=[("X", i) for i in range(NT)], writes=[("dbgout", sname)], sem_key="dbg"))

    for s in range(nseq):
        for h2 in range(2):
            P.dma("sp", X[:, h2 * 8:(h2 + 1) * 8, :], x_d[s, h2 * 1024:(h2 + 1) * 1024, :].rearrange("(i p) d -> p i d", p=128),
                  writes=[("X", i) for i in range(h2 * 8, (h2 + 1) * 8)], sem_key=("xin", h2))
        for l in range(nlayers):
            if "ffn1" in stages:
                ffn(l, 1)
                P.fence(dummy[:])
                dump("ffn1")
            for sname in mixer(l, stages):
                dump(sname)
            if "ffn2" in stages:
                ffn(l, 2)
                P.fence(dummy[:])
                dump("ffn2")
        for h2 in range(2):
            outs.append(P.dma("sp", y_d[s, h2 * 1024:(h2 + 1) * 1024, :].rearrange("(i p) d -> p i d", p=128), X[:, h2 * 8:(h2 + 1) * 8, :],
                              reads=[("X", i) for i in range(h2 * 8, (h2 + 1) * 8)], writes=[("yout", s, h2)], sem_key=("yout", h2)))
    P.emit(final_waits=outs)
    st.close()
    return nc, P


_CACHE = {}


def kernel(**inputs):
    n_cores = 8
    x = np.ascontiguousarray(np.asarray(inputs["x"], dtype=np.float32))
    nseq = x.shape[0] // n_cores
    if "nc" not in _CACHE:
        _CACHE["nc"] = build(nseq, 2)[0]
        _CACHE["consts"] = make_consts()
    nc = _CACHE["nc"]
    cb, cf = _CACHE["consts"]
    params = {k: np.ascontiguousarray(np.asarray(inputs[k], dtype=np.float32)) for k in PARAM_SHAPES}
    in_maps = []
    for c in range(n_cores):
        m = {"x": x[c * nseq:(c + 1) * nseq], "cb": cb, "cf": cf}
        m.update(params)
        in_maps.append(m)
    res = run_bass_kernel_spmd(nc, in_maps, core_ids=list(range(n_cores)))
    return np.concatenate([np.asarray(r["y"]) for r in res.results], axis=0).astype(np.float32)
```

```python
import contextlib
import os
import numpy as np
import ml_dtypes
import concourse.bass as bass
import concourse.mybir as mybir
from concourse.bass_utils import run_bass_kernel_spmd

F32 = mybir.dt.float32
BF16 = mybir.dt.bfloat16
AF = mybir.ActivationFunctionType
ALU = mybir.AluOpType
AX = mybir.AxisListType

T = 2048
D = 1024
DFF = 2816
NIN = 2192
NT = 16
EPS = 1e-6
NEG = -30000.0
ROPE_THETA = 500000.0


class Prog:
    ENG = ("pe", "act", "dve", "pool", "sp")
    EPOCH = 8000

    def __init__(self, nc):
        self.nc = nc
        self.ops = []
        self.last_w = {}
        self.readers = {}
        self.fence_op = None

    def op(self, eng, fn, reads=(), writes=(), dma=False, sem_key=None, extra_deps=()):
        i = len(self.ops)
        deps = set(extra_deps)
        if self.fence_op is not None:
            deps.add(self.fence_op)
        for t in reads:
            w = self.last_w.get(t)
            if w is not None:
                deps.add(w)
        for t in writes:
            w = self.last_w.get(t)
            if w is not None:
                deps.add(w)
            for r in self.readers.get(t, ()):
                deps.add(r)
        for t in reads:
            self.readers.setdefault(t, []).append(i)
        for t in writes:
            self.last_w[t] = i
            self.readers[t] = []
        if dma and sem_key is None:
            sem_key = ("dma",) + tuple(writes)
        self.ops.append(dict(eng=eng, fn=fn, deps=deps, dma=dma, sem_key=sem_key, flag=False))
        return i

    def dma(self, eng, out, in_, reads=(), writes=(), sem_key=None, **kw):
        return self.op(eng, lambda e: e.dma_start(out=out, in_=in_, **kw), reads, writes, dma=True, sem_key=sem_key)

    def fence(self, dummy):
        ops = self.ops
        outstanding = set(self.last_w.values())
        for rs in self.readers.values():
            outstanding.update(rs)
        if self.fence_op is not None:
            outstanding.add(self.fence_op)
        best = {}
        deps = set()
        for d in outstanding:
            o = ops[d]
            if o["dma"]:
                k = ("D", o["sem_key"])
            else:
                k = ("E", o["eng"])
            if k not in best or best[k] < d:
                best[k] = d
        deps = set(best.values())
        self.last_w = {}
        self.readers = {}
        self.fence_op = None
        i = self.op("dve", lambda e: e.memset(dummy, 0.0), extra_deps=deps)
        self.fence_op = i
        return i

    def emit(self, final_waits=()):
        nc = self.nc
        ops = self.ops
        for o in ops:
            if o["eng"] == "pe" and not o["dma"]:
                o["deps"] = {d for d in o["deps"] if not (ops[d]["eng"] == "pe" and not ops[d]["dma"])}
            for d in o["deps"]:
                ops[d]["flag"] = True
        for i in final_waits:
            ops[i]["flag"] = True
        for o in ops:
            if o["dma"]:
                o["flag"] = True
        sem_names = []
        seen = set()
        cnt = {}
        for i, o in enumerate(ops):
            if not o["flag"]:
                continue
            if o["dma"]:
                key = ("D", o["sem_key"])
                cnt[key] = cnt.get(key, 0) + 16
                o["sig"] = (key, cnt[key])
            else:
                ep = cnt.get(("ep", o["eng"]), 0)
                key = ("E", o["eng"], ep)
                cnt[key] = cnt.get(key, 0) + 1
                o["sig"] = (key, cnt[key])
                if cnt[key] >= self.EPOCH:
                    cnt[("ep", o["eng"])] = ep + 1
            if o["sig"][0] not in seen:
                seen.add(o["sig"][0])
                sem_names.append(o["sig"][0])
        self.n_sems = len(sem_names)
        with contextlib.ExitStack() as st:
            sems = {k: st.enter_context(nc.semaphore("s%d" % n)) for n, k in enumerate(sem_names)}
            block = st.enter_context(nc.Block())
            for en in self.ENG:
                mine = [(i, o) for i, o in enumerate(ops) if o["eng"] == en]
                fin = list(final_waits) if en == "sp" else []

                def body(e, mine=mine, fin=fin):
                    waited = {}

                    def wait_for(d):
                        key, val = ops[d]["sig"]
                        if waited.get(key, 0) >= val:
                            return
                        e.wait_ge(sems[key], val)
                        waited[key] = val

                    for i, o in mine:
                        for d in sorted(o["deps"]):
                            wait_for(d)
                        ins = o["fn"](e)
                        if o["flag"]:
                            key, val = o["sig"]
                            ins.then_inc(sems[key], 16 if o["dma"] else 1)
                    for d in fin:
                        wait_for(d)

                dec = {"pe": block.tensor, "act": block.scalar, "dve": block.vector,
                       "pool": block.gpsimd, "sp": block.sync}[en]
                dec(body)
        return self


class Rec:
    def __init__(self):
        self.items = []

    def op(self, eng, fn, reads=(), writes=(), **kw):
        self.items.append((eng, fn, list(reads), list(writes), kw))

    def dma(self, eng, out, in_, reads=(), writes=(), sem_key=None, **kw):
        self.items.append((eng, lambda e: e.dma_start(out=out, in_=in_, **kw), list(reads), list(writes),
                           dict(dma=True, sem_key=sem_key)))


def interleave(P, recs):
    n = max(len(r.items) for r in recs)
    for k in range(n):
        for r in recs:
            if k < len(r.items):
                eng, fn, rd, wr, kw = r.items[k]
                P.op(eng, fn, rd, wr, **kw)


CB = {}
CF = {}


def _alloc(tab, name, n):
    off = tab.get("_n", 0)
    tab[name] = off
    tab["_n"] = off + n
    return off


for _n, _w in [("ident", 128), ("caus", 128), ("winup", 128), ("ov", 64), ("ones", 128),
               ("ad", 512), ("ap", 512), ("a0h", 512), ("a0l", 512), ("mcmp", 2048), ("E", 2048)]:
    _alloc(CB, _n, _w)
for _n, _w in [("identf", 128), ("tril", 128), ("ltri", 128), ("onesf", 128), ("row64", 128),
               ("addmask", 512), ("rope", 17 * 16), ("sel", 12 * 64)]:
    _alloc(CF, _n, _w)
NCB = CB["_n"]
NCF = CF["_n"]


def make_consts():
    cb = np.zeros((128, NCB), np.float32)
    cf = np.zeros((128, NCF), np.float32)
    p = np.arange(128)[:, None]
    q = np.arange(128)[None, :]
    cb[:, CB["ident"]:CB["ident"] + 128] = (p == q)
    cb[:, CB["caus"]:CB["caus"] + 128] = np.where(p <= q, 0.0, NEG)
    cb[:, CB["winup"]:CB["winup"] + 128] = np.where(p > q, 0.0, NEG)
    ncmp = 127
    cs = np.arange(ncmp) * 16
    ce = cs + 32
    ss = np.arange(32) * 64
    se = ss + 64
    ov = np.clip(np.minimum(ce[:, None], se[None, :]) - np.maximum(cs[:, None], ss[None, :]), 0, None) / 32.0
    cb[0:127, CB["ov"]:CB["ov"] + 32] = ov
    cb[0:127, CB["ov"] + 32] = 1.0
    cb[:, CB["ones"]:CB["ones"] + 128] = 1.0
    sizes = (2, 4, 8, 16)
    for g, wn in enumerate(sizes):
        ad = np.zeros((128, 128)); apv = np.zeros((128, 128)); a0 = np.zeros((128, 128))
        for t in range(128):
            for s in range(t - wn + 1, t + 1):
                if s >= 0:
                    ad[s, t] += 1.0 / wn
                else:
                    apv[128 + s, t] += 1.0 / wn
            ad[t, t] -= 1.0
            cntv = min(t + 1, wn)
            for s in range(max(0, t - wn + 1), t + 1):
                a0[s, t] += 1.0 / cntv
            a0[t, t] -= 1.0
        a0h = a0.astype(np.float32).astype(ml_dtypes.bfloat16).astype(np.float32)
        a0l = (a0 - a0h)
        cb[:, CB["ad"] + g * 128:CB["ad"] + (g + 1) * 128] = ad
        cb[:, CB["ap"] + g * 128:CB["ap"] + (g + 1) * 128] = apv
        cb[:, CB["a0h"] + g * 128:CB["a0h"] + (g + 1) * 128] = a0h
        cb[:, CB["a0l"] + g * 128:CB["a0l"] + (g + 1) * 128] = a0l
    tt = np.arange(T)[None, :]
    nn = np.arange(128)[:, None]
    mc = np.where((16 * nn + 31 <= tt) & (nn < 127), 0.0, NEG)
    cb[:, CB["mcmp"]:CB["mcmp"] + T] = mc
    jj = np.arange(32)[:, None]
    cb[0:32, CB["E"]:CB["E"] + T] = ((tt // 64) == jj)

    cf[:, CF["identf"]:CF["identf"] + 128] = (p == q)
    cf[:, CF["tril"]:CF["tril"] + 128] = (q <= p)
    cf[:, CF["ltri"]:CF["ltri"] + 128] = (p <= q)
    cf[:, CF["onesf"]:CF["onesf"] + 128] = 1.0
    cf[64, CF["row64"]:CF["row64"] + 128] = 1.0
    tpos = (np.arange(NT)[None, :] * 128 + np.arange(128)[:, None])
    cur = tpos // 64
    blk = np.arange(32)[None, None, :]
    forced = ((blk == 0) | (blk == cur[:, :, None]) | (blk == cur[:, :, None] - 1)).astype(np.float32)
    am = np.where(blk <= cur[:, :, None], 1000.0 * forced, -1e30).astype(np.float32)
    cf[:, CF["addmask"]:CF["addmask"] + 512] = am.reshape(128, 512)
    inv = (np.float32(ROPE_THETA) ** (-np.arange(8, dtype=np.float32) * np.float32(2.0) / np.float32(16))).astype(np.float32)
    rp = np.zeros((128, 17, 16), np.float32)
    for sl in range(17):
        pos = (tpos[:, sl] if sl < 16 else (np.arange(128) * 16 + 31)).astype(np.float32)
        ang = (pos[:, None] * inv[None, :]).astype(np.float32)
        rp[:, sl, 0:8] = np.cos(ang.astype(np.float64))
        rp[:, sl, 8:16] = np.sin(ang.astype(np.float64))
    cf[:, CF["rope"]:CF["rope"] + 17 * 16] = rp.reshape(128, -1)
    sel = np.zeros((128, 12, 64), np.float32)
    for k in range(12):
        sel[k, k, :] = 1.0
    cf[:, CF["sel"]:CF["sel"] + 768] = sel.reshape(128, -1)
    return cb.astype(ml_dtypes.bfloat16), cf.astype(np.float32)


PARAM_SHAPES = {
    'ffn1_norm': (2, 1024), 'ffn1_w1': (2, 1024, 2816), 'ffn1_w3': (2, 1024, 2816), 'ffn1_w2': (2, 2816, 1024),
    'mix_norm': (2, 1024), 'w_in': (2, 1024, 2192), 'w_out': (2, 1024, 1024),
    'fox_f_bias': (2, 4), 'fox_q_norm': (2, 64), 'fox_k_norm': (2, 64),
    'gmlp_v_norm': (2, 256), 'gmlp_w_s': (2, 4, 128, 128), 'gmlp_b_s': (2, 4, 128),
    'nsa_q_norm': (2, 64), 'nsa_kc_norm': (2, 64), 'nsa_ks_norm': (2, 64), 'nsa_kw_norm': (2, 64),
    'nsa_cmp_pos_k': (2, 32, 64), 'nsa_cmp_k_w1': (2, 2048, 256), 'nsa_cmp_k_w2': (2, 256, 64),
    'nsa_cmp_pos_v': (2, 32, 64), 'nsa_cmp_v_w1': (2, 2048, 256), 'nsa_cmp_v_w2': (2, 256, 64),
    'nsa_gate_bias': (2, 12), 'pool_w': (2, 4, 64, 64), 'pool_scale': (2, 256),
    'ffn2_norm': (2, 1024), 'ffn2_w1': (2, 1024, 2816), 'ffn2_w3': (2, 1024, 2816), 'ffn2_w2': (2, 2816, 1024),
}

ARENA_BYTES = 134 * 1024


def build(nseq, nlayers, dbg=False, stages=("ffn1", "fox", "gmlp", "pool", "nsa", "ffn2")):
    nc = bass.Bass("TRN2", target_bir_lowering=False)
    x_d = nc.dram_tensor("x", [nseq, T, D], F32, kind="ExternalInput").ap()
    y_d = nc.dram_tensor("y", [nseq, T, D], F32, kind="ExternalOutput").ap()
    dbg_d = nc.dram_tensor("dbg", [6, T, D], F32, kind="ExternalOutput").ap() if dbg else None
    W = {k: nc.dram_tensor(k, list(s), F32, kind="ExternalInput").ap() for k, s in PARAM_SHAPES.items()}
    cb_d = nc.dram_tensor("cb", [128, NCB], BF16, kind="ExternalInput").ap()
    cf_d = nc.dram_tensor("cf", [128, NCF], F32, kind="ExternalInput").ap()

    P = Prog(nc)
    st = contextlib.ExitStack()

    def sb(name, shape, dt):
        return st.enter_context(nc.sbuf_tensor(name, shape, dt))

    X = sb("X", [128, NT, D], F32)
    gb = sb("gb", [128, D], F32)
    arena = sb("arena", [128, ARENA_BYTES // 4], F32)
    identb = sb("identb", [128, 128], BF16)
    ssq = sb("ssq", [128, 16], F32)
    rstd = sb("rstd", [128, 16], F32)
    hb = [sb("hb%d" % i, [128, D], BF16) for i in range(2)]
    dummy = sb("fdummy", [128, 8], F32)
    banks = [st.enter_context(nc.psum_tensor("bank%d" % i, [128, 512], F32)) for i in range(8)]
    pTb = banks[7][:, :].bitcast(BF16)
    junk = banks[6][:, :].bitcast(BF16)

    class Arena:
        def __init__(self):
            self.off = 0

        def reset(self, off=0):
            self.off = off

        def view(self, shape, dt, parts=128):
            esz = 4 if dt == F32 else 2
            n = int(np.prod(shape[1:]))
            nbytes = (n * esz + 3) // 4 * 4
            a = arena[:, self.off // 4:(self.off + nbytes) // 4]
            if dt != F32:
                a = a.bitcast(dt)
            a = a[:, 0:n]
            self.off += nbytes
            assert self.off <= ARENA_BYTES, ("arena overflow", self.off)
            if len(shape) == 3:
                a = a.rearrange("p (a b) -> p a b", b=shape[2])
            elif len(shape) == 4:
                a = a.rearrange("p (a b c) -> p a b c", b=shape[2], c=shape[3])
            return a

    AR = Arena()

    P.dma("sp", identb[:], cb_d[:, CB["ident"]:CB["ident"] + 128], writes=["identb"])

    def norm_T(gain_ap, hT):
        P.dma("sp", gb[:], gain_ap.partition_broadcast(128), writes=["gb"])
        for i in range(NT):
            P.op("act", lambda e, i=i: e.activation(out=hb[i % 2][:], in_=X[:, i, :], func=AF.Square,
                                                   accum_out=ssq[:, i:i + 1]),
                 reads=[("X", i)], writes=[("hb", i % 2), ("ssq", i)])
        P.op("act", lambda e: e.activation(out=rstd[:], in_=ssq[:], func=AF.Sqrt, scale=1.0 / D, bias=EPS_T[:, 0:1]),
             reads=[("ssq", i) for i in range(NT)] + ["epst"], writes=["rstd_s"])
        P.op("dve", lambda e: e.reciprocal(out=rstd[:], in_=rstd[:]), reads=["rstd_s"], writes=["rstd"])
        for i in range(NT):
            b = i % 2
            P.op("dve", lambda e, i=i, b=b: e.scalar_tensor_tensor(out=hb[b][:], in0=X[:, i, :], scalar=rstd[:, i:i + 1],
                                                                in1=gb[:], op0=ALU.mult, op1=ALU.mult),
                 reads=[("X", i), "rstd", "gb"], writes=[("hb", b)])
            for c in range(8):
                P.op("pe", lambda e, b=b, c=c: e.transpose(pTb[:, c * 128:(c + 1) * 128], hb[b][:, c * 128:(c + 1) * 128], identb[:]),
                     reads=[("hb", b), "identb"], writes=[("bk", 7)])
            P.op("act", lambda e, i=i: e.activation(out=hT[:, :, i * 128:(i + 1) * 128],
                                                   in_=pTb[:, :].rearrange("p (c t) -> p c t", t=128), func=AF.Copy),
                 reads=[("bk", 7)], writes=[("hT", i)])

    EPS_T = sb("epst", [128, 4], F32)
    P.op("dve", lambda e: e.memset(EPS_T[:], EPS), writes=["epst"])

    def ffn(l, which):
        pre = "ffn%d_" % which
        w1_d = W[pre + "w1"][l].rearrange("(k p) f -> p k f", p=128)
        w3_d = W[pre + "w3"][l].rearrange("(k p) f -> p k f", p=128)
        w2_d = W[pre + "w2"][l]
        AR.reset()
        hT = AR.view([128, 8, T], BF16)
        gT = AR.view([128, 6, T], BF16)
        w2b = [AR.view([128, 6, D], BF16) for _ in range(2)]
        w13 = [AR.view([128, 2, 8, 256], BF16) for _ in range(3)]
        sil = [AR.view([128, 512], F32) for _ in range(2)]
        norm_T(W[pre + "norm"][l], hT)
        import os
        lvl = int(os.environ.get("FFN_LEVEL", "2"))
        groups = [(0, 6), (6, 6), (12, 5), (17, 5)]
        if lvl == 0:
            groups = []
        cnt = 0
        oc = 0
        uc = 0
        for g, (c0, n) in enumerate(groups):
            wb = w2b[g % 2]
            P.dma("pool", wb[:, 0:n, :], w2_d[c0 * 128:(c0 + n) * 128, :].rearrange("(c p) f -> p c f", p=128),
                  writes=[("w2b", g % 2)])
            units = [(c0 + u, min(2, n - u)) for u in range(0, n, 2)]
            for (j0, nj) in units:
                slot = uc % 3
                uc += 1
                ws = w13[slot]
                P.dma("pool", ws[:, 0, :, 0:nj * 128], w1_d[:, :, j0 * 128:(j0 + nj) * 128], writes=[("w13a", slot)])
                P.dma("pool", ws[:, 1, :, 0:nj * 128], w3_d[:, :, j0 * 128:(j0 + nj) * 128], writes=[("w13b", slot)])
                for tb in range(4):
                    hreads = [("hT", 4 * tb + qq) for qq in range(4)]
                    for jj in range(nj):
                        jl = j0 + jj - c0
                        r = cnt % 2
                        cnt += 1
                        pa = banks[r]
                        pb = banks[2 + r]
                        for k in range(8):
                            P.op("pe", lambda e, pa=pa, ws=ws, k=k, jj=jj, tb=tb: e.matmul(
                                pa[:, :], lhsT=ws[:, 0, k, jj * 128:(jj + 1) * 128], rhs=hT[:, k, tb * 512:(tb + 1) * 512],
                                start=(k == 0), stop=(k == 7)), reads=[("w13a", slot)] + hreads, writes=[("pa", r)])
                        for k in range(8):
                            P.op("pe", lambda e, pb=pb, ws=ws, k=k, jj=jj, tb=tb: e.matmul(
                                pb[:, :], lhsT=ws[:, 1, k, jj * 128:(jj + 1) * 128], rhs=hT[:, k, tb * 512:(tb + 1) * 512],
                                start=(k == 0), stop=(k == 7)), reads=[("w13b", slot)] + hreads, writes=[("pb", r)])
                        P.op("act", lambda e, pa=pa, r=r: e.activation(out=sil[r][:], in_=pa[:, :], func=AF.Silu),
                             reads=[("pa", r)], writes=[("sil", r)])
                        P.op("dve", lambda e, pb=pb, r=r, jl=jl, tb=tb: e.tensor_tensor(
                            out=gT[:, jl, tb * 512:(tb + 1) * 512], in0=pb[:, :], in1=sil[r][:], op=ALU.mult),
                            reads=[("pb", r), ("sil", r)], writes=[("gT", jl, tb)])
            for i in range(NT if lvl >= 2 else 0):
                for half in range(2):
                    r = oc % 2
                    oc += 1
                    po = banks[4 + r]
                    for jl in range(n):
                        P.op("pe", lambda e, po=po, jl=jl, i=i, half=half, wb=wb: e.matmul(
                            po[:, :], lhsT=gT[:, jl, i * 128:(i + 1) * 128], rhs=wb[:, jl, half * 512:(half + 1) * 512],
                            start=(jl == 0), stop=(jl == n - 1)),
                            reads=[("gT", jl, i // 4), ("w2b", g % 2)], writes=[("po", r)])
                    if os.environ.get("STT_ALT", "0") == "1":
                        P.op("dve", lambda e, po=po, i=i, half=half: e.tensor_tensor(
                            out=X[:, i, half * 512:(half + 1) * 512], in0=po[:, :],
                            in1=X[:, i, half * 512:(half + 1) * 512], op=ALU.add),
                            reads=[("po", r), ("X", i)], writes=[("X", i)])
                    else:
                        P.op("dve", lambda e, po=po, i=i, half=half: e.scalar_tensor_tensor(
                            out=X[:, i, half * 512:(half + 1) * 512], in0=po[:, :], scalar=EPS_T[:, 2:3],
                            in1=X[:, i, half * 512:(half + 1) * 512], op0=ALU.mult, op1=ALU.add),
                            reads=[("po", r), ("X", i), "eps1"], writes=[("X", i)])

    def mixer(l, seq_stages):
        AR.reset()
        hT = AR.view([128, 8, T], BF16)
        wbuf = AR.view([128, 8, 772], BF16)
        post_limit = AR.off
        cbt = AR.view([128, CB["mcmp"]], BF16)
        cft = AR.view([128, NCF], F32)
        OT = AR.view([128, 4, 2, 128], BF16)
        wout = AR.view([128, 4, D], BF16)
        scrA = AR.view([128, 640], F32)
        scrB = AR.view([128, 640], F32)
        scrC = AR.view([128, 640], F32)
        sm = AR.view([128, 64], F32)
        scrA1 = AR.view([128, 640], F32)
        scrB1 = AR.view([128, 640], F32)
        sm1 = AR.view([128, 64], F32)
        SCR = [dict(A=scrA, B=scrB, sm=sm, b0=0, b1=1, pT=7), dict(A=scrA1, B=scrB1, sm=sm1, b0=2, b1=3, pT=4)]
        Q = [P]

        def run_pairs(body, n=NT):
            for m0 in range(0, n, 2):
                recs = []
                for i_ in (m0, m0 + 1):
                    r_ = Rec()
                    Q[0] = r_
                    body(i_)
                    Q[0] = P
                    recs.append(r_)
                interleave(P, recs)
        PT = [AR.view([128, 512], BF16) for _ in range(3)]
        base_off = AR.off

        def cbv(name, n=128):
            return cbt[:, CB[name]:CB[name] + n]

        def cfv(name, n=128):
            return cft[:, CF[name]:CF[name] + n]

        Q[0].dma("sp", cbt[:], cb_d[:, 0:CB["mcmp"]], writes=["cbt"])
        Q[0].dma("sp", cft[:], cf_d[:, :], writes=["cft"])
        norm_T(W["mix_norm"][l], hT)
        w_in = W["w_in"][l].rearrange("(k p) f -> p k f", p=128)
        w_out = W["w_out"][l]
        HR = [("hT", i) for i in range(NT)]

        def load_win(c0, n):
            Q[0].dma("pool", wbuf[:, :, 0:n], w_in[:, :, c0:c0 + n], writes=["wbuf"])

        def load_wout(m):
            Q[0].dma("pool", wout[0:64, :, :], w_out[m * 256:(m + 1) * 256, :].rearrange("(c p) f -> p c f", p=64),
                  writes=["wout"])

        def proj_tm(i, col0, ncols, bank, btag, c_in_buf=0):
            for k in range(8):
                Q[0].op("pe", lambda e, k=k: e.matmul(bank[:, 0:ncols], lhsT=hT[:, k, i * 128:(i + 1) * 128],
                                                   rhs=wbuf[:, k, c_in_buf:c_in_buf + ncols], start=(k == 0), stop=(k == 7)),
                     reads=[("hT", i), "wbuf"], writes=[btag])

        def wout_tile(i, buf, bsel=(0, 1)):
            for half in range(2):
                bk = banks[bsel[half]]
                for c in range(4):
                    Q[0].op("pe", lambda e, c=c, half=half, bk=bk: e.matmul(
                        bk[:, :], lhsT=OT[0:64, c, buf, :], rhs=wout[0:64, c, half * 512:(half + 1) * 512],
                        start=(c == 0), stop=(c == 3)), reads=[("OT", buf), ("OT", buf, c), "wout"], writes=[("bk", bsel[half])])
                Q[0].op("dve", lambda e, half=half, bk=bk: e.tensor_tensor(
                    out=X[:, i, half * 512:(half + 1) * 512], in0=bk[:, :], in1=X[:, i, half * 512:(half + 1) * 512],
                    op=ALU.add), reads=[("bk", bsel[half]), ("X", i)], writes=[("X", i)])

        def rstd_small(src_ap, dst_ap, n, tagr, tagw):
            npart = dst_ap.shape[0]
            Q[0].op("act", lambda e: e.activation(out=dst_ap, in_=src_ap, func=AF.Sqrt, scale=1.0 / 64, bias=EPS_T[0:npart, 0:1]),
                 reads=[tagr, "epst"], writes=[(tagw, "_s")])
            Q[0].op("dve", lambda e: e.reciprocal(out=dst_ap, in_=dst_ap), reads=[(tagw, "_s")], writes=[tagw])

        def fox():
            AR.reset(base_off)
            qT2 = AR.view([128, 2, T], BF16)
            kT2 = AR.view([128, 2, T], BF16)
            Vaug = AR.view([128, NT, 4, 128], BF16)
            Bt = AR.view([128, 4, NT, NT], F32)
            zt = AR.view([128, NT, 4], F32)
            ctm = AR.view([128, NT, 4], F32)
            cmb = AR.view([128, NT, 4], F32)
            off = AR.view([128, NT, 4], F32)
            totb = AR.view([128, NT, 4], F32)
            Gqk = AR.view([128, 8, 64], F32)
            fbb = AR.view([128, 4], F32)
            qkb = [AR.view([128, 512], BF16) for _ in range(2)]
            load_win(0, 772)
            load_wout(0)
            for hh in range(4):
                Q[0].dma("sp", Gqk[:, hh, :], W["fox_q_norm"][l].partition_broadcast(128), writes=[("Gqk", hh)])
                Q[0].dma("sp", Gqk[:, 4 + hh, :], W["fox_k_norm"][l].partition_broadcast(128), writes=[("Gqk", 4 + hh)])
            GQ = [("Gqk", hh) for hh in range(8)]
            Q[0].dma("sp", fbb[:], W["fox_f_bias"][l].partition_broadcast(128), writes=["fbb"])
            Q[0].op("dve", lambda e: e.memset(Vaug[:, :, :, 64:128], 1.0), writes=["vones"])
            def fox_body(i):
                p = i % 2
                S = SCR[p]
                A, B, smp = S["A"], S["B"], S["sm"]
                b0, b1, pTi = S["b0"], S["b1"], S["pT"]
                bk0, bk1 = banks[b0], banks[b1]
                pTv = banks[pTi][:, :].bitcast(BF16)
                proj_tm(i, 0, 512, bk0, ("bk", b0), 0)
                proj_tm(i, 512, 260, bk1, ("bk", b1), 512)
                Q[0].op("act", lambda e: e.activation(out=A[:, 0:512], in_=bk0[:, :], func=AF.Square),
                        reads=[("bk", b0)], writes=[("sA", p)])
                Q[0].op("dve", lambda e: e.tensor_reduce(out=smp[:, 0:8], in_=A[:, 0:512].rearrange("p (a b) -> p a b", b=64),
                                                         axis=AX.X, op=ALU.add), reads=[("sA", p)], writes=[("sm", p)])
                rstd_small(smp[:, 0:8], smp[:, 8:16], 8, ("sm", p), ("sm2", p))
                Q[0].op("dve", lambda e: e.tensor_tensor(out=B[:, 0:512].rearrange("p (a b) -> p a b", b=64),
                                                         in0=bk0[:, :].rearrange("p (a b) -> p a b", b=64),
                                                         in1=smp[:, 8:16].unsqueeze(2).to_broadcast([128, 8, 64]), op=ALU.mult),
                        reads=[("bk", b0), ("sm2", p)], writes=[("sB", p)])
                qb_ = qkb[p]
                Q[0].op("dve", lambda e: e.tensor_tensor(out=qb_[:], in0=B[:, 0:512],
                                                         in1=Gqk[:, :, :].rearrange("p a b -> p (a b)"), op=ALU.mult),
                        reads=[("sB", p)] + GQ, writes=[("qkb", p)])
                for c in range(4):
                    Q[0].op("pe", lambda e, c=c: e.transpose(pTv[:, c * 128:(c + 1) * 128], qb_[:, c * 128:(c + 1) * 128], identb[:]),
                            reads=[("qkb", p), "identb"], writes=[("bk", pTi)])
                Q[0].op("act", lambda e: e.activation(out=qT2[:, :, i * 128:(i + 1) * 128],
                                                      in_=pTv[:, 0:256].rearrange("p (c t) -> p c t", t=128), func=AF.Copy),
                        reads=[("bk", pTi)], writes=[("qT2", i)])
                Q[0].op("act", lambda e: e.activation(out=kT2[:, :, i * 128:(i + 1) * 128],
                                                      in_=pTv[:, 256:512].rearrange("p (c t) -> p c t", t=128), func=AF.Copy),
                        reads=[("bk", pTi)], writes=[("kT2", i)])
                Q[0].op("act", lambda e: e.activation(out=Vaug[:, i, :, 0:64],
                                                      in_=bk1[:, 0:256].rearrange("p (a b) -> p a b", b=64), func=AF.Copy),
                        reads=[("bk", b1)], writes=[("V", i)])
                Q[0].op("dve", lambda e: e.tensor_tensor(out=zt[:, i, :], in0=bk1[:, 256:260], in1=fbb[:], op=ALU.add),
                        reads=[("bk", b1), "fbb"], writes=[("zt", i)])

            run_pairs(fox_body)
            ZT = [("zt", i) for i in range(NT)]
            ztf = zt[:, :, :].rearrange("p a b -> p (a b)")
            Q[0].op("act", lambda e: e.activation(out=ztf, in_=ztf, func=AF.Exp, scale=-1.0), reads=ZT, writes=["z1"])
            Q[0].op("act", lambda e: e.activation(out=ztf, in_=ztf, func=AF.Ln, bias=EPS_T[:, 1:2]), reads=["z1", "eps1"], writes=["z2"])
            Q[0].op("dve", lambda e: e.tensor_scalar(out=ztf, in0=ztf, scalar1=-1.0, scalar2=None, op0=ALU.mult), reads=["z2"], writes=["logf"])
            Q[0].op("pe", lambda e: e.matmul(banks[0][:, 0:64], lhsT=cfv("ltri"), rhs=ztf, start=True, stop=True),
                 reads=["logf", "cft"], writes=[("bk", 0)])
            Q[0].op("pe", lambda e: e.matmul(banks[0][:, 64:128], lhsT=cfv("onesf"), rhs=ztf, start=True, stop=True),
                 reads=["logf", "cft"], writes=[("bk", 0)])
            totf = totb[:, :, :].rearrange("p a b -> p (a b)")
            Q[0].op("act", lambda e: e.activation(out=totf, in_=banks[0][:, 64:128], func=AF.Copy), reads=[("bk", 0)], writes=["totb"])
            Q[0].op("dve", lambda e: e.memset(off[:, 0, :], 0.0), writes=["off"])
            for j in range(1, NT):
                Q[0].op("dve", lambda e, j=j: e.tensor_tensor(out=off[:, j, :], in0=off[:, j - 1, :], in1=totb[:, j - 1, :], op=ALU.add),
                     reads=["off", "totb"], writes=["off"])
            ctf = ctm[:, :, :].rearrange("p a b -> p (a b)")
            Q[0].op("dve", lambda e: e.tensor_tensor(out=ctf, in0=banks[0][:, 0:64], in1=off[:, :, :].rearrange("p a b -> p (a b)"), op=ALU.add),
                 reads=[("bk", 0), "off"], writes=["ctm"])
            Q[0].op("pe", lambda e: e.matmul(banks[1][:, 0:64], lhsT=cfv("row64"), rhs=ctf, start=True, stop=True),
                 reads=["ctm", "cft"], writes=[("bk", 1)])
            Q[0].op("act", lambda e: e.activation(out=cmb[:, :, :].rearrange("p a b -> p (a b)"), in_=banks[1][:, 0:64], func=AF.Copy),
                 reads=[("bk", 1)], writes=["cmb"])
            for hh in range(4):
                for ks in range(NT):
                    Q[0].op("dve", lambda e, hh=hh, ks=ks: e.tensor_scalar(
                        out=Bt[:, hh, ks, :], in0=cmb[:, :, hh], scalar1=ctm[:, ks, hh:hh + 1], scalar2=None, op0=ALU.subtract),
                        reads=["cmb", "ctm"], writes=[("Bt", hh)])
            DLA = 3
            rdn = AR.view([128, 4, 128], F32)
            blocks = []
            for j in range(NT):
                for hh in range(4):
                    for ks in range(j + 1):
                        blocks.append((j, hh, ks, len(blocks) % 4))
            PTs = [PT[0][:, 0:128], PT[1][:, 0:128], PT[2][:, 0:128], PT[0][:, 256:384]]
            st_banks = [2, 3, 4, 1]

            def st_ap(s_):
                return banks[st_banks[s_]][:, 0:128]

            def acc_ap(g_):
                sl = g_ % 2
                return banks[5 + sl][:, 0:128], ("bk", 5 + sl)

            def emit_S(idx):
                j, hh, ks, s_ = blocks[idx]
                pr = (hh % 2) * 64
                pc = hh // 2
                sb_ = st_ap(s_)
                Q[0].op("pe", lambda e: e.matmul(sb_, lhsT=kT2[pr:pr + 64, pc, ks * 128:(ks + 1) * 128],
                                              rhs=qT2[pr:pr + 64, pc, j * 128:(j + 1) * 128], start=True, stop=(ks != j)),
                     reads=[("kT2", ks), ("qT2", j)], writes=[("bk", st_banks[s_])])
                if ks == j:
                    Q[0].op("pe", lambda e: e.matmul(sb_, lhsT=identb[:], rhs=cbv("caus"), start=False, stop=True),
                         reads=["identb", "cbt"], writes=[("bk", st_banks[s_])])
                Q[0].op("act", lambda e: e.activation(out=PTs[s_], in_=sb_, func=AF.Exp, scale=0.125, bias=Bt[:, hh, ks, j:j + 1]),
                     reads=[("bk", st_banks[s_]), ("Bt", hh)], writes=[("PT", s_)])

            def emit_PV(idx):
                j, hh, ks, s_ = blocks[idx]
                g_ = j * 4 + hh
                accb, acct = acc_ap(g_)
                buf = j % 2
                Q[0].op("pe", lambda e: e.matmul(accb, lhsT=Vaug[:, ks, hh, :], rhs=PTs[s_], start=(ks == 0), stop=(ks == j)),
                     reads=[("V", ks), "vones", ("PT", s_)], writes=[acct])
                if ks == j:
                    rs_ = g_ % 4
                    Q[0].op("act", lambda e: e.activation(out=rdn[0:64, rs_, :], in_=accb[64:128, :], func=AF.Copy),
                         reads=[acct], writes=[("rdn", rs_)])
                    Q[0].op("dve", lambda e: e.reciprocal(out=rdn[0:64, rs_, :], in_=rdn[0:64, rs_, :]),
                         reads=[("rdn", rs_)], writes=[("rdn", rs_)])
                    Q[0].op("dve", lambda e: e.tensor_tensor(out=OT[0:64, hh, buf, :], in0=accb[0:64, :], in1=rdn[0:64, rs_, :], op=ALU.mult),
                         reads=[acct, ("rdn", rs_)], writes=[("OT", buf, hh)])
                    if hh == 3:
                        wout_tile(j, buf, (0, 0))

            for idx in range(len(blocks) + DLA):
                if idx < len(blocks):
                    emit_S(idx)
                if idx >= DLA:
                    emit_PV(idx - DLA)

        def gmlp():
            AR.reset(base_off)
            uT = AR.view([128, 4, T], BF16)
            vgt = AR.view([128, NT, 256], BF16)
            wsn = AR.view([128, 4, 128], F32)
            wsb = AR.view([128, 4, 128], BF16)
            WT = AR.view([128, 4, 128], BF16)
            Gv = AR.view([128, 256], F32)
            bsf = AR.view([128, 512], F32)
            bsh = AR.view([128, 512], BF16)
            bsl = AR.view([128, 512], BF16)
            load_win(772, 512)
            load_wout(1)
            Q[0].dma("sp", Gv[:], W["gmlp_v_norm"][l].partition_broadcast(128), writes=["Gv"])
            Q[0].dma("sp", wsn[:], W["gmlp_w_s"][l].rearrange("g t s -> t g s"), writes=["wsn"])
            Q[0].dma("sp", bsf[0:1, :], W["gmlp_b_s"][l:l + 1].rearrange("o g t -> o (g t)"), writes=["bsf"])
            Q[0].op("dve", lambda e: e.tensor_copy(out=bsh[0:1, :], in_=bsf[0:1, :]), reads=["bsf"], writes=["bsh"])
            Q[0].op("dve", lambda e: e.tensor_tensor(out=bsl[0:1, :], in0=bsf[0:1, :], in1=bsh[0:1, :], op=ALU.subtract),
                 reads=["bsf", "bsh"], writes=["bsl"])
            Q[0].op("dve", lambda e: e.tensor_tensor(out=wsb[:], in0=wsn[:],
                                                  in1=cfv("tril").unsqueeze(1).to_broadcast([128, 4, 128]), op=ALU.mult),
                 reads=["wsn", "cft"], writes=["wsb"])
            for g in range(4):
                Q[0].op("pe", lambda e, g=g: e.transpose(pTb[:, g * 128:(g + 1) * 128], wsb[:, g, :], identb[:]),
                     reads=["wsb", "identb"], writes=[("bk", 7)])
            Q[0].op("act", lambda e: e.activation(out=WT[:], in_=pTb[:, 0:512].rearrange("p (g t) -> p g t", t=128), func=AF.Copy),
                 reads=[("bk", 7)], writes=["WT"])
            uc = 0
            for g in range(4):
                for tb in range(4):
                    r = uc % 2
                    uc += 1
                    bk = banks[2 + r]
                    for k in range(8):
                        Q[0].op("pe", lambda e, bk=bk, g=g, tb=tb, k=k: e.matmul(
                            bk[0:64, :], lhsT=wbuf[:, k, g * 64:(g + 1) * 64], rhs=hT[:, k, tb * 512:(tb + 1) * 512],
                            start=(k == 0), stop=(k == 7)), reads=["wbuf"] + HR, writes=[("bk", 2 + r)])
                    Q[0].op("act", lambda e, bk=bk, g=g, tb=tb: e.activation(out=uT[0:64, g, tb * 512:(tb + 1) * 512], in_=bk[0:64, :],
                                                                         func=AF.Gelu_apprx_tanh), reads=[("bk", 2 + r)], writes=[("uT", tb)])
            def gmlp_body(i):
                p = i % 2
                S = SCR[p]
                A, B, smp = S["A"], S["B"], S["sm"]
                pb_ = S["b0"]
                bkp = banks[pb_]
                sbi = 4 + p
                bk = banks[sbi]
                proj_tm(i, 1028, 256, bkp, ("bk", pb_), 256)
                Q[0].op("act", lambda e: e.activation(out=A[:, 0:256], in_=bkp[:, 0:256], func=AF.Gelu_apprx_tanh),
                        reads=[("bk", pb_)], writes=[("sA", p)])
                Q[0].op("act", lambda e: e.activation(out=B[:, 0:256], in_=A[:, 0:256], func=AF.Square),
                        reads=[("sA", p)], writes=[("sB", p)])
                Q[0].op("dve", lambda e: e.tensor_reduce(out=smp[:, 0:4], in_=B[:, 0:256].rearrange("p (a b) -> p a b", b=64),
                                                         axis=AX.X, op=ALU.add), reads=[("sB", p)], writes=[("sm", p)])
                rstd_small(smp[:, 0:4], smp[:, 8:12], 4, ("sm", p), ("sm2", p))
                Q[0].op("dve", lambda e: e.tensor_tensor(out=B[:, 0:256].rearrange("p (a b) -> p a b", b=64),
                                                         in0=A[:, 0:256].rearrange("p (a b) -> p a b", b=64),
                                                         in1=smp[:, 8:12].unsqueeze(2).to_broadcast([128, 4, 64]), op=ALU.mult),
                        reads=[("sA", p), ("sm2", p), ("sB", p)], writes=[("sB", p)])
                Q[0].op("dve", lambda e: e.tensor_tensor(out=vgt[:, i, :], in0=B[:, 0:256], in1=Gv[:], op=ALU.mult),
                        reads=[("sB", p), "Gv"], writes=[("vgt", i)])
                for g in range(4):
                    Q[0].op("pe", lambda e, g=g: e.matmul(bk[0:64, g * 128:(g + 1) * 128], lhsT=vgt[:, i, g * 64:(g + 1) * 64],
                                                          rhs=WT[:, g, :], start=True, stop=False),
                            reads=[("vgt", i), "WT"], writes=[("bk", sbi)])
                    Q[0].op("pe", lambda e, g=g: e.matmul(bk[0:64, g * 128:(g + 1) * 128], lhsT=cbv("ones")[0:1, 0:64],
                                                          rhs=bsh[0:1, g * 128:(g + 1) * 128], start=False, stop=False),
                            reads=["cbt", "bsh"], writes=[("bk", sbi)])
                    Q[0].op("pe", lambda e, g=g: e.matmul(bk[0:64, g * 128:(g + 1) * 128], lhsT=cbv("ones")[0:1, 0:64],
                                                          rhs=bsl[0:1, g * 128:(g + 1) * 128], start=False, stop=True),
                            reads=["cbt", "bsl"], writes=[("bk", sbi)])
                Q[0].op("dve", lambda e: e.tensor_tensor(
                    out=OT[0:64, :, p, :], in0=bk[0:64, :].rearrange("p (g t) -> p g t", t=128),
                    in1=uT[0:64, :, i * 128:(i + 1) * 128], op=ALU.mult),
                    reads=[("bk", sbi), ("uT", i // 4)], writes=[("OT", p)])
                wout_tile(i, p, (S["b0"], S["b1"]))

            run_pairs(gmlp_body)

        def pool():
            AR.reset(base_off)
            zt = AR.view([128, NT, 256], BF16)
            wp = AR.view([128, 4, 64], BF16)
            psc = AR.view([128, 4], F32)
            pl = [AR.view([128, 4, 128], BF16) for _ in range(2)]
            load_win(1936, 256)
            load_wout(3)
            Q[0].dma("pool", wp[0:64, :, :], W["pool_w"][l].rearrange("g d e -> d g e"), writes=["wp"])
            Q[0].dma("sp", psc[0:64, :], W["pool_scale"][l].rearrange("(g e) -> e g", e=64), writes=["psc"],
                  allow_slow_non_contiguous=True)
            def pool_body(i):
                p = i % 2
                S = SCR[p]
                pb_ = S["b0"]
                bkp = banks[pb_]
                s1 = 4 + p
                s2 = 6 + p
                bk = banks[s1]
                bk2 = banks[s2]
                proj_tm(i, 1936, 256, bkp, ("bk", pb_), 0)
                Q[0].op("act", lambda e: e.activation(out=zt[:, i, :], in_=bkp[:, 0:256], func=AF.Copy),
                        reads=[("bk", pb_)], writes=[("zt", i)])
                for g in range(4):
                    o_ = bk[0:64, g * 128:(g + 1) * 128]
                    lh = zt[:, i, g * 64:(g + 1) * 64]
                    if i == 0:
                        Q[0].op("pe", lambda e, o_=o_, lh=lh, g=g: e.matmul(o_, lhsT=lh, rhs=cbv("a0h", 512)[:, g * 128:(g + 1) * 128], start=True, stop=False),
                                reads=[("zt", i), "cbt"], writes=[("bk", s1)])
                        Q[0].op("pe", lambda e, o_=o_, lh=lh, g=g: e.matmul(o_, lhsT=lh, rhs=cbv("a0l", 512)[:, g * 128:(g + 1) * 128], start=False, stop=True),
                                reads=[("zt", i), "cbt"], writes=[("bk", s1)])
                    else:
                        lp = zt[:, i - 1, g * 64:(g + 1) * 64]
                        Q[0].op("pe", lambda e, o_=o_, lh=lh, g=g: e.matmul(o_, lhsT=lh, rhs=cbv("ad", 512)[:, g * 128:(g + 1) * 128], start=True, stop=False),
                                reads=[("zt", i), "cbt"], writes=[("bk", s1)])
                        Q[0].op("pe", lambda e, o_=o_, lp=lp, g=g: e.matmul(o_, lhsT=lp, rhs=cbv("ap", 512)[:, g * 128:(g + 1) * 128], start=False, stop=True),
                                reads=[("zt", i - 1), "cbt"], writes=[("bk", s1)])
                Q[0].op("act", lambda e: e.activation(out=pl[p][0:64, :, :], in_=bk[0:64, :].rearrange("p (g t) -> p g t", t=128), func=AF.Copy),
                        reads=[("bk", s1)], writes=[("pl", p)])
                for g in range(4):
                    Q[0].op("pe", lambda e, g=g: e.matmul(bk2[0:64, g * 128:(g + 1) * 128], lhsT=wp[0:64, g, :], rhs=pl[p][0:64, g, :],
                                                          start=True, stop=True), reads=["wp", ("pl", p)], writes=[("bk", s2)])
                for g in range(4):
                    Q[0].op("dve", lambda e, g=g: e.tensor_scalar(out=OT[0:64, g, p, :], in0=bk2[0:64, g * 128:(g + 1) * 128],
                                                                  scalar1=psc[0:64, g:g + 1], scalar2=None, op0=ALU.mult),
                            reads=[("bk", s2), "psc"], writes=[("OT", p)])
                wout_tile(i, p, (S["b0"], S["b1"]))

            run_pairs(pool_body)

        def nsa():
            AR.reset(base_off)
            qT4 = AR.view([128, 4, T], BF16)
            KV3 = AR.view([128, 3, T], BF16)
            vsa = AR.view([128, NT, 128], BF16)
            vwa = AR.view([128, NT, 128], BF16)
            ghl = AR.view([128, T], BF16)
            gsc = [AR.view([128, 512], F32) for _ in range(2)]
            gbias = AR.view([128, 4], F32)
            Gall = AR.view([128, 10, 64], F32)
            NBb = [AR.view([128, 640], BF16) for _ in range(2)]
            rts = [[AR.view([128, 4, 8], F32) for _ in range(4)] for _ in range(2)]
            Q[0].op("dve", lambda e: e.memset(wbuf[:, :, 652:704], 0.0), writes=["wbuf"])
            load_win(1284, 652)
            load_wout(2)
            Q[0].op("dve", lambda e: e.memset(Gall[:, :, :], 1.0), writes=["GallI"])
            for hh in range(4):
                Q[0].dma("sp", Gall[:, hh, :], W["nsa_q_norm"][l].partition_broadcast(128), reads=["GallI"], writes=[("Gall", hh)])
            Q[0].dma("sp", Gall[:, 6, :], W["nsa_ks_norm"][l].partition_broadcast(128), reads=["GallI"], writes=[("Gall", 6)])
            Q[0].dma("sp", Gall[:, 8, :], W["nsa_kw_norm"][l].partition_broadcast(128), reads=["GallI"], writes=[("Gall", 8)])
            GA = ["GallI"] + [("Gall", hh) for hh in (0, 1, 2, 3, 6, 8)]
            Q[0].op("dve", lambda e: e.memset(gbias[:, :], 0.0), writes=["gbias0"])
            Q[0].dma("sp", gbias[0:12, 0:1], W["nsa_gate_bias"][l].rearrange("(c o) -> c o", o=1), reads=["gbias0"], writes=["gbias"])
            Q[0].op("dve", lambda e: e.memset(vsa[:, :, 64:128], 1.0), writes=["vsones"])
            Q[0].op("dve", lambda e: e.memset(vwa[:, :, 64:128], 1.0), writes=["vwones"])
            rope = cfv("rope", 17 * 16).rearrange("p (s c) -> p s c", c=16)

            def rope_ops(view, nh, slot, wtag, p=0):
                x1 = view[:, :, 0:8]
                x2 = view[:, :, 8:16]
                cosb = rope[:, slot, 0:8].unsqueeze(1).to_broadcast([128, nh, 8])
                sinb = rope[:, slot, 8:16].unsqueeze(1).to_broadcast([128, nh, 8])
                ra, rb_, rc, rd = [t_[:, 0:nh, :] for t_ in rts[p]]
                Q[0].op("dve", lambda e: e.tensor_tensor(out=ra, in0=x1, in1=cosb, op=ALU.mult), reads=[wtag, "cft"], writes=[("ra", p)])
                Q[0].op("dve", lambda e: e.tensor_tensor(out=rb_, in0=x2, in1=sinb, op=ALU.mult), reads=[wtag, "cft"], writes=[("rb", p)])
                Q[0].op("dve", lambda e: e.tensor_tensor(out=rc, in0=x2, in1=cosb, op=ALU.mult), reads=[wtag, "cft"], writes=[("rc", p)])
                Q[0].op("dve", lambda e: e.tensor_tensor(out=rd, in0=x1, in1=sinb, op=ALU.mult), reads=[wtag, "cft"], writes=[("rd", p)])
                Q[0].op("dve", lambda e: e.tensor_tensor(out=x1, in0=ra, in1=rb_, op=ALU.subtract), reads=[("ra", p), ("rb", p), wtag], writes=[wtag])
                Q[0].op("dve", lambda e: e.tensor_tensor(out=x2, in0=rc, in1=rd, op=ALU.add), reads=[("rc", p), ("rd", p), wtag], writes=[wtag])

            def nsa_body(i):
                p = i % 2
                S = SCR[p]
                A, B, smp = S["A"], S["B"], S["sm"]
                b0, b1, pTi = S["b0"], S["b1"], S["pT"]
                bk0, bk1 = banks[b0], banks[b1]
                pTv = banks[pTi][:, :].bitcast(BF16)
                proj_tm(i, 1284, 512, bk0, ("bk", b0), 0)
                proj_tm(i, 1796, 140, bk1, ("bk", b1), 512)
                Q[0].op("act", lambda e: e.activation(out=A[:, 0:512], in_=bk0[:, :], func=AF.Copy), reads=[("bk", b0)], writes=[("sA", p)])
                Q[0].op("act", lambda e: e.activation(out=A[:, 512:640], in_=bk1[:, 0:128], func=AF.Copy), reads=[("bk", b1), ("sA", p)], writes=[("sA", p)])
                Q[0].op("act", lambda e: e.activation(out=B[:, 0:640], in_=A[:, 0:640], func=AF.Square), reads=[("sA", p)], writes=[("sB", p)])
                Q[0].op("dve", lambda e: e.tensor_reduce(out=smp[:, 0:10], in_=B[:, 0:640].rearrange("p (a b) -> p a b", b=64),
                                                         axis=AX.X, op=ALU.add), reads=[("sB", p)], writes=[("sm", p)])
                rstd_small(smp[:, 0:10], smp[:, 16:26], 10, ("sm", p), ("sm2", p))
                Q[0].op("dve", lambda e: e.tensor_tensor(out=B[:, 0:640].rearrange("p (a b) -> p a b", b=64),
                                                         in0=A[:, 0:640].rearrange("p (a b) -> p a b", b=64),
                                                         in1=smp[:, 16:26].unsqueeze(2).to_broadcast([128, 10, 64]), op=ALU.mult),
                        reads=[("sA", p), ("sm2", p), ("sB", p)], writes=[("sB", p)])
                Q[0].op("dve", lambda e: e.tensor_tensor(out=B[:, 0:640], in0=B[:, 0:640],
                                                         in1=Gall[:, :, :].rearrange("p a b -> p (a b)"), op=ALU.mult),
                        reads=[("sB", p)] + GA, writes=[("sB", p)])
                for (c0_, c1_) in ((256, 384), (448, 512), (576, 640)):
                    Q[0].op("act", lambda e, c0_=c0_, c1_=c1_: e.activation(out=B[:, c0_:c1_], in_=A[:, c0_:c1_], func=AF.Copy),
                            reads=[("sA", p), ("sB", p)], writes=[("sB", p)])
                v3 = B[:, 0:640].rearrange("p (a b) -> p a b", b=64)
                rope_ops(v3[:, 0:4, :], 4, i, ("sB", p), p)
                rope_ops(v3[:, 6:7, :], 1, i, ("sB", p), p)
                rope_ops(v3[:, 8:9, :], 1, i, ("sB", p), p)
                nb = NBb[p]
                Q[0].op("act", lambda e: e.activation(out=nb[:], in_=B[:, 0:640], func=AF.Copy), reads=[("sB", p)], writes=[("NBb", p)])
                for c in range(5):
                    Q[0].op("pe", lambda e, c=c: e.transpose(pTv[:, c * 128:(c + 1) * 128], nb[:, c * 128:(c + 1) * 128], identb[:]),
                            reads=[("NBb", p), "identb"], writes=[("bk", pTi)])
                for hh in range(4):
                    pr_ = (hh % 2) * 64
                    Q[0].op("act", lambda e, hh=hh, pr_=pr_: e.activation(out=qT4[0:64, hh, i * 128:(i + 1) * 128],
                                                                          in_=pTv[pr_:pr_ + 64, (hh // 2) * 128:(hh // 2 + 1) * 128], func=AF.Copy),
                            reads=[("bk", pTi)], writes=[("qT4", i, hh)])
                Q[0].op("act", lambda e: e.activation(out=KV3[:, :, i * 128:(i + 1) * 128],
                                                      in_=pTv[:, 256:640].rearrange("p (c t) -> p c t", t=128), func=AF.Copy),
                        reads=[("bk", pTi)], writes=[("KV3", i)])
                Q[0].op("dve", lambda e: e.tensor_copy(out=vsa[:, i, 0:64], in_=nb[:, 448:512]), reads=[("NBb", p)], writes=[("vsa", i)])
                Q[0].op("dve", lambda e: e.tensor_copy(out=vwa[:, i, 0:64], in_=nb[:, 576:640]), reads=[("NBb", p)], writes=[("vwa", i)])

            run_pairs(nsa_body)
            for tb in range(4):
                r = tb % 2
                bk = banks[5 + r]
                gsc_ = gsc[r]
                for k in range(8):
                    Q[0].op("pe", lambda e, bk=bk, tb=tb, k=k: e.matmul(bk[0:64, :], lhsT=wbuf[:, k, 640:704], rhs=hT[:, k, tb * 512:(tb + 1) * 512],
                                                                   start=(k == 0), stop=(k == 7)), reads=["wbuf"] + HR, writes=[("bk", 5 + r)])
                Q[0].op("act", lambda e, bk=bk, gsc_=gsc_: e.activation(out=gsc_[0:64, :], in_=bk[0:64, :], func=AF.Sigmoid,
                                                                    bias=gbias[0:64, 0:1]), reads=[("bk", 5 + r), "gbias", "gbias0"], writes=[("gsc", r)])
                Q[0].op("dve", lambda e, gsc_=gsc_, tb=tb: e.tensor_copy(out=ghl[0:32, tb * 512:(tb + 1) * 512], in_=gsc_[0:32, :]),
                        reads=[("gsc", r)], writes=[("ghi", tb)])
                Q[0].op("dve", lambda e, gsc_=gsc_, tb=tb: e.tensor_tensor(out=ghl[32:64, tb * 512:(tb + 1) * 512], in0=gsc_[0:32, :],
                                                                       in1=ghl[0:32, tb * 512:(tb + 1) * 512], op=ALU.subtract),
                        reads=[("gsc", r), ("ghi", tb)], writes=[("gT", tb)])
            P.fence(dummy[:])
            import os
            nlvl = int(os.environ.get("NSA_LEVEL", "9"))
            if nlvl == 0:
                return
            AR.reset(0)
            kcp = AR.view([128, 32, 128], BF16)
            W1 = AR.view([128, 32, 256], BF16)
            mcmp = AR.view([128, T], BF16)
            Et = AR.view([128, T], BF16)
            W2 = AR.view([128, 2, 2, 64], BF16)
            posn = AR.view([128, 128], F32)
            pos2T = AR.view([128, 32], F32)
            hg = AR.view([128, 4, 128], BF16)
            kcn = AR.view([128, 64], F32)
            kcnb = AR.view([128, 128], BF16)
            kcmpT = AR.view([128, 128], BF16)
            vca = AR.view([128, 128], BF16)
            gkc = AR.view([128, 64], F32)
            MT = [AR.view([128, 128], BF16) for _ in range(2)]
            impt = AR.view([128, 64], F32)
            m8 = AR.view([128, 16], F32)
            Mtm = AR.view([128, 128], BF16)
            scrD = AR.view([128, 512], F32)
            Usb_ = AR.view([128, 132], F32)
            assert AR.off <= post_limit, ("nsa post overflow", AR.off, post_limit)
            Q[0].op("dve", lambda e: e.memset(Mtm[:, :], 0.0), writes=["Mtm0"])
            Q[0].op("dve", lambda e: e.memset(posn[:, :], 0.0), writes=["posn0"])
            Q[0].dma("sp", mcmp[:], cb_d[:, CB["mcmp"]:CB["mcmp"] + T], writes=["mcmp"])
            Q[0].dma("sp", Et[0:32, :], cb_d[0:32, CB["E"]:CB["E"] + T], writes=["Et"])
            Q[0].dma("pool", W1[0:64, :, :], W["nsa_cmp_k_w1"][l].rearrange("(l d) j -> d l j", d=64), writes=["W1k"])
            Q[0].dma("pool", W1[64:128, :, :], W["nsa_cmp_v_w1"][l].rearrange("(l d) j -> d l j", d=64), writes=["W1v"])
            Q[0].dma("pool", W2[:, 0, :, :], W["nsa_cmp_k_w2"][l].rearrange("(c p) d -> p c d", p=128), writes=["W2k"])
            Q[0].dma("pool", W2[:, 1, :, :], W["nsa_cmp_v_w2"][l].rearrange("(c p) d -> p c d", p=128), writes=["W2v"])
            Q[0].dma("sp", posn[0:32, 0:64], W["nsa_cmp_pos_k"][l], reads=["posn0"], writes=["posnk"])
            Q[0].dma("sp", posn[0:32, 64:128], W["nsa_cmp_pos_v"][l], reads=["posn0"], writes=["posnv"])
            Q[0].dma("sp", gkc[:], W["nsa_kc_norm"][l].partition_broadcast(128), writes=["gkc"])
            if nlvl == 10:
                return
            Q[0].op("pe", lambda e: e.transpose(banks[0][:, 0:128], posn[:, :], cfv("identf")),
                 reads=["posnk", "posnv", "posn0", "cft"], writes=[("bk", 0)])
            Q[0].op("act", lambda e: e.activation(out=pos2T[:], in_=banks[0][:, 0:32], func=AF.Copy), reads=[("bk", 0)], writes=["pos2T"])
            if nlvl == 11:
                return
            kvv = KV3[:, 0, :].rearrange("p (n s) -> p n s", s=16)
            KVR = [("KV3", i) for i in range(NT)]
            Q[0].op("dve", lambda e: e.memset(kcp[:, :, :], 0.0), writes=["kcp0"])
            for ll in range(32):
                src = kvv[:, 0:127, ll] if ll < 16 else kvv[:, 1:128, ll - 16]
                Q[0].op("dve", lambda e, ll=ll, src=src: e.tensor_scalar(
                    out=kcp[:, ll, 0:127], in0=src, scalar1=pos2T[:, ll:ll + 1], scalar2=None, op0=ALU.add),
                    reads=KVR + ["pos2T", "kcp0"], writes=[("kcp", ll)])
            Q[0].op("dve", lambda e: e.memset(hg[:, :, :], 0.0), writes=["hg0"])
            if nlvl == 12:
                return
            nvar = os.environ.get("NSA_VAR", "")
            for kv in range(1 if nvar == "B" else 2):
                for jc in range(2):
                    reg = kv * 2 + jc
                    for ll in range(32):
                        Q[0].op("pe", lambda e, kv=kv, jc=jc, ll=ll, reg=reg: e.matmul(
                            banks[2 + kv][:, jc * 128:jc * 128 + 128], lhsT=W1[kv * 64:(kv + 1) * 64, ll, jc * 128:(jc + 1) * 128],
                            rhs=kcp[kv * 64:(kv + 1) * 64, ll, :], start=(ll == 0), stop=(ll == 31)),
                            reads=["W1k", "W1v", ("kcp", ll), "kcp0"], writes=[("bk", 2 + kv)])
            for kv in range(2):
                Q[0].op("act", lambda e, kv=kv: e.activation(out=hg[:, 2 * kv:2 * kv + 2, :], in_=banks[2 + kv][:, 0:256].rearrange("p (r n) -> p r n", n=128),
                                                         func=AF.Gelu_apprx_tanh), reads=[("bk", 2 + kv), "hg0"], writes=[("hg", kv)])
            if nlvl == 13:
                return
            for jc in range(2):
                Q[0].op("pe", lambda e, jc=jc: e.matmul(banks[4][:, 0:64], lhsT=hg[:, jc, :], rhs=W2[:, 0, jc, :],
                                                     start=(jc == 0), stop=(jc == 1)), reads=[("hg", 0), "W2k"], writes=[("bk", 4)])
            for jc in range(2):
                Q[0].op("pe", lambda e, jc=jc: e.matmul(banks[4][:, 64:128], lhsT=hg[:, 2 + jc, :], rhs=W2[:, 1, jc, :],
                                                     start=(jc == 0), stop=(jc == 1)), reads=[("hg", 1), "W2v"], writes=[("bk", 4)])
            Q[0].op("dve", lambda e: e.memset(vca[:, 0:64], 0.0), writes=["vca0"])
            Q[0].op("dve", lambda e: e.memset(vca[:, 64:128], 1.0), writes=["vca1"])
            Q[0].op("act", lambda e: e.activation(out=vca[:, 0:64], in_=banks[4][:, 64:128], func=AF.Copy),
                 reads=[("bk", 4), "vca0"], writes=["vca"])
            if nlvl == 15:
                return
            Q[0].op("dve", lambda e: e.memset(kcn[:], 0.0), writes=["kcn0"])
            Q[0].op("act", lambda e: e.activation(out=scrA[:, 0:64], in_=banks[4][:, 0:64], func=AF.Square, accum_out=sm[:, 0:1]),
                 reads=[("bk", 4)], writes=["scrA", "sm"])
            rstd_small(sm[:, 0:1], sm[:, 8:9], 1, "sm", "sm2")
            Q[0].op("dve", lambda e: e.scalar_tensor_tensor(out=kcn[:, :], in0=banks[4][:, 0:64], scalar=sm[:, 8:9], in1=gkc[:, :],
                                                         op0=ALU.mult, op1=ALU.mult), reads=[("bk", 4), "sm2", "gkc", "kcn0"], writes=["kcn"])
            if nlvl == 16:
                return
            rope_ops(kcn[:, :].rearrange("p (a b) -> p a b", b=64), 1, 16, "kcn")
            Q[0].op("dve", lambda e: e.memset(kcnb[:, 64:128], 0.0), writes=["kcnb0"])
            Q[0].op("act", lambda e: e.activation(out=kcnb[:, 0:64], in_=kcn[:], func=AF.Copy), reads=["kcn"], writes=["kcnb"])
            Q[0].op("pe", lambda e: e.transpose(pTb[:, 0:128], kcnb[:, :], identb[:]), reads=["kcnb", "kcnb0", "identb"], writes=[("bk", 7)])
            Q[0].op("act", lambda e: e.activation(out=kcmpT[0:64, :], in_=pTb[0:64, 0:128], func=AF.Copy), reads=[("bk", 7)], writes=["kcmpT"])

            selb2 = scrB1[:, 0:384].bitcast(BF16)
            PT3 = scrB1[:, 384:640].bitcast(BF16)
            scrCc = [scrC, scrA1]
            Usb = Usb_
            Q[0].op("dve", lambda e: e.tensor_copy(out=selb2[0:32, :], in_=cfv("sel", 768)[0:32, :]), reads=["cft"], writes=["selb2a"])
            Q[0].op("dve", lambda e: e.tensor_copy(out=selb2[32:64, :], in_=cfv("sel", 768)[0:32, :]), reads=["cft"], writes=["selb2b"])
            PTn = PT + [PT3]
            b0bf = banks[0][:, :].bitcast(BF16)
            GT = [("gT", tb_) for tb_ in range(4)]

            def gate_w(accb, acct, br, j):
                Q[0].op("act", lambda e: e.activation(out=scrA[0:64, 0:512], in_=accb[64:128, :], func=AF.Ln, bias=EPS_T[0:64, 3:4]),
                        reads=[acct, "eps1"], writes=["scrA"])
                Q[0].op("act", lambda e: e.activation(out=scrA[0:64, 0:512], in_=scrA[0:64, 0:512], func=AF.Exp, scale=-1.0),
                        reads=["scrA"], writes=["scrA"])
                for hh in range(4):
                    Q[0].op("pe", lambda e, hh=hh: e.matmul(banks[0][0:64, hh * 128:(hh + 1) * 128],
                                                            lhsT=selb2[0:64, (3 * hh + br) * 64:(3 * hh + br + 1) * 64],
                                                            rhs=ghl[0:64, j * 128:(j + 1) * 128], start=True, stop=True),
                            reads=["selb2a", "selb2b", ("gT", j // 4)], writes=[("bk", 0)])
                Q[0].op("dve", lambda e: e.tensor_tensor(out=scrB[0:64, 0:512], in0=banks[0][0:64, :], in1=scrA[0:64, 0:512], op=ALU.mult),
                        reads=["scrA", ("bk", 0)], writes=["gw"])

            seq = [("cmp", 0)]
            for j in range(NT):
                if j + 1 < NT:
                    seq.append(("cmp", j + 1))
                seq.append(("win", j))
                seq.append(("sel", j))
            blocks = []
            for kind, j in seq:
                if kind == "cmp":
                    blocks.append((kind, j, None, True, True))
                else:
                    k_lo = max(0, j - 4) if kind == "win" else 0
                    for ks in range(k_lo, j + 1):
                        blocks.append((kind, j, ks, ks == k_lo, ks == j))
            st_banks = [1, 2, 3, 4]
            DLA = 3
            acc_of = {"cmp": 7, "win": 5, "sel": 6}

            def emit_S(idx):
                kind, j, ks, first, last = blocks[idx]
                sl = idx % 4
                bi = st_banks[sl]
                sb_ = banks[bi]
                qrhs = qT4[0:64, :, j * 128:(j + 1) * 128]
                qreads = [("qT4", j, h_) for h_ in range(4)]
                if kind == "cmp":
                    Q[0].op("pe", lambda e: e.matmul(sb_[:, :], lhsT=kcmpT[0:64, :], rhs=qrhs, start=True, stop=False),
                            reads=["kcmpT"] + qreads, writes=[("bk", bi)])
                    Q[0].op("pe", lambda e: e.matmul(sb_[:, :], lhsT=identb[:],
                                                     rhs=mcmp[:, j * 128:(j + 1) * 128].unsqueeze(1).to_broadcast([128, 4, 128]),
                                                     start=False, stop=True), reads=["identb", "mcmp"], writes=[("bk", bi)])
                else:
                    kvi = 2 if kind == "win" else 1
                    extra = []
                    if ks == j:
                        extra.append("caus")
                    if kind == "win" and ks == j - 4:
                        extra.append("winup")
                    if kind == "sel" and j >= 8:
                        extra.append("sel")
                    Q[0].op("pe", lambda e: e.matmul(sb_[:, :], lhsT=KV3[0:64, kvi, ks * 128:(ks + 1) * 128], rhs=qrhs,
                                                     start=True, stop=(len(extra) == 0)),
                            reads=[("KV3", ks)] + qreads, writes=[("bk", bi)])
                    for xi, kind2 in enumerate(extra):
                        lastx = (xi == len(extra) - 1)
                        if kind2 == "sel":
                            Q[0].op("pe", lambda e, lastx=lastx: e.matmul(
                                sb_[:, :], lhsT=Et[0:32, ks * 128:(ks + 1) * 128],
                                rhs=MT[j % 2][0:32, :].unsqueeze(1).to_broadcast([32, 4, 128]), start=False, stop=lastx),
                                reads=["Et", ("MT", j % 2)], writes=[("bk", bi)])
                        else:
                            Q[0].op("pe", lambda e, kind2=kind2, lastx=lastx: e.matmul(
                                sb_[:, :], lhsT=identb[:], rhs=cbv(kind2).unsqueeze(1).to_broadcast([128, 4, 128]), start=False, stop=lastx),
                                reads=["identb", "cbt"], writes=[("bk", bi)])
                Q[0].op("act", lambda e: e.activation(out=PTn[sl][:], in_=sb_[:, :], func=AF.Exp, scale=0.125),
                        reads=[("bk", bi)], writes=[("PT", sl)])

            def emit_PV(idx):
                kind, j, ks, first, last = blocks[idx]
                sl = idx % 4
                ai = acc_of[kind]
                accb = banks[ai]
                acct = ("bk", ai)
                buf = j % 2
                cc = scrCc[j % 2]
                cct = ("scrCc", j % 2)
                if kind == "cmp":
                    Q[0].op("pe", lambda e: e.matmul(accb[:, :], lhsT=vca[:, :], rhs=PTn[sl][:], start=True, stop=True),
                            reads=["vca", "vca1", ("PT", sl)], writes=[acct])
                    if j >= 8:
                        for hh in range(4):
                            Q[0].op("pe", lambda e, hh=hh: e.matmul(banks[0][:, hh * 33:(hh + 1) * 33], lhsT=PTn[sl][:, hh * 128:(hh + 1) * 128],
                                                                    rhs=cbv("ov", 64)[:, 0:33], start=True, stop=True),
                                    reads=[("PT", sl), "cbt"], writes=[("bk", 0)])
                        Q[0].op("act", lambda e: e.activation(out=Usb[:, 0:132], in_=banks[0][:, 0:132], func=AF.Copy),
                                reads=[("bk", 0)], writes=["Usb"])
                    gate_w(accb, acct, 0, j)
                    Q[0].op("dve", lambda e: e.tensor_tensor(out=cc[0:64, 0:512], in0=accb[0:64, :], in1=scrB[0:64, 0:512], op=ALU.mult),
                            reads=[acct, "gw"], writes=[cct])
                    if j >= 8:
                        U = Usb[:, 0:132].rearrange("p (h c) -> p h c", c=33)
                        Q[0].op("dve", lambda e: e.tensor_scalar(out=sm[:, 32:36], in0=U[:, :, 32], scalar1=1e-30, scalar2=None, op0=ALU.max),
                                reads=["Usb"], writes=["rD"])
                        Q[0].op("dve", lambda e: e.reciprocal(out=sm[:, 32:36], in_=sm[:, 32:36]), reads=["rD"], writes=["rD"])
                        Q[0].op("dve", lambda e: e.tensor_copy(out=impt[:, 0:32], in_=cfv("addmask", 512)[:, j * 32:(j + 1) * 32]),
                                reads=["cft"], writes=["impt"])
                        for hh in range(4):
                            Q[0].op("dve", lambda e, hh=hh: e.scalar_tensor_tensor(out=impt[:, 0:32], in0=U[:, hh, 0:32], scalar=sm[:, 32 + hh:33 + hh],
                                                                                   in1=impt[:, 0:32], op0=ALU.mult, op1=ALU.add),
                                    reads=["Usb", "rD", "impt"], writes=["impt"])
                        Q[0].op("dve", lambda e: e.max(out=m8[:, 0:8], in_=impt[:, 0:32]), reads=["impt"], writes=["m8a"])
                        Q[0].op("dve", lambda e: e.match_replace(out=impt[:, 32:64], in_to_replace=m8[:, 0:8], in_values=impt[:, 0:32], imm_value=-3e38),
                                reads=["impt", "m8a"], writes=["impt2"])
                        Q[0].op("dve", lambda e: e.max(out=m8[:, 8:16], in_=impt[:, 32:64]), reads=["impt2"], writes=["m8b"])
                        Q[0].op("dve", lambda e: e.tensor_scalar(out=impt[:, 32:64], in0=impt[:, 0:32], scalar1=m8[:, 15:16], scalar2=None, op0=ALU.is_ge),
                                reads=["impt", "m8b", "impt2"], writes=["selm"])
                        Q[0].op("dve", lambda e: e.tensor_scalar(out=Mtm[:, 0:32], in0=impt[:, 32:64], scalar1=-1.0, scalar2=-NEG, op0=ALU.add, op1=ALU.mult),
                                reads=["selm", "Mtm0"], writes=["Mtm"])
                        Q[0].op("pe", lambda e: e.transpose(b0bf[:, 0:128], Mtm[:, :], identb[:]), reads=["Mtm", "Mtm0", "identb"], writes=[("bk", 0)])
                        Q[0].op("act", lambda e: e.activation(out=MT[j % 2][0:32, :], in_=b0bf[0:32, 0:128], func=AF.Copy),
                                reads=[("bk", 0)], writes=[("MT", j % 2)])
                    return
                va, vtag = (vwa, "vwa") if kind == "win" else (vsa, "vsa")
                Q[0].op("pe", lambda e: e.matmul(accb[:, :], lhsT=va[:, ks, :], rhs=PTn[sl][:], start=first, stop=last),
                        reads=[(vtag, ks), vtag[0:2] + "ones", ("PT", sl)], writes=[acct])
                if not last:
                    return
                gate_w(accb, acct, 2 if kind == "win" else 1, j)
                Q[0].op("dve", lambda e: e.tensor_tensor(out=scrD[0:64, 0:512], in0=accb[0:64, :], in1=scrB[0:64, 0:512], op=ALU.mult),
                        reads=[acct, "gw"], writes=["scrD"])
                if kind == "win":
                    Q[0].op("dve", lambda e: e.tensor_tensor(out=cc[0:64, 0:512], in0=cc[0:64, 0:512], in1=scrD[0:64, 0:512], op=ALU.add),
                            reads=[cct, "scrD"], writes=[cct])
                else:
                    Q[0].op("dve", lambda e: e.tensor_tensor(out=OT[0:64, :, buf, :],
                                                             in0=scrD[0:64, 0:512].rearrange("p (h t) -> p h t", t=128),
                                                             in1=cc[0:64, 0:512].rearrange("p (h t) -> p h t", t=128), op=ALU.add),
                            reads=["scrD", cct], writes=[("OT", buf)])
                    wout_tile(j, buf, (0, 0))

            if nlvl == 1:
                return
            for idx in range(len(blocks) + DLA):
                if idx < len(blocks):
                    emit_S(idx)
                if idx >= DLA:
                    emit_PV(idx - DLA)

        fns = {"fox": fox, "gmlp": gmlp, "pool": pool, "nsa": nsa}
        for sname in ("fox", "gmlp", "pool", "nsa"):
            if sname in seq_stages:
                fns[sname]()
                P.fence(dummy[:])
                yield sname

    P.op("dve", lambda e: e.memset(EPS_T[:, 1:2], 1.0), reads=["epst"], writes=["eps1"])
    P.op("dve", lambda e: e.memset(EPS_T[:, 2:3], 0.5), reads=["epst", "eps1"], writes=["eps1"])
    P.op("dve", lambda e: e.memset(EPS_T[:, 3:4], 1e-18), reads=["epst", "eps1"], writes=["eps1"])
    outs = []
    stage_idx = {"ffn1": 0, "fox": 1, "gmlp": 2, "pool": 3, "nsa": 4, "ffn2": 5}

    def dump(sname):
        if dbg:
            outs.append(P.dma("sp", dbg_d[stage_idx[sname]].rearrange("(i p) d -> p i d", p=128), X[:, :, :],
                              reads=[("X", i) for i in range(NT)], writes=[("dbgout", sname)], sem_key="dbg"))

    for s in range(nseq):
        for h2 in range(2):
            P.dma("sp", X[:, h2 * 8:(h2 + 1) * 8, :], x_d[s, h2 * 1024:(h2 + 1) * 1024, :].rearrange("(i p) d -> p i d", p=128),
                  writes=[("X", i) for i in range(h2 * 8, (h2 + 1) * 8)], sem_key=("xin", h2))
        for l in range(nlayers):
            if "ffn1" in stages:
                ffn(l, 1)
                P.fence(dummy[:])
                dump("ffn1")
            for sname in mixer(l, stages):
                dump(sname)
            if "ffn2" in stages:
                ffn(l, 2)
                P.fence(dummy[:])
                dump("ffn2")
        for h2 in range(2):
            outs.append(P.dma("sp", y_d[s, h2 * 1024:(h2 + 1) * 1024, :].rearrange("(i p) d -> p i d", p=128), X[:, h2 * 8:(h2 + 1) * 8, :],
                              reads=[("X", i) for i in range(h2 * 8, (h2 + 1) * 8)], writes=[("yout", s, h2)], sem_key=("yout", h2)))
    P.emit(final_waits=outs)
    st.close()
    return nc, P


_CACHE = {}


def kernel(**inputs):
    n_cores = 8
    x = np.ascontiguousarray(np.asarray(inputs["x"], dtype=np.float32))
    nseq = x.shape[0] // n_cores
    if "nc" not in _CACHE:
        _CACHE["nc"] = build(nseq, 2)[0]
        _CACHE["consts"] = make_consts()
    nc = _CACHE["nc"]
    cb, cf = _CACHE["consts"]
    params = {k: np.ascontiguousarray(np.asarray(inputs[k], dtype=np.float32)) for k in PARAM_SHAPES}
    in_maps = []
    for c in range(n_cores):
        m = {"x": x[c * nseq:(c + 1) * nseq], "cb": cb, "cf": cf}
        m.update(params)
        in_maps.append(m)
    res = run_bass_kernel_spmd(nc, in_maps, core_ids=list(range(n_cores)))
    return np.concatenate([np.asarray(r["y"]) for r in res.results], axis=0).astype(np.float32)
```
